# Optimizing a Trainium2 kernel written in Bass

```python
import math
import jax, jax.numpy as jnp
from jax import lax
import numpy as np

D_MODEL = 1024
BATCH = 8
SEQ = 2048
DEPTH = 2

D_FF = 2816
EPS = 1e-6
FOURIER_GROUPS = 8
FOURIER_GROUP_DIM = 128
FOURIER_WIDTH = FOURIER_GROUPS * FOURIER_GROUP_DIM
DIL_PAIRS = ((128, 1), (512, 4), (2048, 16))
DIL_HEADS_PER_GROUP = 2
DIL_HEAD_DIM = 64
DIL_HEADS = DIL_HEADS_PER_GROUP * len(DIL_PAIRS)
DIL_WIDTH = DIL_HEADS * DIL_HEAD_DIM
DIL_OUT_WIDTH = DIL_HEADS_PER_GROUP * DIL_HEAD_DIM
DIFF_HEADS = 4
DIFF_HEAD_DIM = 64
DIFF_QK_WIDTH = DIFF_HEADS * 2 * DIFF_HEAD_DIM
DIFF_V_DIM = 2 * DIFF_HEAD_DIM
DIFF_V_WIDTH = DIFF_HEADS * DIFF_V_DIM
SUBLN_EPS = 1e-5
Q_BLOCK = 128
IN_WIDTH = FOURIER_WIDTH + 3 * DIL_WIDTH + 2 * DIFF_QK_WIDTH + DIFF_V_WIDTH
N_BRANCHES = 3
NUM_BUCKETS = 32
MAX_DISTANCE = 1024
N_BIAS_HEADS = DIL_HEADS + DIFF_HEADS
NEG_INF = -1e30

kernel_name = "hybrid_gated_fourier_dilated_diff_encoder"


def rmsnorm(x, g, eps=EPS):
    xf = x.astype(jnp.float32)
    y = xf * lax.rsqrt(jnp.mean(xf * xf, axis=-1, keepdims=True) + eps)
    return (y * g.astype(jnp.float32)).astype(x.dtype)


def swiglu(h, w_gate, w_up, w_down):
    return (jax.nn.silu(h @ w_gate) * (h @ w_up)) @ w_down


def rel_bucket(rel):
    half = NUM_BUCKETS // 2
    max_exact = half // 2
    ret = jnp.where(rel > 0, half, 0)
    n = jnp.abs(rel)
    nf = jnp.maximum(n, 1).astype(jnp.float32)
    large = max_exact + (jnp.log(nf / max_exact) / math.log(MAX_DISTANCE / max_exact)
                         * (half - max_exact)).astype(jnp.int32)
    large = jnp.minimum(large, half - 1)
    return ret + jnp.where(n < max_exact, n, large)


def fourier_mix(a):
    Bsz, S, _ = a.shape
    ag = a.astype(jnp.float32).reshape(Bsz, S, FOURIER_GROUPS, FOURIER_GROUP_DIM)
    y = jnp.fft.fftn(ag, axes=(1, 3), norm="ortho").real
    return y.reshape(Bsz, S, FOURIER_WIDTH).astype(a.dtype)


def dilated_window_attention(q, k, v, bias_cols, window, dil):
    Bsz, S, H, dh = q.shape
    radius = window // (2 * dil)
    blk = radius
    L = S // dil
    nb = -(-L // blk)
    Lp = nb * blk

    def to_sub(t):
        t = t.reshape(Bsz, L, dil, H, dh).transpose(0, 2, 3, 1, 4)
        return jnp.pad(t, ((0, 0), (0, 0), (0, 0), (0, Lp - L), (0, 0)))

    def windows(t):
        tp = jnp.pad(to_sub(t), ((0, 0), (0, 0), (0, 0), (blk, blk), (0, 0)))
        tp = tp.reshape(Bsz, dil, H, nb + 2, blk, dh)
        return jnp.concatenate([tp[:, :, :, :-2], tp[:, :, :, 1:-1], tp[:, :, :, 2:]], axis=4)

    qs = to_sub(q).reshape(Bsz, dil, H, nb, blk, dh).astype(jnp.float32)
    kw = windows(k).astype(jnp.float32)
    vw = windows(v).astype(jnp.float32)

    i = jnp.arange(blk)[:, None]
    m = jnp.arange(3 * blk)[None, :]
    rel_t = m - blk - i
    bias = bias_cols[rel_bucket(rel_t * dil)].astype(jnp.float32)
    bias = jnp.transpose(bias, (2, 0, 1))
    tk = (jnp.arange(nb)[:, None, None] - 1) * blk + m[None]
    valid = (jnp.abs(rel_t) <= radius)[None] & (tk >= 0) & (tk < L)

    logits = jnp.einsum('bdhjqc,bdhjkc->bdhjqk', qs, kw) * (dh ** -0.5)
    logits = logits + bias[None, None, :, None]
    logits = jnp.where(valid, logits, NEG_INF)
    lse = jax.nn.logsumexp(logits, axis=-1)
    p = jnp.exp(logits - lse[..., None])
    o = jnp.einsum('bdhjqk,bdhjkc->bdhjqc', p, vw)

    o = o.reshape(Bsz, dil, H, Lp, dh)[:, :, :, :L].transpose(0, 3, 1, 2, 4).reshape(Bsz, S, H, dh)
    lse = lse.reshape(Bsz, dil, H, Lp)[:, :, :, :L].transpose(0, 3, 1, 2).reshape(Bsz, S, H)
    return o, lse


def dilated_mixture(qb, kb, vb, rel_bias):
    Bsz, S, _ = qb.shape
    q = qb.reshape(Bsz, S, DIL_HEADS, DIL_HEAD_DIM)
    k = kb.reshape(Bsz, S, DIL_HEADS, DIL_HEAD_DIM)
    v = vb.reshape(Bsz, S, DIL_HEADS, DIL_HEAD_DIM)
    outs, lses = [], []
    for g, (window, dil) in enumerate(DIL_PAIRS):
        hs = slice(g * DIL_HEADS_PER_GROUP, (g + 1) * DIL_HEADS_PER_GROUP)
        o, lse = dilated_window_attention(q[:, :, hs], k[:, :, hs], v[:, :, hs],
                                          rel_bias[:, hs], window, dil)
        outs.append(o)
        lses.append(lse)
    o = jnp.stack(outs, axis=0)
    w = jax.nn.softmax(jnp.stack(lses, axis=0), axis=0)
    y = jnp.sum(w[..., None] * o, axis=0)
    return y.reshape(Bsz, S, DIL_OUT_WIDTH).astype(qb.dtype)


def diff_attention(qc, kc, vc, bias_cols, lam, subln_g, lam_init):
    Bsz, S, _ = qc.shape
    H, dh = DIFF_HEADS, DIFF_HEAD_DIM
    nq = S // Q_BLOCK
    qb = qc.reshape(Bsz, nq, Q_BLOCK, H, 2, dh).transpose(1, 0, 2, 3, 4, 5).astype(jnp.float32)
    kf = kc.reshape(Bsz, S, H, 2, dh).astype(jnp.float32)
    vf = vc.reshape(Bsz, S, H, DIFF_V_DIM).astype(jnp.float32)
    kpos = jnp.arange(S)
    scale = dh ** -0.5

    def one_block(args):
        j, qj = args
        qpos = j * Q_BLOCK + jnp.arange(Q_BLOCK)
        bias = bias_cols[rel_bucket(kpos[None, :] - qpos[:, None])].astype(jnp.float32)
        bias = jnp.transpose(bias, (2, 0, 1))
        logits = jnp.einsum('bqhmc,bkhmc->bhmqk', qj, kf) * scale + bias[None, :, None]
        p = jax.nn.softmax(logits, axis=-1)
        a = p[:, :, 0] - lam * p[:, :, 1]
        return jnp.einsum('bhqk,bkhe->bqhe', a, vf)

    o = lax.map(one_block, (jnp.arange(nq), qb))
    o = o.transpose(1, 0, 2, 3, 4).reshape(Bsz, S, H, DIFF_V_DIM)
    o = rmsnorm(o, subln_g, SUBLN_EPS) * (1.0 - lam_init)
    return o.reshape(Bsz, S, DIFF_V_WIDTH).astype(qc.dtype)


def setup_inputs(seed: int = 0) -> dict:
    key = jax.random.key(seed)
    ks = jax.random.split(key, 26)

    def nrm(k, shape, fan_in):
        return jax.random.normal(k, shape, jnp.float32) * (fan_in ** -0.5)

    def gain(k, shape):
        return 1.0 + 0.05 * jax.random.normal(k, shape, jnp.float32)

    L = DEPTH
    return {
        "x": jax.random.normal(ks[0], (BATCH, SEQ, D_MODEL), jnp.float32),
        "g_ffn1": gain(ks[1], (L, D_MODEL)),
        "w_ffn1_gate": nrm(ks[2], (L, D_MODEL, D_FF), D_MODEL),
        "w_ffn1_up": nrm(ks[3], (L, D_MODEL, D_FF), D_MODEL),
        "w_ffn1_down": nrm(ks[4], (L, D_FF, D_MODEL), D_FF),
        "g_mix": gain(ks[5], (L, D_MODEL)),
        "w_in": nrm(ks[6], (L, D_MODEL, IN_WIDTH), D_MODEL),
        "w_gate": nrm(ks[7], (L, D_MODEL, N_BRANCHES * D_MODEL), D_MODEL),
        "b_gate": 0.02 * jax.random.normal(ks[8], (L, N_BRANCHES * D_MODEL), jnp.float32),
        "w_br_a": nrm(ks[9], (L, FOURIER_WIDTH, D_MODEL), FOURIER_WIDTH),
        "w_br_b": nrm(ks[10], (L, DIL_OUT_WIDTH, D_MODEL), DIL_OUT_WIDTH),
        "w_br_c": nrm(ks[11], (L, DIFF_V_WIDTH, D_MODEL), DIFF_V_WIDTH),
        "w_out": nrm(ks[12], (L, D_MODEL, D_MODEL), D_MODEL),
        "lam_q1": 0.1 * jax.random.normal(ks[13], (L, DIFF_HEAD_DIM), jnp.float32),
        "lam_k1": 0.1 * jax.random.normal(ks[14], (L, DIFF_HEAD_DIM), jnp.float32),
        "lam_q2": 0.1 * jax.random.normal(ks[15], (L, DIFF_HEAD_DIM), jnp.float32),
        "lam_k2": 0.1 * jax.random.normal(ks[16], (L, DIFF_HEAD_DIM), jnp.float32),
        "subln_g": gain(ks[17], (L, DIFF_V_DIM)),
        "rel_bias": 0.5 * jax.random.normal(ks[18], (NUM_BUCKETS, N_BIAS_HEADS), jnp.float32),
        "g_ffn2": gain(ks[19], (L, D_MODEL)),
        "w_ffn2_gate": nrm(ks[20], (L, D_MODEL, D_FF), D_MODEL),
        "w_ffn2_up": nrm(ks[21], (L, D_MODEL, D_FF), D_MODEL),
        "w_ffn2_down": nrm(ks[22], (L, D_FF, D_MODEL), D_FF),
        "g_final": gain(ks[23], (D_MODEL,)),
    }


def reference(x, g_ffn1, w_ffn1_gate, w_ffn1_up, w_ffn1_down, g_mix, w_in, w_gate, b_gate,
              w_br_a, w_br_b, w_br_c, w_out, lam_q1, lam_k1, lam_q2, lam_k2, subln_g,
              rel_bias, g_ffn2, w_ffn2_gate, w_ffn2_up, w_ffn2_down, g_final):
    Bsz, S, D = x.shape
    c0 = FOURIER_WIDTH
    c1 = c0 + DIL_WIDTH
    c2 = c1 + DIL_WIDTH
    c3 = c2 + DIL_WIDTH
    c4 = c3 + DIFF_QK_WIDTH
    c5 = c4 + DIFF_QK_WIDTH
    for l in range(DEPTH):
        x = x + 0.5 * swiglu(rmsnorm(x, g_ffn1[l]), w_ffn1_gate[l], w_ffn1_up[l], w_ffn1_down[l])

        u = rmsnorm(x, g_mix[l])
        p = u @ w_in[l]
        a_in = p[..., :c0]
        qb, kb, vb = p[..., c0:c1], p[..., c1:c2], p[..., c2:c3]
        qc, kc, vc = p[..., c3:c4], p[..., c4:c5], p[..., c5:]

        y_a = fourier_mix(a_in) @ w_br_a[l]
        y_b = dilated_mixture(qb, kb, vb, rel_bias[:, :DIL_HEADS]) @ w_br_b[l]
        lam_init = 0.8 - 0.6 * math.exp(-0.3 * l)
        lam = (jnp.exp(jnp.sum(lam_q1[l].astype(jnp.float32) * lam_k1[l].astype(jnp.float32)))
               - jnp.exp(jnp.sum(lam_q2[l].astype(jnp.float32) * lam_k2[l].astype(jnp.float32)))
               + lam_init)
        y_c = diff_attention(qc, kc, vc, rel_bias[:, DIL_HEADS:], lam, subln_g[l], lam_init) @ w_br_c[l]

        gates = jax.nn.sigmoid(u @ w_gate[l] + b_gate[l]).reshape(Bsz, S, N_BRANCHES, D)
        merged = gates[:, :, 0] * y_a + gates[:, :, 1] * y_b + gates[:, :, 2] * y_c
        x = x + merged @ w_out[l]

        x = x + 0.5 * swiglu(rmsnorm(x, g_ffn2[l]), w_ffn2_gate[l], w_ffn2_up[l], w_ffn2_down[l])
    return rmsnorm(x, g_final)
```

```python
import math
from contextlib import ExitStack
import numpy as np
import concourse.bass as bass
import concourse.mybir as mybir
from concourse.bass_utils import run_bass_kernel_spmd

F32 = mybir.dt.float32
AF = mybir.ActivationFunctionType
ALU = mybir.AluOpType
AX = mybir.AxisListType

S = 2048
D = 1024
NT = 16
TG = 512
NTG = 4
DC = 8
FF = 2816
NF = 22
L = 2
EPS = 1e-6
SUBLN_EPS = 1e-5
NSLOT = 4
SLOT = 3072
ARENA = 22528
DIL = (1, 4, 16)
NEG = -30000.0
SAME_ENG = True
NCV = 49


class Sem:
    __slots__ = ("name", "h", "count", "barrier")

    def __init__(self, name, barrier=True):
        self.name = name
        self.h = None
        self.count = 0
        self.barrier = barrier


class Buf:
    __slots__ = ("name", "w", "r", "sem")

    def __init__(self, name, sem=None):
        self.name = name
        self.w = None
        self.r = {}
        self.sem = sem


class Q:
    def __init__(self, name, sem):
        self.name = name
        self.ops = []
        self.sem = sem
        self.waited = {}


class Prog:
    ENGS = ("pe", "act", "dve", "pool", "sp")

    def __init__(self, nc):
        self.nc = nc
        self.sems = []
        self.q = {n: Q(n, self.new_sem("e_" + n)) for n in self.ENGS}
        self.ninst = 0
        self.dry = False

    def new_sem(self, name, barrier=True):
        s = Sem("%s_%d" % (name, len(self.sems)), barrier)
        self.sems.append(s)
        return s

    def buf(self, name, dma=False, barrier=True):
        return Buf(name, self.new_sem("d_" + name, barrier) if dma else None)

    def emit(self, eng, fn, reads=(), writes=(), signal=True, dsem=None):
        if self.dry:
            return
        q = self.q[eng]
        deps = {}
        for b in reads:
            if b.w is not None:
                s, v = b.w
                if deps.get(s, 0) < v:
                    deps[s] = v
        for b in writes:
            if b.w is not None:
                s, v = b.w
                if deps.get(s, 0) < v:
                    deps[s] = v
            for s, v in b.r.items():
                if deps.get(s, 0) < v:
                    deps[s] = v
        for s, v in deps.items():
            if s is q.sem and (eng == "pe" or not SAME_ENG):
                continue
            if q.waited.get(s, 0) < v:
                q.ops.append(("w", s, v))
                q.waited[s] = v
        if dsem is not None:
            dsem.count += 16
            tag = (dsem, dsem.count)
            inc = (dsem, 16)
        elif signal:
            q.sem.count += 1
            tag = (q.sem, q.sem.count)
            inc = (q.sem, 1)
        else:
            tag = (q.sem, q.sem.count + 1)
            inc = None
        q.ops.append(("i", fn, inc))
        self.ninst += 1
        for b in reads:
            if b.r.get(tag[0], 0) < tag[1]:
                b.r[tag[0]] = tag[1]
        for b in writes:
            b.w = tag
            b.r = {}

    def barrier(self, engines=None):
        if self.dry:
            return
        for e in engines or self.ENGS:
            q = self.q[e]
            for s in self.sems:
                if s.count > 0 and s.barrier and s is not q.sem and q.waited.get(s, 0) < s.count:
                    q.ops.append(("w", s, s.count))
                    q.waited[s] = s.count

    def mm(self, out, lhsT, rhs, start, stop, reads, writes, signal=None):
        self.emit("pe", lambda e: e.matmul(out, lhsT, rhs, start=start, stop=stop),
                  reads, writes, signal=stop if signal is None else signal)

    def tr(self, out, in_, ident, reads, writes, signal=True):
        self.emit("pe", lambda e: e.transpose(out, in_, ident), reads, writes, signal=signal)

    def act(self, out, in_, func, reads, writes, bias=None, scale=None):
        kw = {}
        if bias is not None:
            kw["bias"] = bias
        if scale is not None:
            kw["scale"] = scale
        self.emit("act", lambda e: e.activation(out=out, in_=in_, func=func, **kw), reads, writes)

    def copy(self, eng, out, in_, reads, writes):
        if eng == "act":
            self.act(out, in_, AF.Copy, reads, writes)
        else:
            self.emit(eng, lambda e: e.tensor_copy(out, in_), reads, writes)

    def tt(self, eng, out, in0, in1, op, reads, writes):
        self.emit(eng, lambda e: e.tensor_tensor(out=out, in0=in0, in1=in1, op=op), reads, writes)

    def stt(self, eng, out, in0, scalar, in1, op0, op1, reads, writes):
        self.emit(eng, lambda e: e.scalar_tensor_tensor(out=out, in0=in0, scalar=scalar, in1=in1, op0=op0, op1=op1),
                  reads, writes)

    def dma(self, eng, out, in_, sem, reads=(), writes=()):
        self.emit(eng, lambda e: e.dma_start(out=out, in_=in_), reads, writes, dsem=sem)

    def finalize(self, stack):
        nc = self.nc
        for s in self.sems:
            if s.count > 0:
                s.h = stack.enter_context(nc.semaphore(s.name))
        block = stack.enter_context(nc.Block())
        decos = {"pe": block.tensor, "act": block.scalar, "dve": block.vector,
                 "pool": block.gpsimd, "sp": block.sync}
        for name in self.ENGS:
            q = self.q[name]

            def body(e, q=q):
                for op in q.ops:
                    if op[0] == "w":
                        e.wait_ge(op[1].h, op[2])
                    else:
                        ins = op[1](e)
                        if op[2] is not None:
                            ins.then_inc(op[2][0].h, op[2][1])

            decos[name](body)


class WStream:
    def __init__(self, P, slots):
        self.P = P
        self.slots = slots
        self.bufs = [P.buf("ws%d" % i, dma=True, barrier=False) for i in range(len(slots))]
        self.plan = []
        self.m = 0
        self.loaded = 0

    def reset(self):
        self.m = 0
        self.loaded = 0

    def next(self, src, rows, n):
        P = self.P
        if P.dry:
            self.plan.append((src, rows, n))
            i = (len(self.plan) - 1) % NSLOT
            return self.slots[i], self.bufs[i]
        m = self.m
        assert self.plan[m][1:] == (rows, n), (m, self.plan[m][1:], rows, n)
        hi = min(m + NSLOT - 1, len(self.plan) - 1)
        while self.loaded <= hi:
            k = self.loaded
            src_k, rows_k, n_k = self.plan[k]
            i = k % NSLOT
            P.dma("sp", self.slots[i][0:rows_k, 0:n_k], src_k, self.bufs[i].sem, writes=[self.bufs[i]])
            self.loaded += 1
        self.m += 1
        i = m % NSLOT
        return self.slots[i], self.bufs[i]


def build(stages=None, dbg=False):
    if stages is None:
        stages = ["setup", "in"]
        for l in range(L):
            stages += ["ffn:%d:0" % l, "m1:%d" % l, "m2:%d" % l, "m3:%d" % l, "m4:%d" % l, "m5:%d" % l, "ffn:%d:1" % l]
        stages += ["out"]
    nc = bass.Bass("TRN2", target_bir_lowering=False)
    skind = "ExternalOutput" if dbg else "Internal"

    def din(name, shape):
        return nc.dram_tensor(name, shape, F32, kind="ExternalInput").ap()

    x_d = din("x", [S, D])
    out_d = nc.dram_tensor("out", [S, D], F32, kind="ExternalOutput").ap()
    wgu_d = din("wgu", [L * 2 * NF, 128, 2048])
    wd_d = din("wd", [L * 2 * 8, 128, 2816])
    winf_d = din("winf", [L * 22, 128, 1024])
    winvb_d = din("winvb", [L, 128, 3072])
    winvc_d = din("winvc", [L * 2, 128, 2048])
    wm5a_d = din("wm5a", [L * 8, 128, 2048])
    wm5b_d = din("wm5b", [L * 8, 128, 1152])
    wm5c_d = din("wm5c", [L * 8, 128, 1536])
    wout_d = din("wout", [L * 8, 128, 1024])
    dft_d = din("dft", [32, 128, 2048])
    cvec_d = din("cvec", [128, L * NCV])
    cs128_d = din("cs128", [128, 256])
    ident_d = din("ident", [128, 128])
    relb_d = din("relb", [32, 10])
    ohc_d = din("ohc", [33, 1536])
    ohb_d = din("ohb", [33, 3 * 512])
    lamv_d = din("lamv", [L * 4, 64])
    gfin_d = din("gfin", [1, 1024])
    qk_s = nc.dram_tensor("qk_s", [14 * 128, S], F32, kind=skind).ap()
    vb_s = nc.dram_tensor("vb_s", [S, 384], F32, kind=skind).ap()
    vc_s = nc.dram_tensor("vc_s", [S, 512], F32, kind=skind).ap()
    t_s = nc.dram_tensor("t_s", [NT, 128, 2048], F32, kind=skind).ap()
    z_s = nc.dram_tensor("z_s", [13 * 128, S], F32, kind=skind).ap()
    bvc_s = nc.dram_tensor("bvc_s", [4, 1536], F32, kind=skind).ap()
    bvb_s = nc.dram_tensor("bvb_s", [6, 512], F32, kind=skind).ap()

    with ExitStack() as st:
        P = Prog(nc)
        sb = lambda n, s: st.enter_context(nc.sbuf_tensor(n + "_sb", s, F32))
        xT = sb("xT", [128, DC, S])
        cst = sb("cst", [128, 1024])
        cvec = sb("cvec", [128, L * NCV])
        relbc = sb("relbc", [128, 320])
        wsl = sb("wsl", [128, NSLOT * SLOT])
        arena = sb("arena", [128, ARENA])
        ps = [st.enter_context(nc.psum_tensor("ps%d" % i, [128, 512], F32)) for i in range(8)]
        PS = [P.buf("ps%d" % i) for i in range(8)]
        XT = [P.buf("xT%d" % i) for i in range(NTG)]
        CST = P.buf("cst", dma=True)
        W = WStream(P, [wsl[:, i * SLOT:(i + 1) * SLOT] for i in range(NSLOT)])

        ones_mean = cst[:, 0:128]
        ones_sub = cst[:, 128:256]
        ones1 = cst[:, 256:384]
        ident = cst[:, 384:512]
        cs128 = cst[:, 512:768]
        eps6 = cst[:, 768:769]
        eps5 = cst[:, 769:770]
        lcol = cst[:, 772:780]
        rb33 = cst[0:33, 784:794]

        def tsl(i):
            return slice(i * 128, (i + 1) * 128)

        def gsl(tg):
            return slice(tg * TG, (tg + 1) * TG)

        class Arena:
            def __init__(self):
                self.off = 0

            def take(self, n, rows=128):
                a = arena[0:rows, self.off:self.off + n]
                self.off += n
                assert self.off <= ARENA, self.off
                return a

        def stage_setup():
            A = Arena()
            P.emit("pool", lambda e: e.memset(cst[:, 0:128], 1.0 / D), writes=[CST])
            P.emit("pool", lambda e: e.memset(cst[:, 128:256], 1.0 / 128), writes=[CST])
            P.emit("pool", lambda e: e.memset(cst[:, 256:384], 1.0), writes=[CST])
            P.emit("pool", lambda e: e.memset(cst[:, 768:769], EPS), writes=[CST])
            P.emit("pool", lambda e: e.memset(cst[:, 769:770], SUBLN_EPS), writes=[CST])
            P.emit("pool", lambda e: e.memset(cst[0:33, 784:794], NEG), writes=[CST])
            P.dma("sp", ident, ident_d[:, :], CST.sem, writes=[CST])
            P.dma("sp", cs128, cs128_d[:, :], CST.sem, writes=[CST])
            P.dma("sp", cvec[:], cvec_d[:, :], CST.sem, writes=[CST])
            P.dma("sp", relbc[:], bass.AP(tensor=relb_d.tensor, offset=0, ap=[[0, 128], [1, 320]]), CST.sem, writes=[CST])
            P.dma("sp", cst[0:32, 784:794], relb_d[:, :], CST.sem, writes=[CST])
            lam = A.take(8 * 64).rearrange("p (a b) -> p a b", a=8)
            LB = P.buf("lam", dma=True)
            for i in range(8):
                P.dma("sp", lam[:, i, :], bass.AP(tensor=lamv_d.tensor, offset=i * 64, ap=[[0, 128], [1, 64]]), LB.sem, writes=[LB])
            sc = A.take(8)
            for l in range(L):
                lam_init = 0.8 - 0.6 * math.exp(-0.3 * l)
                for j in range(2):
                    P.tt("dve", lam[:, 4 * l + 2 * j, :], lam[:, 4 * l + 2 * j, :], lam[:, 4 * l + 2 * j + 1, :], ALU.mult, [LB], [LB])
                    P.emit("dve", lambda e, l=l, j=j: e.reduce_sum(out=sc[:, 2 * l + j:2 * l + j + 1], in_=lam[:, 4 * l + 2 * j, :], axis=AX.X), [LB], [LB])
                P.act(sc[:, 2 * l:2 * l + 2], sc[:, 2 * l:2 * l + 2], AF.Exp, [LB], [LB])
                P.tt("dve", sc[:, 4 + l:5 + l], sc[:, 2 * l + 1:2 * l + 2], sc[:, 2 * l:2 * l + 1], ALU.subtract, [LB], [LB])
                P.emit("dve", lambda e, l=l, li=lam_init: e.tensor_scalar(cst[:, 772 + 2 * l:773 + 2 * l], sc[:, 4 + l:5 + l], -li, None, op0=ALU.add), [LB], [CST])
                P.emit("dve", lambda e, l=l, li=lam_init: e.tensor_scalar(cst[:, 773 + 2 * l:774 + 2 * l], cvec[:, l * NCV + 48:l * NCV + 49], 1.0 - li, None, op0=ALU.mult), [CST], [CST])
            oh = A.take(1536, rows=33)
            OH = P.buf("oh", dma=True)
            fv = A.take(1536, rows=4)
            FV = P.buf("fv", dma=True)
            P.dma("sp", oh, ohc_d[:, :], OH.sem, writes=[OH])
            for c3 in range(3):
                P.mm(ps[0][0:4, :], rb33[:, 6:10], oh[:, c3 * 512:(c3 + 1) * 512], True, True, [CST, OH], [PS[0]])
                P.copy("dve", fv[:, c3 * 512:(c3 + 1) * 512], ps[0][0:4, :], [PS[0]], [FV])
            P.dma("pool", bvc_s[:, :], fv, FV.sem, reads=[FV])
            P.dma("sp", oh, ohb_d[:, :], OH.sem, writes=[OH])
            fb = A.take(512, rows=2)
            FB = P.buf("fb", dma=True)
            for g in range(3):
                P.mm(ps[1][0:2, :], rb33[:, 2 * g:2 * g + 2], oh[:, g * 512:(g + 1) * 512], True, True, [CST, OH], [PS[1]])
                P.copy("dve", fb, ps[1][0:2, :], [PS[1]], [FB])
                P.dma("pool", bvb_s[2 * g:2 * g + 2, :], fb, FB.sem, reads=[FB])

        def stage_in():
            A = Arena()
            xin = [A.take(1024) for _ in range(2)]
            XI = [P.buf("xin%d" % i, dma=True) for i in range(2)]
            for tt in range(NT):
                b = tt % 2
                P.dma("sp", xin[b], x_d[tsl(tt), :], XI[b].sem, writes=[XI[b]])
                for cq in range(2):
                    pb = (tt * 2 + cq) % 4
                    for k in range(4):
                        c = cq * 4 + k
                        P.tr(ps[pb][:, k * 128:(k + 1) * 128], xin[b][:, tsl(c)], ident, [XI[b], CST], [PS[pb]], signal=(k == 3))
                    P.copy("dve" if cq == 0 else "act", xT[:, cq * 4:cq * 4 + 4, tsl(tt)],
                           ps[pb][:, :].rearrange("p (a b) -> p a b", a=4), [PS[pb]], [XT[tt // 4]])

        def norm_group(tg, gcol, dst, DST, sq, SQ, rs, RS, psn):
            for c in range(DC):
                b = c % 2
                P.tt("pool", sq[b], xT[:, c, gsl(tg)], xT[:, c, gsl(tg)], ALU.mult, [XT[tg]], [SQ[b]])
                P.mm(ps[psn][:, :], ones_mean, sq[b], c == 0, c == DC - 1, [CST, SQ[b]], [PS[psn]], signal=True)
            P.act(rs, ps[psn][:, :], AF.Sqrt, [PS[psn], CST], [RS], bias=eps6)
            P.emit("dve", lambda e: e.reciprocal(rs, rs), [RS], [RS])
            for c in range(DC):
                P.stt("dve", dst[:, c, :], xT[:, c, gsl(tg)], gcol[:, c:c + 1], rs, ALU.mult, ALU.mult, [XT[tg], RS, CST], [DST])

        def stage_ffn(l, k):
            A = Arena()
            hT = [A.take(DC * TG).rearrange("p (c t) -> p c t", c=DC) for _ in range(2)]
            HT = [P.buf("hT%d" % i) for i in range(2)]
            act = A.take(NF * TG).rearrange("p (f t) -> p f t", f=NF)
            ACTB = [P.buf("act%d" % f) for f in range(NF)]
            sg = [A.take(TG) for _ in range(2)]
            SG = [P.buf("sg%d" % i) for i in range(2)]
            sq = [A.take(TG) for _ in range(2)]
            SQ = [P.buf("sq%d" % i) for i in range(2)]
            rs = A.take(TG)
            RS = P.buf("rs")
            gcol = cvec[:, l * NCV + (0 if k == 0 else 16):l * NCV + (0 if k == 0 else 16) + 8]
            import os
            cut = int(os.environ.get("FFN_CUT", "9"))
            norm_group(0, gcol, hT[0], HT[0], sq, SQ, rs, RS, 6)
            if cut == 1:
                return
            for tg in range(NTG if cut > 4 else (2 if cut == 4 else 1)):
                h, H = hT[tg % 2], HT[tg % 2]
                for f in range(NF):
                    w, WB = W.next(wgu_d[(l * 2 + k) * NF + f], 128, 2048)
                    wv = w[:, 0:2048].rearrange("p (a c j) -> p a c j", a=2, c=DC)
                    pg, pu = f % 2, 2 + f % 2
                    for c in range(DC):
                        P.mm(ps[pg][:, :], wv[:, 0, c, :], h[:, c, :], c == 0, c == DC - 1, [WB, H], [PS[pg]])
                    for c in range(DC):
                        P.mm(ps[pu][:, :], wv[:, 1, c, :], h[:, c, :], c == 0, c == DC - 1, [WB, H], [PS[pu]])
                    P.act(sg[f % 2], ps[pg][:, :], AF.Silu, [PS[pg]], [SG[f % 2]])
                    P.tt("dve", act[:, f, :], sg[f % 2], ps[pu][:, :], ALU.mult, [SG[f % 2], PS[pu]], [ACTB[f]])
                if cut == 2:
                    return
                if tg + 1 < NTG:
                    norm_group(tg + 1, gcol, hT[(tg + 1) % 2], HT[(tg + 1) % 2], sq, SQ, rs, RS, 6)
                for dc in range(DC):
                    w, WB = W.next(wd_d[(l * 2 + k) * 8 + dc], 128, 2816)
                    wv = w[:, 0:2816].rearrange("p (f j) -> p f j", f=NF)
                    py = 4 + dc % 2
                    for f in range(NF):
                        P.mm(ps[py][:, :], wv[:, f, :], act[:, f, :], f == 0, f == NF - 1, [WB, ACTB[f]], [PS[py]])
                    P.stt("dve", xT[:, dc, gsl(tg)], ps[py][:, :], 0.5, xT[:, dc, gsl(tg)], ALU.mult, ALU.add, [PS[py], XT[tg]], [XT[tg]])

        def stage_m1(l):
            A = Arena()
            uT = [A.take(DC * TG).rearrange("p (c t) -> p c t", c=DC) for _ in range(2)]
            UT = [P.buf("uT%d" % i) for i in range(2)]
            sq = [A.take(TG) for _ in range(2)]
            SQ = [P.buf("sq%d" % i) for i in range(2)]
            rs = A.take(TG)
            RS = P.buf("rs")
            stF = [A.take(TG) for _ in range(3)]
            STF = [P.buf("stF%d" % i, dma=True) for i in range(3)]
            aT = [A.take(TG) for _ in range(2)]
            AT = [P.buf("aT%d" % i) for i in range(2)]
            stT = [A.take(256) for _ in range(3)]
            STT = [P.buf("stT%d" % i, dma=True) for i in range(3)]
            stV = [A.take(TG) for _ in range(2)]
            STV = [P.buf("stV%d" % i, dma=True) for i in range(2)]
            gcol = cvec[:, l * NCV + 8:l * NCV + 16]
            norm_group(0, gcol, uT[0], UT[0], sq, SQ, rs, RS, 6)
            nT = 0
            nF = 0
            nV = 0
            for tg in range(NTG):
                u, U = uT[tg % 2], UT[tg % 2]
                for g in range(8):
                    w, WB = W.next(winf_d[l * 22 + g], 128, 1024)
                    wv = w[:, 0:1024].rearrange("p (c j) -> p c j", c=DC)
                    pa = g % 2
                    for c in range(DC):
                        P.mm(ps[pa][:, :], wv[:, c, :], u[:, c, :], c == 0, c == DC - 1, [WB, U], [PS[pa]])
                    P.copy("act", aT[g % 2], ps[pa][:, :], [PS[pa]], [AT[g % 2]])
                    for t4 in range(4):
                        pt = 2 + nT % 2
                        P.mm(ps[pt][:, 0:256], aT[g % 2][:, tsl(t4)], cs128, True, True, [AT[g % 2], CST], [PS[pt]])
                        P.copy("dve", stT[nT % 3], ps[pt][:, 0:256], [PS[pt]], [STT[nT % 3]])
                        P.dma("pool", t_s[tg * 4 + t4, :, g * 256:(g + 1) * 256], stT[nT % 3], STT[nT % 3].sem, reads=[STT[nT % 3]])
                        nT += 1
                for j in range(14):
                    w, WB = W.next(winf_d[l * 22 + 8 + j], 128, 1024)
                    wv = w[:, 0:1024].rearrange("p (c j) -> p c j", c=DC)
                    pa = j % 2
                    for c in range(DC):
                        P.mm(ps[pa][:, :], wv[:, c, :], u[:, c, :], c == 0, c == DC - 1, [WB, U], [PS[pa]])
                    P.copy("act" if j % 2 else "dve", stF[nF % 3], ps[pa][:, :], [PS[pa]], [STF[nF % 3]])
                    P.dma("pool", qk_s[tsl(j), gsl(tg)], stF[nF % 3], STF[nF % 3].sem, reads=[STF[nF % 3]])
                    nF += 1
                for (src, n, dst, c0) in ((winvb_d[l], 384, vb_s, 0), (winvc_d[l * 2], 256, vc_s, 0), (winvc_d[l * 2 + 1], 256, vc_s, 256)):
                    w, WB = W.next(src, 128, DC * n)
                    wv = w[:, 0:DC * n].rearrange("p (c j) -> p c j", c=DC)
                    for t4 in range(4):
                        pv = 4 + nV % 2
                        for c in range(DC):
                            P.mm(ps[pv][:, 0:n], u[:, c, tsl(t4)], wv[:, c, :], c == 0, c == DC - 1, [WB, U], [PS[pv]])
                        P.copy("act" if nV % 2 else "dve", stV[nV % 2][:, 0:n], ps[pv][:, 0:n], [PS[pv]], [STV[nV % 2]])
                        P.dma("pool", dst[tsl(tg * 4 + t4), c0:c0 + n], stV[nV % 2][:, 0:n], STV[nV % 2].sem, reads=[STV[nV % 2]])
                        nV += 1
                if tg + 1 < NTG:
                    norm_group(tg + 1, gcol, uT[(tg + 1) % 2], UT[(tg + 1) % 2], sq, SQ, rs, RS, 6)

        def stage_m2(l):
            A = Arena()
            Tb = A.take(NT * 1024).rearrange("p (s j) -> p s j", s=NT)
            TB = [P.buf("Tb%d" % i, dma=True) for i in range(4)]
            stF = [A.take(TG) for _ in range(3)]
            STF = [P.buf("stF%d" % i, dma=True) for i in range(3)]
            nF = 0
            for gh in range(2):
                for q4 in range(4):
                    P.dma("sp", Tb[:, q4 * 4:(q4 + 1) * 4, :],
                          t_s[q4 * 4:(q4 + 1) * 4, :, gh * 1024:(gh + 1) * 1024].rearrange("s p j -> p s j"),
                          TB[q4].sem, writes=[TB[q4]])
                for sg_ in range(4):
                    for part in range(2):
                        for piece in range(4):
                            w, WB = W.next(dft_d[(sg_ * 2 + part) * 4 + piece], 128, 2048)
                            wv = w[:, 0:2048].rearrange("p (i j) -> p i j", i=4)
                            for gl in range(4):
                                for i in range(4):
                                    s_t = piece * 4 + i
                                    P.mm(ps[gl][:, :], Tb[:, s_t, gl * 256 + part * 128:gl * 256 + part * 128 + 128], wv[:, i, :],
                                         part == 0 and piece == 0 and i == 0, part == 1 and piece == 3 and i == 3,
                                         [WB, TB[piece]], [PS[gl]], signal=(i == 3 and gl == 3) or (part == 1 and piece == 3 and i == 3))
                    for gl in range(4):
                        P.copy("act" if gl % 2 else "dve", stF[nF % 3], ps[gl][:, :], [PS[gl]], [STF[nF % 3]])
                        P.dma("pool", z_s[tsl(gh * 4 + gl), gsl(sg_)], stF[nF % 3], STF[nF % 3].sem, reads=[STF[nF % 3]])
                        nF += 1

        def stage_m3(l):
            A = Arena()
            qb = A.take(S)
            kb = A.take(S)
            QB = P.buf("qb", dma=True)
            KB = P.buf("kb", dma=True)
            vg = [A.take(NT * 128).rearrange("p (s j) -> p s j", s=NT) for _ in range(2)]
            VG = [P.buf("vg%d" % i, dma=True) for i in range(2)]
            emb = A.take(6 * 384).rearrange("p (a b) -> p a b", a=6)
            EMB = P.buf("emb", dma=True)
            acc = A.take(2 * 2 * S, rows=64).rearrange("p (h k t) -> p h k t", h=2, k=2)
            ACC = [P.buf("acc%d" % i, dma=True) for i in range(2)]
            E = [A.take(384).rearrange("p (a b) -> p a b", a=3) for _ in range(4)]
            EB = [P.buf("E%d" % i) for i in range(4)]
            P.dma("sp", emb, bass.AP(tensor=bvb_s.tensor, offset=0, ap=[[1, 128], [512, 6], [1, 384]]), EMB.sem, writes=[EMB])
            P.act(emb, emb, AF.Exp, [EMB], [EMB])

            def geom(g, i):
                d = DIL[g]
                tpc = NT // d
                tb = i % tpc
                dl = [dd for dd in (-1, 0, 1) if 0 <= tb + dd < tpc]

                def perm(j):
                    r, t_ = j // tpc, j % tpc
                    o = t_ * 128 * d + r
                    return slice(o, o + 127 * d + 1, d)
                return dl, perm

            def load_g(g):
                d = DIL[g]
                tpc = NT // d
                P.dma("sp", qb, qk_s[tsl(g), :], QB.sem, writes=[QB])
                P.dma("sp", kb, qk_s[tsl(3 + g), :], KB.sem, writes=[KB])
                v, VB_ = vg[g % 2], VG[g % 2]
                for r in range(d):
                    src = bass.AP(tensor=vb_s.tensor, offset=r * 384 + g * 128, ap=[[d * 384, 128], [128 * d * 384, tpc], [1, 128]])
                    P.dma("sp", v[:, r * tpc:(r + 1) * tpc, :], src, VB_.sem, writes=[VB_])

            upairs = [(g, i) for g in range(3) for i in range(NT)]

            def emit_S(pi):
                g, i = upairs[pi]
                dl, perm = geom(g, i)
                a0, a1 = dl[0] + 1, dl[-1] + 2
                for dd in dl:
                    for hh in range(2):
                        n = 2 * pi + hh
                        hs = slice(hh * 64, (hh + 1) * 64)
                        pss = ps[n % 4][:, 0:384].rearrange("p (a b) -> p a b", a=3)
                        P.mm(pss[:, dd + 1, :], kb[hs, perm(i + dd)], qb[hs, perm(i)], True, True, [KB, QB], [PS[n % 4]], signal=(dd == dl[-1]))
                for hh in range(2):
                    n = 2 * pi + hh
                    pss = ps[n % 4][:, 0:384].rearrange("p (a b) -> p a b", a=3)
                    mview = emb[:, 2 * g + hh, :].rearrange("p (a b) -> p a b", a=3)[:, :, ::-1]
                    e_, EB_ = E[n % 4], EB[n % 4]
                    P.act(e_[:, a0:a1, :], pss[:, a0:a1, :], AF.Exp, [PS[n % 4]], [EB_], scale=0.125)
                    P.tt("dve", e_[:, a0:a1, :], e_[:, a0:a1, :], mview[:, a0:a1, :], ALU.mult, [EB_, EMB], [EB_])

            def emit_PV(pi):
                g, i = upairs[pi]
                dl, perm = geom(g, i)
                v, VB_ = vg[g % 2], VG[g % 2]
                for hh in range(2):
                    n = 2 * pi + hh
                    hs = slice(hh * 64, (hh + 1) * 64)
                    e_, EB_ = E[n % 4], EB[n % 4]
                    pU = 4 + n % 4
                    psu = ps[pU][0:64, 0:256].rearrange("p (a b) -> p a b", a=2)
                    for dd in dl:
                        P.mm(psu[:, 0, :], v[:, i + dd, hs], e_[:, dd + 1, :], dd == dl[0], dd == dl[-1], [VB_, EB_], [PS[pU]], signal=False)
                    for dd in dl:
                        P.mm(psu[:, 1, :], ones1[:, 0:64], e_[:, dd + 1, :], dd == dl[0], dd == dl[-1], [CST, EB_], [PS[pU]])
                    dst = acc[:, hh, :, perm(i)]
                    if g == 0:
                        P.copy("dve", dst, psu, [PS[pU]], [ACC[hh]])
                    else:
                        P.tt("dve", dst, psu, dst, ALU.add, [PS[pU], ACC[hh]], [ACC[hh]])

            NPB = len(upairs)
            load_g(0)
            emit_S(0)
            for pi in range(NPB):
                g, i = upairs[pi]
                if pi + 1 < NPB:
                    if upairs[pi + 1][0] != g:
                        load_g(g + 1)
                    emit_S(pi + 1)
                emit_PV(pi)
            for hh in range(2):
                P.emit("dve", lambda e, hh=hh: e.reciprocal(acc[:, hh, 1, :], acc[:, hh, 1, :]), [ACC[hh]], [ACC[hh]])
                P.tt("dve", acc[:, hh, 0, :], acc[:, hh, 0, :], acc[:, hh, 1, :], ALU.mult, [ACC[hh]], [ACC[hh]])
                P.dma("pool", z_s[8 * 128 + hh * 64:8 * 128 + (hh + 1) * 64, :], acc[:, hh, 0, :], ACC[hh].sem, reads=[ACC[hh]])

        def stage_m4(l):
            A = Arena()
            qh = [A.take(S) for _ in range(2)]
            kh = [A.take(S) for _ in range(2)]
            vh = [A.take(NT * 128).rearrange("p (s j) -> p s j", s=NT) for _ in range(2)]
            strip = [A.take(1408) for _ in range(2)]
            QH = [P.buf("qh%d" % i, dma=True) for i in range(2)]
            KH = [P.buf("kh%d" % i, dma=True) for i in range(2)]
            VH = [P.buf("vh%d" % i, dma=True) for i in range(2)]
            SB_ = [P.buf("strip%d" % i, dma=True) for i in range(2)]
            E = [A.take(TG) for _ in range(4)]
            EB = [P.buf("E%d" % i) for i in range(4)]
            dacc = [[A.take(TG) for _ in range(2)] for _ in range(2)]
            DACC = [[P.buf("dacc%d%d" % (i, j)) for j in range(2)] for i in range(2)]
            r0 = A.take(TG)
            r1 = A.take(TG)
            av = A.take(TG)
            TMP = P.buf("tmp")
            TMP2 = P.buf("tmp2")
            zst = [A.take(TG) for _ in range(2)]
            ZST = [P.buf("zst%d" % i, dma=True) for i in range(2)]
            neglam = lcol[:, 2 * l:2 * l + 1]
            gsub = lcol[:, 2 * l + 1:2 * l + 2]

            def load_head(h):
                b = h % 2
                P.dma("sp", qh[b], qk_s[tsl(6 + h), :], QH[b].sem, writes=[QH[b]])
                P.dma("sp", kh[b], qk_s[tsl(10 + h), :], KH[b].sem, writes=[KH[b]])
                P.dma("sp", vh[b], vc_s[:, h * 128:(h + 1) * 128].rearrange("(s p) j -> p s j", p=128), VH[b].sem, writes=[VH[b]])
                P.dma("sp", strip[b], bass.AP(tensor=bvc_s.tensor, offset=h * 1536, ap=[[1, 128], [1, 1408]]), SB_[b].sem, writes=[SB_[b]])
                P.act(strip[b], strip[b], AF.Exp, [SB_[b]], [SB_[b]])

            pairs = [(h, qg, kt) for h in range(4) for qg in range(NTG) for kt in range(NT)]
            state = {"s": 0}

            def emit_S(p):
                h, qg, kt = pairs[p]
                b = h % 2
                for m in range(2):
                    n = 2 * p + m
                    ms = slice(m * 64, (m + 1) * 64)
                    P.mm(ps[n % 4][:, :], kh[b][ms, tsl(kt)], qh[b][ms, qg * TG:(qg + 1) * TG], True, True, [KH[b], QH[b]], [PS[n % 4]])

            def emit_exp(n):
                h, qg, kt = pairs[n // 2]
                m = n % 2
                b = h % 2
                Q0 = qg * TG
                pS = n % 4
                e_, EB_ = E[n % 4], EB[n % 4]
                bpos = relbc[:, 31 * 10 + 6 + h:31 * 10 + 7 + h]
                bneg = relbc[:, 15 * 10 + 6 + h:15 * 10 + 7 + h]
                qa = min(max(128 * kt - 640, Q0), Q0 + TG)
                qe = min(max(128 * kt + 768, Q0), Q0 + TG)
                if qa > Q0:
                    P.act(e_[:, 0:qa - Q0], ps[pS][:, 0:qa - Q0], AF.Exp, [PS[pS], CST], [EB_], bias=bpos, scale=0.125)
                if qe > qa:
                    P.act(e_[:, qa - Q0:qe - Q0], ps[pS][:, qa - Q0:qe - Q0], AF.Exp, [PS[pS]], [EB_], scale=0.125)
                    clo = 128 * kt - qe + 768
                    chi = 128 * kt - qa + 767
                    P.tt("dve", e_[:, qa - Q0:qe - Q0], e_[:, qa - Q0:qe - Q0], strip[b][:, clo:chi + 1][:, ::-1], ALU.mult,
                         [EB_, SB_[b]], [EB_])
                if qe < Q0 + TG:
                    P.act(e_[:, qe - Q0:TG], ps[pS][:, qe - Q0:TG], AF.Exp, [PS[pS], CST], [EB_], bias=bneg, scale=0.125)

            def emit_PV(n):
                h, qg, kt = pairs[n // 2]
                m = n % 2
                b = h % 2
                par = (h * NTG + qg) % 2
                e_, EB_ = E[n % 4], EB[n % 4]
                P.mm(ps[4 + m][:, :], vh[b][:, kt, :], e_, kt == 0, kt == NT - 1, [VH[b], EB_], [PS[4 + m]], signal=True)
                eng = "pool" if m == 0 else "dve"
                if kt == 0:
                    P.copy(eng, dacc[par][m], e_, [EB_], [DACC[par][m]])
                else:
                    P.tt(eng, dacc[par][m], dacc[par][m], e_, ALU.add, [EB_, DACC[par][m]], [DACC[par][m]])

            def epilogue_a(h, qg):
                par = (h * NTG + qg) % 2
                P.mm(ps[6][:, :], ones1, dacc[par][0], True, True, [CST, DACC[par][0]], [PS[6]])
                P.mm(ps[7][:, :], ones1, dacc[par][1], True, True, [CST, DACC[par][1]], [PS[7]])
                P.emit("dve", lambda e: e.reciprocal(r0, ps[6][:, :]), [PS[6]], [TMP])
                P.emit("dve", lambda e: e.reciprocal(r1, ps[7][:, :]), [PS[7]], [TMP])
                P.tt("dve", r0, ps[4][:, :], r0, ALU.mult, [PS[4], TMP], [TMP])
                P.tt("dve", r1, ps[5][:, :], r1, ALU.mult, [PS[5], TMP], [TMP])
                P.stt("dve", av, r1, neglam, r0, ALU.mult, ALU.add, [TMP, CST], [TMP2])

            def epilogue_b(h, qg):
                P.tt("pool", r1, av, av, ALU.mult, [TMP2, TMP], [TMP])
                P.mm(ps[6][:, :], ones_sub, r1, True, True, [CST, TMP], [PS[6]])
                P.act(r0, ps[6][:, :], AF.Sqrt, [PS[6], CST, TMP], [TMP], bias=eps5)
                P.emit("dve", lambda e: e.reciprocal(r0, r0), [TMP], [TMP])
                k = state["s"]
                z, Z = zst[k % 2], ZST[k % 2]
                P.stt("dve", z, av, gsub, r0, ALU.mult, ALU.mult, [TMP, TMP2, CST], [Z])
                P.dma("pool", z_s[tsl(9 + h), qg * TG:(qg + 1) * TG], z, Z.sem, reads=[Z])
                state["s"] += 1

            NP = len(pairs)
            load_head(0)
            emit_S(0)
            emit_exp(0)
            emit_exp(1)
            pending = None
            for p in range(NP):
                h, qg, kt = pairs[p]
                if kt == 0 and qg == 0 and h + 1 < 4:
                    load_head(h + 1)
                if p + 1 < NP:
                    emit_S(p + 1)
                    emit_exp(2 * p + 2)
                    emit_exp(2 * p + 3)
                emit_PV(2 * p)
                emit_PV(2 * p + 1)
                if pending is not None and kt == 2:
                    epilogue_b(*pending)
                    pending = None
                if kt == NT - 1:
                    epilogue_a(h, qg)
                    pending = (h, qg)
            epilogue_b(*pending)

        def stage_m5(l):
            A = Arena()
            uT = [A.take(DC * TG).rearrange("p (c t) -> p c t", c=DC) for _ in range(1)]
            UT = [P.buf("uT0")]
            sq = [A.take(TG) for _ in range(2)]
            SQ = [P.buf("sq%d" % i) for i in range(2)]
            rs = A.take(TG)
            RS = P.buf("rs")
            zsl = A.take(13 * TG).rearrange("p (k t) -> p k t", k=13)
            ZS = P.buf("zsl", dma=True)
            mg = A.take(DC * TG).rearrange("p (c t) -> p c t", c=DC)
            MG = [P.buf("mg%d" % i) for i in range(DC)]
            sig = [A.take(TG) for _ in range(2)]
            SIG = [P.buf("sig%d" % i) for i in range(2)]
            tmp = [A.take(TG) for _ in range(2)]
            TMPB = [P.buf("tmp%d" % i) for i in range(2)]
            gcol = cvec[:, l * NCV + 8:l * NCV + 16]
            brs = ((wm5a_d, 8, 0), (wm5b_d, 1, 8), (wm5c_d, 4, 9))
            n = 0
            for tg in range(NTG):
                u, U = uT[0], UT[0]
                norm_group(tg, gcol, u, U, sq, SQ, rs, RS, 6)
                P.dma("sp", zsl, z_s[:, gsl(tg)].rearrange("(k p) t -> p k t", p=128), ZS.sem, writes=[ZS])
                for dc in range(DC):
                    for br, (wsrc, nk, koff) in enumerate(brs):
                        w, WB = W.next(wsrc[l * 8 + dc], 128, (8 + nk) * 128)
                        wv = w[:, 0:(8 + nk) * 128].rearrange("p (c j) -> p c j", c=8 + nk)
                        pg, py = n % 2, 2 + n % 2
                        for c in range(DC):
                            P.mm(ps[pg][:, :], wv[:, c, :], u[:, c, :], c == 0, c == DC - 1, [WB, U], [PS[pg]])
                        bcol = cvec[:, l * NCV + 24 + br * 8 + dc:l * NCV + 25 + br * 8 + dc]
                        P.act(sig[n % 2], ps[pg][:, :], AF.Sigmoid, [PS[pg], CST], [SIG[n % 2]], bias=bcol)
                        for kk in range(nk):
                            P.mm(ps[py][:, :], wv[:, 8 + kk, :], zsl[:, koff + kk, :], kk == 0, kk == nk - 1, [WB, ZS], [PS[py]])
                        if br == 0:
                            P.tt("dve", mg[:, dc, :], sig[n % 2], ps[py][:, :], ALU.mult, [SIG[n % 2], PS[py]], [MG[dc]])
                        else:
                            P.tt("dve", tmp[n % 2], sig[n % 2], ps[py][:, :], ALU.mult, [SIG[n % 2], PS[py]], [TMPB[n % 2]])
                            P.tt("pool", mg[:, dc, :], mg[:, dc, :], tmp[n % 2], ALU.add, [TMPB[n % 2], MG[dc]], [MG[dc]])
                        n += 1
                for d2 in range(DC):
                    w, WB = W.next(wout_d[l * 8 + d2], 128, 1024)
                    wv = w[:, 0:1024].rearrange("p (c j) -> p c j", c=DC)
                    po = 4 + d2 % 2
                    for c in range(DC):
                        P.mm(ps[po][:, :], wv[:, c, :], mg[:, c, :], c == 0, c == DC - 1, [WB, MG[c]], [PS[po]])
                    P.tt("dve", xT[:, d2, gsl(tg)], ps[po][:, :], xT[:, d2, gsl(tg)], ALU.add, [PS[po], XT[tg]], [XT[tg]])

        def stage_out(raw=False):
            A = Arena()
            xo = [A.take(1024) for _ in range(2)]
            XO = [P.buf("xo%d" % i) for i in range(2)]
            sqo = A.take(1024)
            SQO = P.buf("sqo")
            ss = A.take(8)
            yo = [A.take(1024) for _ in range(2)]
            YO = [P.buf("yo%d" % i, dma=True) for i in range(2)]
            gf = A.take(1024)
            GF = P.buf("gf", dma=True)
            P.dma("sp", gf, bass.AP(tensor=gfin_d.tensor, offset=0, ap=[[0, 128], [1, 1024]]), GF.sem, writes=[GF])
            for tt in range(NT):
                b = tt % 2
                for cq in range(2):
                    pb = (tt * 2 + cq) % 4
                    for k in range(4):
                        c = cq * 4 + k
                        P.tr(ps[pb][:, k * 128:(k + 1) * 128], xT[:, c, tsl(tt)], ident, [XT[tt // 4], CST], [PS[pb]], signal=(k == 3))
                    P.copy("dve" if cq == 0 else "act", xo[b][:, cq * 512:(cq + 1) * 512], ps[pb][:, :], [PS[pb]], [XO[b]])
                if raw:
                    P.copy("dve", yo[b], xo[b], [XO[b]], [YO[b]])
                else:
                    P.tt("pool", sqo, xo[b], xo[b], ALU.mult, [XO[b]], [SQO])
                    P.emit("dve", lambda e: e.reduce_sum(out=ss[:, 0:1], in_=sqo, axis=AX.X), [SQO], [SQO])
                    P.emit("dve", lambda e: e.tensor_scalar(ss[:, 0:1], ss[:, 0:1], 1.0 / D, EPS, op0=ALU.mult, op1=ALU.add), [SQO], [SQO])
                    P.act(ss[:, 0:1], ss[:, 0:1], AF.Sqrt, [SQO], [SQO])
                    P.emit("dve", lambda e: e.reciprocal(ss[:, 0:1], ss[:, 0:1]), [SQO], [SQO])
                    P.stt("dve", yo[b], xo[b], ss[:, 0:1], gf, ALU.mult, ALU.mult, [XO[b], SQO, GF], [YO[b]])
                P.dma("pool", out_d[tsl(tt), :], yo[b], YO[b].sem, reads=[YO[b]])

        def run_all():
            for sname in stages:
                parts = sname.split(":")
                if parts[0] == "setup":
                    stage_setup()
                elif parts[0] == "in":
                    stage_in()
                elif parts[0] == "ffn":
                    stage_ffn(int(parts[1]), int(parts[2]))
                elif parts[0] in ("m1", "m2", "m3", "m4", "m5"):
                    {"m1": stage_m1, "m2": stage_m2, "m3": stage_m3, "m4": stage_m4, "m5": stage_m5}[parts[0]](int(parts[1]))
                elif parts[0] == "out":
                    stage_out(False)
                elif parts[0] == "outraw":
                    stage_out(True)
                else:
                    raise ValueError(sname)
                P.barrier()

        P.dry = True
        run_all()
        P.dry = False
        W.reset()
        run_all()
        P.barrier()
        P.finalize(st)
    return nc, P


def _lhsT_tiles(Wm):
    K, N = Wm.shape
    return Wm.reshape(K // 128, 128, N // 128, 128).transpose(2, 1, 0, 3)


def _rel_bucket(rel):
    rel = np.asarray(rel, dtype=np.int64)
    ret = np.where(rel > 0, 16, 0)
    n = np.abs(rel)
    nf = np.maximum(n, 1).astype(np.float32)
    large = 8 + (np.log(nf / np.float32(8)) / np.float32(math.log(1024 / 8)) * np.float32(8)).astype(np.int32)
    large = np.minimum(large, 15)
    return ret + np.where(n < 8, n, large)


_CONST_CACHE = {}


def _constants():
    if _CONST_CACHE:
        return _CONST_CACHE
    s = np.arange(S, dtype=np.int64)
    ang = 2.0 * np.pi * ((s[:, None] * s[None, :]) % S).astype(np.float64) / S
    norm = 1.0 / math.sqrt(S * 128.0)
    mats = [np.cos(ang) * norm, -np.sin(ang) * norm]
    dft = np.stack([m.reshape(4, 4, 128, 4, 512).transpose(3, 0, 2, 1, 4) for m in mats], axis=1)
    _CONST_CACHE["dft"] = np.ascontiguousarray(dft.reshape(32, 128, 2048), dtype=np.float32)
    c = np.arange(128, dtype=np.int64)
    a2 = 2.0 * np.pi * ((c[:, None] * c[None, :]) % 128).astype(np.float64) / 128
    _CONST_CACHE["cs128"] = np.ascontiguousarray(np.concatenate([np.cos(a2), np.sin(a2)], axis=1), dtype=np.float32)
    _CONST_CACHE["ident"] = np.eye(128, dtype=np.float32)
    ohc = np.zeros((33, 1536), np.float32)
    i = np.arange(1535)
    ohc[_rel_bucket(i - 767), i] = 1.0
    _CONST_CACHE["ohc"] = ohc
    ohb = np.zeros((33, 3, 512), np.float32)
    i = np.arange(511)
    for g, d in enumerate(DIL):
        mrel = i - 255
        bk = np.where(np.abs(mrel) <= 64, _rel_bucket(mrel * d), 32)
        ohb[bk, g, i] = 1.0
    _CONST_CACHE["ohb"] = np.ascontiguousarray(ohb.reshape(33, 1536))
    return _CONST_CACHE


def prep_inputs(inp):
    f = lambda a: np.ascontiguousarray(np.asarray(a), dtype=np.float32)
    g = {k: np.asarray(v) for k, v in inp.items()}
    o = dict(_constants())
    wgu, wd, winf, winvb, winvc, wm5a, wm5b, wm5c, wout = [], [], [], [], [], [], [], [], []
    fcols = np.concatenate([np.arange(0, 1792), np.arange(2176, 3200)])
    for l in range(L):
        for (wg_, wu_, wdn_) in ((g["w_ffn1_gate"], g["w_ffn1_up"], g["w_ffn1_down"]),
                                 (g["w_ffn2_gate"], g["w_ffn2_up"], g["w_ffn2_down"])):
            tg_ = _lhsT_tiles(wg_[l])
            tu_ = _lhsT_tiles(wu_[l])
            wgu.append(np.stack([tg_, tu_], axis=2).reshape(NF, 128, 2048))
            wd.append(_lhsT_tiles(wdn_[l]).reshape(8, 128, 2816))
        win = g["w_in"][l]
        winf.append(_lhsT_tiles(win[:, fcols]).reshape(22, 128, 1024))
        winvb.append(win[:, 1792:2176].reshape(8, 128, 384).transpose(1, 0, 2).reshape(1, 128, 3072))
        for hf in range(2):
            winvc.append(win[:, 3200 + hf * 256:3200 + (hf + 1) * 256].reshape(8, 128, 256).transpose(1, 0, 2).reshape(1, 128, 2048))
        for (lst, wb, br) in ((wm5a, g["w_br_a"][l], 0), (wm5b, g["w_br_b"][l], 1), (wm5c, g["w_br_c"][l], 2)):
            gt = _lhsT_tiles(g["w_gate"][l][:, br * 1024:(br + 1) * 1024])
            bt = _lhsT_tiles(wb)
            lst.append(np.concatenate([gt, bt], axis=2).reshape(8, 128, -1))
        wout.append(_lhsT_tiles(g["w_out"][l]).reshape(8, 128, 1024))
    o["wgu"] = f(np.concatenate(wgu, 0))
    o["wd"] = f(np.concatenate(wd, 0))
    o["winf"] = f(np.concatenate(winf, 0))
    o["winvb"] = f(np.concatenate(winvb, 0))
    o["winvc"] = f(np.concatenate(winvc, 0))
    o["wm5a"] = f(np.concatenate(wm5a, 0))
    o["wm5b"] = f(np.concatenate(wm5b, 0))
    o["wm5c"] = f(np.concatenate(wm5c, 0))
    o["wout"] = f(np.concatenate(wout, 0))
    cv = np.zeros((128, L * NCV), np.float32)
    for l in range(L):
        b0 = l * NCV
        cv[:, b0 + 0:b0 + 8] = g["g_ffn1"][l].reshape(8, 128).T
        cv[:, b0 + 8:b0 + 16] = g["g_mix"][l].reshape(8, 128).T
        cv[:, b0 + 16:b0 + 24] = g["g_ffn2"][l].reshape(8, 128).T
        cv[:, b0 + 24:b0 + 48] = g["b_gate"][l].reshape(24, 128).T
        cv[:, b0 + 48] = g["subln_g"][l]
    o["cvec"] = cv
    o["relb"] = f(g["rel_bias"])
    o["lamv"] = f(np.stack([g[k][l] for l in range(L) for k in ("lam_q1", "lam_k1", "lam_q2", "lam_k2")], 0))
    o["gfin"] = f(g["g_final"].reshape(1, 1024))
    return o


_NC_CACHE = {}


def kernel(**inputs):
    shared = prep_inputs(inputs)
    x = np.ascontiguousarray(np.asarray(inputs["x"]), dtype=np.float32)
    if "nc" not in _NC_CACHE:
        _NC_CACHE["nc"] = build()[0]
    nc = _NC_CACHE["nc"]
    in_maps = []
    for b in range(8):
        m = dict(shared)
        m["x"] = x[b]
        in_maps.append(m)
    res = run_bass_kernel_spmd(nc, in_maps, core_ids=list(range(8)))
    return np.stack([np.asarray(r["out"]) for r in res.results], axis=0).astype(np.float32)
```

```python
import math
from contextlib import ExitStack
import numpy as np
import concourse.bass as bass
import concourse.mybir as mybir
from concourse.bass_utils import run_bass_kernel_spmd

F32 = mybir.dt.float32
AF = mybir.ActivationFunctionType
ALU = mybir.AluOpType
AX = mybir.AxisListType

S = 2048
D = 1024
NT = 16
TG = 512
NTG = 4
DC = 8
FF = 2816
NF = 22
L = 2
EPS = 1e-6
SUBLN_EPS = 1e-5
NSLOT = 4
SLOT = 3072
ARENA = 22528
DIL = (1, 4, 16)
NEG = -30000.0
SAME_ENG = True
NCV = 49


class Sem:
    __slots__ = ("name", "h", "count", "barrier")

    def __init__(self, name, barrier=True):
        self.name = name
        self.h = None
        self.count = 0
        self.barrier = barrier


class Buf:
    __slots__ = ("name", "w", "r", "sem")

    def __init__(self, name, sem=None):
        self.name = name
        self.w = None
        self.r = {}
        self.sem = sem


class Q:
    def __init__(self, name, sem):
        self.name = name
        self.ops = []
        self.sem = sem
        self.waited = {}


class Prog:
    ENGS = ("pe", "act", "dve", "pool", "sp")

    def __init__(self, nc):
        self.nc = nc
        self.sems = []
        self.q = {n: Q(n, self.new_sem("e_" + n)) for n in self.ENGS}
        self.ninst = 0
        self.dry = False

    def new_sem(self, name, barrier=True):
        s = Sem("%s_%d" % (name, len(self.sems)), barrier)
        self.sems.append(s)
        return s

    def buf(self, name, dma=False, barrier=True):
        return Buf(name, self.new_sem("d_" + name, barrier) if dma else None)

    def emit(self, eng, fn, reads=(), writes=(), signal=True, dsem=None):
        if self.dry:
            return
        q = self.q[eng]
        deps = {}
        for b in reads:
            if b.w is not None:
                s, v = b.w
                if deps.get(s, 0) < v:
                    deps[s] = v
        for b in writes:
            if b.w is not None:
                s, v = b.w
                if deps.get(s, 0) < v:
                    deps[s] = v
            for s, v in b.r.items():
                if deps.get(s, 0) < v:
                    deps[s] = v
        for s, v in deps.items():
            if s is q.sem and (eng == "pe" or not SAME_ENG):
                continue
            if q.waited.get(s, 0) < v:
                q.ops.append(("w", s, v))
                q.waited[s] = v
        if dsem is not None:
            dsem.count += 16
            tag = (dsem, dsem.count)
            inc = (dsem, 16)
        elif signal:
            q.sem.count += 1
            tag = (q.sem, q.sem.count)
            inc = (q.sem, 1)
        else:
            tag = (q.sem, q.sem.count + 1)
            inc = None
        q.ops.append(("i", fn, inc))
        self.ninst += 1
        for b in reads:
            if b.r.get(tag[0], 0) < tag[1]:
                b.r[tag[0]] = tag[1]
        for b in writes:
            b.w = tag
            b.r = {}

    def barrier(self, engines=None):
        if self.dry:
            return
        for e in engines or self.ENGS:
            q = self.q[e]
            for s in self.sems:
                if s.count > 0 and s.barrier and s is not q.sem and q.waited.get(s, 0) < s.count:
                    q.ops.append(("w", s, s.count))
                    q.waited[s] = s.count

    def mm(self, out, lhsT, rhs, start, stop, reads, writes, signal=None):
        self.emit("pe", lambda e: e.matmul(out, lhsT, rhs, start=start, stop=stop),
                  reads, writes, signal=stop if signal is None else signal)

    def tr(self, out, in_, ident, reads, writes, signal=True):
        self.emit("pe", lambda e: e.transpose(out, in_, ident), reads, writes, signal=signal)

    def act(self, out, in_, func, reads, writes, bias=None, scale=None):
        kw = {}
        if bias is not None:
            kw["bias"] = bias
        if scale is not None:
            kw["scale"] = scale
        self.emit("act", lambda e: e.activation(out=out, in_=in_, func=func, **kw), reads, writes)

    def copy(self, eng, out, in_, reads, writes):
        if eng == "act":
            self.act(out, in_, AF.Copy, reads, writes)
        else:
            self.emit(eng, lambda e: e.tensor_copy(out, in_), reads, writes)

    def tt(self, eng, out, in0, in1, op, reads, writes):
        self.emit(eng, lambda e: e.tensor_tensor(out=out, in0=in0, in1=in1, op=op), reads, writes)

    def stt(self, eng, out, in0, scalar, in1, op0, op1, reads, writes):
        self.emit(eng, lambda e: e.scalar_tensor_tensor(out=out, in0=in0, scalar=scalar, in1=in1, op0=op0, op1=op1),
                  reads, writes)

    def dma(self, eng, out, in_, sem, reads=(), writes=()):
        self.emit(eng, lambda e: e.dma_start(out=out, in_=in_), reads, writes, dsem=sem)

    def finalize(self, stack):
        nc = self.nc
        for s in self.sems:
            if s.count > 0:
                s.h = stack.enter_context(nc.semaphore(s.name))
        block = stack.enter_context(nc.Block())
        decos = {"pe": block.tensor, "act": block.scalar, "dve": block.vector,
                 "pool": block.gpsimd, "sp": block.sync}
        for name in self.ENGS:
            q = self.q[name]

            def body(e, q=q):
                for op in q.ops:
                    if op[0] == "w":
                        e.wait_ge(op[1].h, op[2])
                    else:
                        ins = op[1](e)
                        if op[2] is not None:
                            ins.then_inc(op[2][0].h, op[2][1])

            decos[name](body)


class WStream:
    def __init__(self, P, slots):
        self.P = P
        self.slots = slots
        self.bufs = [P.buf("ws%d" % i, dma=True, barrier=False) for i in range(len(slots))]
        self.plan = []
        self.m = 0
        self.loaded = 0

    def reset(self):
        self.m = 0
        self.loaded = 0

    def next(self, src, rows, n):
        P = self.P
        if P.dry:
            self.plan.append((src, rows, n))
            i = (len(self.plan) - 1) % NSLOT
            return self.slots[i], self.bufs[i]
        m = self.m
        assert self.plan[m][1:] == (rows, n), (m, self.plan[m][1:], rows, n)
        hi = min(m + NSLOT - 1, len(self.plan) - 1)
        while self.loaded <= hi:
            k = self.loaded
            src_k, rows_k, n_k = self.plan[k]
            i = k % NSLOT
            P.dma("sp", self.slots[i][0:rows_k, 0:n_k], src_k, self.bufs[i].sem, writes=[self.bufs[i]])
            self.loaded += 1
        self.m += 1
        i = m % NSLOT
        return self.slots[i], self.bufs[i]


def build(stages=None, dbg=False):
    if stages is None:
        stages = ["setup", "in"]
        for l in range(L):
            stages += ["ffn:%d:0" % l, "m1:%d" % l, "m2:%d" % l, "m3:%d" % l, "m4:%d" % l, "m5:%d" % l, "ffn:%d:1" % l]
        stages += ["out"]
    nc = bass.Bass("TRN2", target_bir_lowering=False)
    skind = "ExternalOutput" if dbg else "Internal"

    def din(name, shape):
        return nc.dram_tensor(name, shape, F32, kind="ExternalInput").ap()

    x_d = din("x", [S, D])
    out_d = nc.dram_tensor("out", [S, D], F32, kind="ExternalOutput").ap()
    wgu_d = din("wgu", [L * 2 * NF, 128, 2048])
    wd_d = din("wd", [L * 2 * 8, 128, 2816])
    winf_d = din("winf", [L * 22, 128, 1024])
    winvb_d = din("winvb", [L, 128, 3072])
    winvc_d = din("winvc", [L * 2, 128, 2048])
    wm5a_d = din("wm5a", [L * 8, 128, 2048])
    wm5b_d = din("wm5b", [L * 8, 128, 1152])
    wm5c_d = din("wm5c", [L * 8, 128, 1536])
    wout_d = din("wout", [L * 8, 128, 1024])
    dftf_d = din("dftf", [8, 128, 2048])
    fconst_d = din("fconst", [128, 520])
    cvec_d = din("cvec", [128, L * NCV])
    cs128_d = din("cs128", [128, 256])
    ident_d = din("ident", [128, 128])
    relb_d = din("relb", [32, 10])
    ohc_d = din("ohc", [33, 1536])
    ohb_d = din("ohb", [33, 3 * 512])
    lamv_d = din("lamv", [L * 4, 64])
    gfin_d = din("gfin", [1, 1024])
    qk_s = nc.dram_tensor("qk_s", [14 * 128, S], F32, kind=skind).ap()
    vb_s = nc.dram_tensor("vb_s", [S, 384], F32, kind=skind).ap()
    vc_s = nc.dram_tensor("vc_s", [S, 512], F32, kind=skind).ap()
    a_s = nc.dram_tensor("a_s", [8 * 128, S], F32, kind=skind).ap()
    z_s = nc.dram_tensor("z_s", [13 * 128, S], F32, kind=skind).ap()
    bvc_s = nc.dram_tensor("bvc_s", [4, 1536], F32, kind=skind).ap()
    bvb_s = nc.dram_tensor("bvb_s", [6, 512], F32, kind=skind).ap()

    with ExitStack() as st:
        P = Prog(nc)
        sb = lambda n, s: st.enter_context(nc.sbuf_tensor(n + "_sb", s, F32))
        xT = sb("xT", [128, DC, S])
        cst = sb("cst", [128, 1024])
        cvec = sb("cvec", [128, L * NCV])
        relbc = sb("relbc", [128, 320])
        wsl = sb("wsl", [128, NSLOT * SLOT])
        arena = sb("arena", [128, ARENA])
        ps = [st.enter_context(nc.psum_tensor("ps%d" % i, [128, 512], F32)) for i in range(8)]
        PS = [P.buf("ps%d" % i) for i in range(8)]
        XT = [P.buf("xT%d" % i) for i in range(NTG)]
        CST = P.buf("cst", dma=True)
        W = WStream(P, [wsl[:, i * SLOT:(i + 1) * SLOT] for i in range(NSLOT)])

        ones_mean = cst[:, 0:128]
        ones_sub = cst[:, 128:256]
        ones1 = cst[:, 256:384]
        ident = cst[:, 384:512]
        cs128 = cst[:, 512:768]
        eps6 = cst[:, 768:769]
        eps5 = cst[:, 769:770]
        lcol = cst[:, 772:780]
        rb33 = cst[0:33, 784:794]

        def tsl(i):
            return slice(i * 128, (i + 1) * 128)

        def gsl(tg):
            return slice(tg * TG, (tg + 1) * TG)

        class Arena:
            def __init__(self):
                self.off = 0

            def take(self, n, rows=128):
                a = arena[0:rows, self.off:self.off + n]
                self.off += n
                assert self.off <= ARENA, self.off
                return a

        def stage_setup():
            A = Arena()
            P.emit("pool", lambda e: e.memset(cst[:, 0:128], 1.0 / D), writes=[CST])
            P.emit("pool", lambda e: e.memset(cst[:, 128:256], 1.0 / 128), writes=[CST])
            P.emit("pool", lambda e: e.memset(cst[:, 256:384], 1.0), writes=[CST])
            P.emit("pool", lambda e: e.memset(cst[:, 768:769], EPS), writes=[CST])
            P.emit("pool", lambda e: e.memset(cst[:, 769:770], SUBLN_EPS), writes=[CST])
            P.emit("pool", lambda e: e.memset(cst[0:33, 784:794], NEG), writes=[CST])
            P.dma("sp", ident, ident_d[:, :], CST.sem, writes=[CST])
            P.dma("sp", cs128, cs128_d[:, :], CST.sem, writes=[CST])
            P.dma("sp", cvec[:], cvec_d[:, :], CST.sem, writes=[CST])
            P.dma("sp", relbc[:], bass.AP(tensor=relb_d.tensor, offset=0, ap=[[0, 128], [1, 320]]), CST.sem, writes=[CST])
            P.dma("sp", cst[0:32, 784:794], relb_d[:, :], CST.sem, writes=[CST])
            lam = A.take(8 * 64).rearrange("p (a b) -> p a b", a=8)
            LB = P.buf("lam", dma=True)
            for i in range(8):
                P.dma("sp", lam[:, i, :], bass.AP(tensor=lamv_d.tensor, offset=i * 64, ap=[[0, 128], [1, 64]]), LB.sem, writes=[LB])
            sc = A.take(8)
            for l in range(L):
                lam_init = 0.8 - 0.6 * math.exp(-0.3 * l)
                for j in range(2):
                    P.tt("dve", lam[:, 4 * l + 2 * j, :], lam[:, 4 * l + 2 * j, :], lam[:, 4 * l + 2 * j + 1, :], ALU.mult, [LB], [LB])
                    P.emit("dve", lambda e, l=l, j=j: e.reduce_sum(out=sc[:, 2 * l + j:2 * l + j + 1], in_=lam[:, 4 * l + 2 * j, :], axis=AX.X), [LB], [LB])
                P.act(sc[:, 2 * l:2 * l + 2], sc[:, 2 * l:2 * l + 2], AF.Exp, [LB], [LB])
                P.tt("dve", sc[:, 4 + l:5 + l], sc[:, 2 * l + 1:2 * l + 2], sc[:, 2 * l:2 * l + 1], ALU.subtract, [LB], [LB])
                P.emit("dve", lambda e, l=l, li=lam_init: e.tensor_scalar(cst[:, 772 + 2 * l:773 + 2 * l], sc[:, 4 + l:5 + l], -li, None, op0=ALU.add), [LB], [CST])
                P.emit("dve", lambda e, l=l, li=lam_init: e.tensor_scalar(cst[:, 773 + 2 * l:774 + 2 * l], cvec[:, l * NCV + 48:l * NCV + 49], 1.0 - li, None, op0=ALU.mult), [CST], [CST])
            oh = A.take(1536, rows=33)
            OH = P.buf("oh", dma=True)
            fv = A.take(1536, rows=4)
            FV = P.buf("fv", dma=True)
            P.dma("sp", oh, ohc_d[:, :], OH.sem, writes=[OH])
            for c3 in range(3):
                P.mm(ps[0][0:4, :], rb33[:, 6:10], oh[:, c3 * 512:(c3 + 1) * 512], True, True, [CST, OH], [PS[0]])
                P.copy("dve", fv[:, c3 * 512:(c3 + 1) * 512], ps[0][0:4, :], [PS[0]], [FV])
            P.dma("pool", bvc_s[:, :], fv, FV.sem, reads=[FV])
            P.dma("sp", oh, ohb_d[:, :], OH.sem, writes=[OH])
            fb = A.take(512, rows=2)
            FB = P.buf("fb", dma=True)
            for g in range(3):
                P.mm(ps[1][0:2, :], rb33[:, 2 * g:2 * g + 2], oh[:, g * 512:(g + 1) * 512], True, True, [CST, OH], [PS[1]])
                P.copy("dve", fb, ps[1][0:2, :], [PS[1]], [FB])
                P.dma("pool", bvb_s[2 * g:2 * g + 2, :], fb, FB.sem, reads=[FB])

        def stage_in():
            A = Arena()
            xin = [A.take(1024) for _ in range(2)]
            XI = [P.buf("xin%d" % i, dma=True) for i in range(2)]
            for tt in range(NT):
                b = tt % 2
                P.dma("sp", xin[b], x_d[tsl(tt), :], XI[b].sem, writes=[XI[b]])
                for cq in range(2):
                    pb = (tt * 2 + cq) % 4
                    for k in range(4):
                        c = cq * 4 + k
                        P.tr(ps[pb][:, k * 128:(k + 1) * 128], xin[b][:, tsl(c)], ident, [XI[b], CST], [PS[pb]], signal=(k == 3))
                    P.copy("dve" if cq == 0 else "act", xT[:, cq * 4:cq * 4 + 4, tsl(tt)],
                           ps[pb][:, :].rearrange("p (a b) -> p a b", a=4), [PS[pb]], [XT[tt // 4]])

        def norm_group(tg, gcol, dst, DST, sq, SQ, rs, RS, psn):
            for c in range(DC):
                b = c % 2
                P.tt("pool", sq[b], xT[:, c, gsl(tg)], xT[:, c, gsl(tg)], ALU.mult, [XT[tg]], [SQ[b]])
                P.mm(ps[psn][:, :], ones_mean, sq[b], c == 0, c == DC - 1, [CST, SQ[b]], [PS[psn]], signal=True)
            P.act(rs, ps[psn][:, :], AF.Sqrt, [PS[psn], CST], [RS], bias=eps6)
            P.emit("dve", lambda e: e.reciprocal(rs, rs), [RS], [RS])
            for c in range(DC):
                P.stt("dve", dst[:, c, :], xT[:, c, gsl(tg)], gcol[:, c:c + 1], rs, ALU.mult, ALU.mult, [XT[tg], RS, CST], [DST])

        def stage_ffn(l, k):
            A = Arena()
            hT = [A.take(DC * TG).rearrange("p (c t) -> p c t", c=DC) for _ in range(2)]
            HT = [P.buf("hT%d" % i) for i in range(2)]
            act = A.take(NF * TG).rearrange("p (f t) -> p f t", f=NF)
            ACTB = [P.buf("act%d" % f) for f in range(NF)]
            sg = [A.take(TG) for _ in range(2)]
            SG = [P.buf("sg%d" % i) for i in range(2)]
            sq = [A.take(TG) for _ in range(2)]
            SQ = [P.buf("sq%d" % i) for i in range(2)]
            rs = A.take(TG)
            RS = P.buf("rs")
            gcol = cvec[:, l * NCV + (0 if k == 0 else 16):l * NCV + (0 if k == 0 else 16) + 8]
            import os
            cut = int(os.environ.get("FFN_CUT", "9"))
            norm_group(0, gcol, hT[0], HT[0], sq, SQ, rs, RS, 6)
            if cut == 1:
                return
            for tg in range(NTG if cut > 4 else (2 if cut == 4 else 1)):
                h, H = hT[tg % 2], HT[tg % 2]
                for f in range(NF):
                    w, WB = W.next(wgu_d[(l * 2 + k) * NF + f], 128, 2048)
                    wv = w[:, 0:2048].rearrange("p (a c j) -> p a c j", a=2, c=DC)
                    pg, pu = f % 2, 2 + f % 2
                    for c in range(DC):
                        P.mm(ps[pg][:, :], wv[:, 0, c, :], h[:, c, :], c == 0, c == DC - 1, [WB, H], [PS[pg]])
                    for c in range(DC):
                        P.mm(ps[pu][:, :], wv[:, 1, c, :], h[:, c, :], c == 0, c == DC - 1, [WB, H], [PS[pu]])
                    P.act(sg[f % 2], ps[pg][:, :], AF.Silu, [PS[pg]], [SG[f % 2]])
                    P.tt("dve", act[:, f, :], sg[f % 2], ps[pu][:, :], ALU.mult, [SG[f % 2], PS[pu]], [ACTB[f]])
                if cut == 2:
                    return
                if tg + 1 < NTG:
                    norm_group(tg + 1, gcol, hT[(tg + 1) % 2], HT[(tg + 1) % 2], sq, SQ, rs, RS, 6)
                for dc in range(DC):
                    w, WB = W.next(wd_d[(l * 2 + k) * 8 + dc], 128, 2816)
                    wv = w[:, 0:2816].rearrange("p (f j) -> p f j", f=NF)
                    py = 4 + dc % 2
                    for f in range(NF):
                        P.mm(ps[py][:, :], wv[:, f, :], act[:, f, :], f == 0, f == NF - 1, [WB, ACTB[f]], [PS[py]])
                    P.stt("dve", xT[:, dc, gsl(tg)], ps[py][:, :], 0.5, xT[:, dc, gsl(tg)], ALU.mult, ALU.add, [PS[py], XT[tg]], [XT[tg]])

        def stage_m1(l):
            A = Arena()
            uT = [A.take(DC * TG).rearrange("p (c t) -> p c t", c=DC) for _ in range(2)]
            UT = [P.buf("uT%d" % i) for i in range(2)]
            sq = [A.take(TG) for _ in range(2)]
            SQ = [P.buf("sq%d" % i) for i in range(2)]
            rs = A.take(TG)
            RS = P.buf("rs")
            stF = [A.take(TG) for _ in range(3)]
            STF = [P.buf("stF%d" % i, dma=True) for i in range(3)]
            aT = [A.take(TG) for _ in range(2)]
            AT = [P.buf("aT%d" % i) for i in range(2)]
            stT = [A.take(256) for _ in range(3)]
            STT = [P.buf("stT%d" % i, dma=True) for i in range(3)]
            stV = [A.take(TG) for _ in range(2)]
            STV = [P.buf("stV%d" % i, dma=True) for i in range(2)]
            gcol = cvec[:, l * NCV + 8:l * NCV + 16]
            norm_group(0, gcol, uT[0], UT[0], sq, SQ, rs, RS, 6)
            nT = 0
            nF = 0
            nV = 0
            for tg in range(NTG):
                u, U = uT[tg % 2], UT[tg % 2]
                for g in range(8):
                    w, WB = W.next(winf_d[l * 22 + g], 128, 1024)
                    wv = w[:, 0:1024].rearrange("p (c j) -> p c j", c=DC)
                    pa = g % 2
                    for c in range(DC):
                        P.mm(ps[pa][:, :], wv[:, c, :], u[:, c, :], c == 0, c == DC - 1, [WB, U], [PS[pa]])
                    P.copy("act" if g % 2 else "dve", stF[nF % 3], ps[pa][:, :], [PS[pa]], [STF[nF % 3]])
                    P.dma("pool", a_s[tsl(g), gsl(tg)], stF[nF % 3], STF[nF % 3].sem, reads=[STF[nF % 3]])
                    nF += 1
                for j in range(14):
                    w, WB = W.next(winf_d[l * 22 + 8 + j], 128, 1024)
                    wv = w[:, 0:1024].rearrange("p (c j) -> p c j", c=DC)
                    pa = j % 2
                    for c in range(DC):
                        P.mm(ps[pa][:, :], wv[:, c, :], u[:, c, :], c == 0, c == DC - 1, [WB, U], [PS[pa]])
                    P.copy("act" if j % 2 else "dve", stF[nF % 3], ps[pa][:, :], [PS[pa]], [STF[nF % 3]])
                    P.dma("pool", qk_s[tsl(j), gsl(tg)], stF[nF % 3], STF[nF % 3].sem, reads=[STF[nF % 3]])
                    nF += 1
                for (src, n, dst, c0) in ((winvb_d[l], 384, vb_s, 0), (winvc_d[l * 2], 256, vc_s, 0), (winvc_d[l * 2 + 1], 256, vc_s, 256)):
                    w, WB = W.next(src, 128, DC * n)
                    wv = w[:, 0:DC * n].rearrange("p (c j) -> p c j", c=DC)
                    for t4 in range(4):
                        pv = 4 + nV % 2
                        for c in range(DC):
                            P.mm(ps[pv][:, 0:n], u[:, c, tsl(t4)], wv[:, c, :], c == 0, c == DC - 1, [WB, U], [PS[pv]])
                        P.copy("act" if nV % 2 else "dve", stV[nV % 2][:, 0:n], ps[pv][:, 0:n], [PS[pv]], [STV[nV % 2]])
                        P.dma("pool", dst[tsl(tg * 4 + t4), c0:c0 + n], stV[nV % 2][:, 0:n], STV[nV % 2].sem, reads=[STV[nV % 2]])
                        nV += 1
                if tg + 1 < NTG:
                    norm_group(tg + 1, gcol, uT[(tg + 1) % 2], UT[(tg + 1) % 2], sq, SQ, rs, RS, 6)

        def stage_m2(l):
            A = Arena()
            H = S // 2
            aT = [A.take(S) for _ in range(2)]
            AT = [P.buf("aTg%d" % i, dma=True) for i in range(2)]
            apf = A.take(H)
            amf = A.take(H)
            FO = P.buf("fold")
            nyb = [A.take(TG) for _ in range(2)]
            NYB = [P.buf("ny%d" % i) for i in range(2)]
            Tf = [A.take(8 * 256).rearrange("p (s j) -> p s j", s=8) for _ in range(2)]
            TF = [P.buf("Tf%d" % i) for i in range(2)]
            osb = [A.take(TG) for _ in range(2)]
            OSB = [P.buf("osb%d" % i) for i in range(2)]
            Y = [A.take(S) for _ in range(2)]
            YB = [P.buf("Y%d" % i, dma=True) for i in range(2)]
            fc = A.take(520)
            FC = P.buf("fc", dma=True)
            sgnn = fc[:, 0:512]
            cnyq = fc[:, 512:520]
            P.dma("sp", fc, fconst_d[:, :], FC.sem, writes=[FC])
            P.emit("pool", lambda e: e.memset(amf[:, 0:1], 0.0), writes=[FO])

            def load(g):
                P.dma("sp", aT[g % 2], a_s[tsl(g), :], AT[g % 2].sem, writes=[AT[g % 2]])

            def fold(g):
                a, AB = aT[g % 2], AT[g % 2]
                P.tt("pool", apf[:, 1:H], a[:, 1:H], a[:, S - 1:H:-1], ALU.add, [AB], [FO])
                P.copy("pool", apf[:, 0:1], a[:, 0:1], [AB], [FO])
                P.tt("dve", amf[:, 1:H], a[:, 1:H], a[:, S - 1:H:-1], ALU.subtract, [AB], [FO])
                ny, NY = nyb[g % 2], NYB[g % 2]
                P.emit("dve", lambda e: e.tensor_scalar(ny, sgnn, a[:, H:H + 1], None, op0=ALU.mult), [AB, FC], [NY])

            def chdft(g):
                t, TB_ = Tf[g % 2], TF[g % 2]
                for st_ in range(8):
                    pc = st_ % 2
                    P.mm(ps[pc][:, 0:128], apf[:, tsl(st_)], cs128[:, 0:128], True, True, [FO, CST], [PS[pc]], signal=False)
                    P.mm(ps[pc][:, 128:256], amf[:, tsl(st_)], cs128[:, 128:256], True, True, [FO, CST], [PS[pc]], signal=True)
                    P.copy("act" if st_ % 2 else "dve", t[:, st_, :], ps[pc][:, 0:256], [PS[pc]], [TB_])

            def seqdft(g):
                t, TB_ = Tf[g % 2], TF[g % 2]
                ny, NY = nyb[g % 2], NYB[g % 2]
                for sg_ in range(2):
                    for mat in range(2):
                        bank = 2 + sg_ if mat == 0 else 4 + sg_
                        for piece in range(2):
                            w, WB = W.next(dftf_d[(sg_ * 2 + mat) * 2 + piece], 128, 2048)
                            wv = w[:, 0:2048].rearrange("p (i j) -> p i j", i=4)
                            for i in range(4):
                                s_t = piece * 4 + i
                                P.mm(ps[bank][:, :], t[:, s_t, mat * 128:(mat + 1) * 128], wv[:, i, :],
                                     piece == 0 and i == 0, mat == 1 and piece == 1 and i == 3,
                                     [WB, TB_], [PS[bank]], signal=(i == 3))
                    P.mm(ps[2 + sg_][:, :], cs128[:, 0:128], ny, False, True, [CST, NY], [PS[2 + sg_]])
                for s_t in range(8):
                    P.mm(ps[6][:, 0:1], t[:, s_t, 0:128], cnyq[:, s_t:s_t + 1], s_t == 0, False, [TB_, FC], [PS[6]], signal=False)
                P.mm(ps[6][:, 0:1], cs128[:, 0:128], ny[:, 0:1], False, True, [CST, NY], [PS[6]])

            def assemble(g):
                y, YB_ = Y[g % 2], YB[g % 2]
                for sg_ in range(2):
                    e_ps, o_ps = ps[2 + sg_], ps[4 + sg_]
                    P.copy("act", osb[sg_], o_ps[:, :], [PS[4 + sg_]], [OSB[sg_]])
                    P.tt("dve", y[:, sg_ * TG:(sg_ + 1) * TG], e_ps[:, :], osb[sg_], ALU.subtract, [PS[2 + sg_], OSB[sg_]], [YB_])
                    if sg_ == 0:
                        P.tt("dve", y[:, S - 1:S - TG:-1], e_ps[:, 1:TG], osb[0][:, 1:TG], ALU.add, [PS[2], OSB[0]], [YB_])
                    else:
                        P.tt("dve", y[:, S - TG:H:-1], e_ps[:, :], osb[1], ALU.add, [PS[3], OSB[1]], [YB_])
                P.copy("act", y[:, H:H + 1], ps[6][:, 0:1], [PS[6]], [YB_])
                P.dma("pool", z_s[tsl(g), :], y, YB_.sem, reads=[YB_])

            load(0)
            load(1)
            fold(0)
            chdft(0)
            for g in range(8):
                if g + 1 < 8:
                    fold(g + 1)
                seqdft(g)
                if g + 1 < 8:
                    chdft(g + 1)
                if g + 2 < 8:
                    load(g + 2)
                assemble(g)

        def stage_m3(l):
            A = Arena()
            qb = A.take(S)
            kb = A.take(S)
            QB = P.buf("qb", dma=True)
            KB = P.buf("kb", dma=True)
            vg = [A.take(NT * 128).rearrange("p (s j) -> p s j", s=NT) for _ in range(2)]
            VG = [P.buf("vg%d" % i, dma=True) for i in range(2)]
            emb = A.take(6 * 384).rearrange("p (a b) -> p a b", a=6)
            EMB = P.buf("emb", dma=True)
            acc = A.take(2 * 2 * S, rows=64).rearrange("p (h k t) -> p h k t", h=2, k=2)
            ACC = [P.buf("acc%d" % i, dma=True) for i in range(2)]
            E = [A.take(384).rearrange("p (a b) -> p a b", a=3) for _ in range(4)]
            EB = [P.buf("E%d" % i) for i in range(4)]
            P.dma("sp", emb, bass.AP(tensor=bvb_s.tensor, offset=0, ap=[[1, 128], [512, 6], [1, 384]]), EMB.sem, writes=[EMB])
            P.act(emb, emb, AF.Exp, [EMB], [EMB])

            def geom(g, i):
                d = DIL[g]
                tpc = NT // d
                tb = i % tpc
                dl = [dd for dd in (-1, 0, 1) if 0 <= tb + dd < tpc]

                def perm(j):
                    r, t_ = j // tpc, j % tpc
                    o = t_ * 128 * d + r
                    return slice(o, o + 127 * d + 1, d)
                return dl, perm

            def load_g(g):
                d = DIL[g]
                tpc = NT // d
                P.dma("sp", qb, qk_s[tsl(g), :], QB.sem, writes=[QB])
                P.dma("sp", kb, qk_s[tsl(3 + g), :], KB.sem, writes=[KB])
                v, VB_ = vg[g % 2], VG[g % 2]
                for r in range(d):
                    for t0 in range(0, tpc, 4):
                        nt_ = min(4, tpc - t0)
                        src = bass.AP(tensor=vb_s.tensor, offset=r * 384 + g * 128 + t0 * 128 * d * 384,
                                      ap=[[d * 384, 128], [128 * d * 384, nt_], [1, 128]])
                        P.dma("sp", v[:, r * tpc + t0:r * tpc + t0 + nt_, :], src, VB_.sem, writes=[VB_])

            upairs = [(g, i) for g in range(3) for i in range(NT)]

            def emit_S(pi):
                g, i = upairs[pi]
                dl, perm = geom(g, i)
                a0, a1 = dl[0] + 1, dl[-1] + 2
                for dd in dl:
                    for hh in range(2):
                        n = 2 * pi + hh
                        hs = slice(hh * 64, (hh + 1) * 64)
                        pss = ps[n % 4][:, 0:384].rearrange("p (a b) -> p a b", a=3)
                        P.mm(pss[:, dd + 1, :], kb[hs, perm(i + dd)], qb[hs, perm(i)], True, True, [KB, QB], [PS[n % 4]], signal=(dd == dl[-1]))
                for hh in range(2):
                    n = 2 * pi + hh
                    pss = ps[n % 4][:, 0:384].rearrange("p (a b) -> p a b", a=3)
                    mview = emb[:, 2 * g + hh, :].rearrange("p (a b) -> p a b", a=3)[:, :, ::-1]
                    e_, EB_ = E[n % 4], EB[n % 4]
                    P.act(e_[:, a0:a1, :], pss[:, a0:a1, :], AF.Exp, [PS[n % 4]], [EB_], scale=0.125)
                    P.tt("dve", e_[:, a0:a1, :], e_[:, a0:a1, :], mview[:, a0:a1, :], ALU.mult, [EB_, EMB], [EB_])

            def emit_PV(pi):
                g, i = upairs[pi]
                dl, perm = geom(g, i)
                v, VB_ = vg[g % 2], VG[g % 2]
                for hh in range(2):
                    n = 2 * pi + hh
                    hs = slice(hh * 64, (hh + 1) * 64)
                    e_, EB_ = E[n % 4], EB[n % 4]
                    pU = 4 + n % 4
                    psu = ps[pU][0:64, 0:256].rearrange("p (a b) -> p a b", a=2)
                    for dd in dl:
                        P.mm(psu[:, 0, :], v[:, i + dd, hs], e_[:, dd + 1, :], dd == dl[0], dd == dl[-1], [VB_, EB_], [PS[pU]], signal=False)
                    for dd in dl:
                        P.mm(psu[:, 1, :], ones1[:, 0:64], e_[:, dd + 1, :], dd == dl[0], dd == dl[-1], [CST, EB_], [PS[pU]])
                    dst = acc[:, hh, :, perm(i)]
                    if g == 0:
                        P.copy("dve", dst, psu, [PS[pU]], [ACC[hh]])
                    else:
                        P.tt("dve", dst, psu, dst, ALU.add, [PS[pU], ACC[hh]], [ACC[hh]])

            NPB = len(upairs)
            load_g(0)
            emit_S(0)
            for pi in range(NPB):
                g, i = upairs[pi]
                if pi + 1 < NPB:
                    if upairs[pi + 1][0] != g:
                        load_g(g + 1)
                    emit_S(pi + 1)
                emit_PV(pi)
            for hh in range(2):
                P.emit("dve", lambda e, hh=hh: e.reciprocal(acc[:, hh, 1, :], acc[:, hh, 1, :]), [ACC[hh]], [ACC[hh]])
                P.tt("dve", acc[:, hh, 0, :], acc[:, hh, 0, :], acc[:, hh, 1, :], ALU.mult, [ACC[hh]], [ACC[hh]])
                P.dma("pool", z_s[8 * 128 + hh * 64:8 * 128 + (hh + 1) * 64, :], acc[:, hh, 0, :], ACC[hh].sem, reads=[ACC[hh]])

        def stage_m4(l):
            A = Arena()
            qh = [A.take(S) for _ in range(2)]
            kh = [A.take(S) for _ in range(2)]
            vh = [A.take(NT * 128).rearrange("p (s j) -> p s j", s=NT) for _ in range(2)]
            strip = [A.take(1408) for _ in range(2)]
            QH = [P.buf("qh%d" % i, dma=True) for i in range(2)]
            KH = [P.buf("kh%d" % i, dma=True) for i in range(2)]
            VH = [P.buf("vh%d" % i, dma=True) for i in range(2)]
            SB_ = [P.buf("strip%d" % i, dma=True) for i in range(2)]
            E = [A.take(TG) for _ in range(4)]
            EB = [P.buf("E%d" % i) for i in range(4)]
            dacc = [[A.take(TG) for _ in range(2)] for _ in range(2)]
            DACC = [[P.buf("dacc%d%d" % (i, j)) for j in range(2)] for i in range(2)]
            r0 = A.take(TG)
            r1 = A.take(TG)
            av = A.take(TG)
            TMP = P.buf("tmp")
            TMP2 = P.buf("tmp2")
            zst = [A.take(TG) for _ in range(2)]
            ZST = [P.buf("zst%d" % i, dma=True) for i in range(2)]
            neglam = lcol[:, 2 * l:2 * l + 1]
            gsub = lcol[:, 2 * l + 1:2 * l + 2]

            def load_head(h):
                b = h % 2
                P.dma("sp", qh[b], qk_s[tsl(6 + h), :], QH[b].sem, writes=[QH[b]])
                P.dma("sp", kh[b], qk_s[tsl(10 + h), :], KH[b].sem, writes=[KH[b]])
                vsrc = vc_s[:, h * 128:(h + 1) * 128].rearrange("(s p) j -> p s j", p=128)
                for q4 in range(4):
                    P.dma("sp", vh[b][:, q4 * 4:(q4 + 1) * 4, :], vsrc[:, q4 * 4:(q4 + 1) * 4, :], VH[b].sem, writes=[VH[b]])
                P.dma("sp", strip[b], bass.AP(tensor=bvc_s.tensor, offset=h * 1536, ap=[[1, 128], [1, 1408]]), SB_[b].sem, writes=[SB_[b]])
                P.act(strip[b], strip[b], AF.Exp, [SB_[b]], [SB_[b]])

            pairs = [(h, qg, kt) for h in range(4) for qg in range(NTG) for kt in range(NT)]
            state = {"s": 0}

            def emit_S(p):
                h, qg, kt = pairs[p]
                b = h % 2
                for m in range(2):
                    n = 2 * p + m
                    ms = slice(m * 64, (m + 1) * 64)
                    P.mm(ps[n % 4][:, :], kh[b][ms, tsl(kt)], qh[b][ms, qg * TG:(qg + 1) * TG], True, True, [KH[b], QH[b]], [PS[n % 4]])

            def emit_exp(n):
                h, qg, kt = pairs[n // 2]
                m = n % 2
                b = h % 2
                Q0 = qg * TG
                pS = n % 4
                e_, EB_ = E[n % 4], EB[n % 4]
                bpos = relbc[:, 31 * 10 + 6 + h:31 * 10 + 7 + h]
                bneg = relbc[:, 15 * 10 + 6 + h:15 * 10 + 7 + h]
                qa = min(max(128 * kt - 640, Q0), Q0 + TG)
                qe = min(max(128 * kt + 768, Q0), Q0 + TG)
                if qa > Q0:
                    P.act(e_[:, 0:qa - Q0], ps[pS][:, 0:qa - Q0], AF.Exp, [PS[pS], CST], [EB_], bias=bpos, scale=0.125)
                if qe > qa:
                    P.act(e_[:, qa - Q0:qe - Q0], ps[pS][:, qa - Q0:qe - Q0], AF.Exp, [PS[pS]], [EB_], scale=0.125)
                    clo = 128 * kt - qe + 768
                    chi = 128 * kt - qa + 767
                    P.tt("dve", e_[:, qa - Q0:qe - Q0], e_[:, qa - Q0:qe - Q0], strip[b][:, clo:chi + 1][:, ::-1], ALU.mult,
                         [EB_, SB_[b]], [EB_])
                if qe < Q0 + TG:
                    P.act(e_[:, qe - Q0:TG], ps[pS][:, qe - Q0:TG], AF.Exp, [PS[pS], CST], [EB_], bias=bneg, scale=0.125)

            def emit_PV(n):
                h, qg, kt = pairs[n // 2]
                m = n % 2
                b = h % 2
                par = (h * NTG + qg) % 2
                e_, EB_ = E[n % 4], EB[n % 4]
                P.mm(ps[4 + m][:, :], vh[b][:, kt, :], e_, kt == 0, kt == NT - 1, [VH[b], EB_], [PS[4 + m]], signal=True)
                eng = "pool" if m == 0 else "dve"
                if kt == 0:
                    P.copy(eng, dacc[par][m], e_, [EB_], [DACC[par][m]])
                else:
                    P.tt(eng, dacc[par][m], dacc[par][m], e_, ALU.add, [EB_, DACC[par][m]], [DACC[par][m]])

            def epilogue_a(h, qg):
                par = (h * NTG + qg) % 2
                P.mm(ps[6][:, :], ones1, dacc[par][0], True, True, [CST, DACC[par][0]], [PS[6]])
                P.mm(ps[7][:, :], ones1, dacc[par][1], True, True, [CST, DACC[par][1]], [PS[7]])
                P.emit("dve", lambda e: e.reciprocal(r0, ps[6][:, :]), [PS[6]], [TMP])
                P.emit("dve", lambda e: e.reciprocal(r1, ps[7][:, :]), [PS[7]], [TMP])
                P.tt("dve", r0, ps[4][:, :], r0, ALU.mult, [PS[4], TMP], [TMP])
                P.tt("dve", r1, ps[5][:, :], r1, ALU.mult, [PS[5], TMP], [TMP])
                P.stt("dve", av, r1, neglam, r0, ALU.mult, ALU.add, [TMP, CST], [TMP2])

            def epilogue_b(h, qg):
                P.tt("pool", r1, av, av, ALU.mult, [TMP2, TMP], [TMP])
                P.mm(ps[6][:, :], ones_sub, r1, True, True, [CST, TMP], [PS[6]])
                P.act(r0, ps[6][:, :], AF.Sqrt, [PS[6], CST, TMP], [TMP], bias=eps5)
                P.emit("dve", lambda e: e.reciprocal(r0, r0), [TMP], [TMP])
                k = state["s"]
                z, Z = zst[k % 2], ZST[k % 2]
                P.stt("dve", z, av, gsub, r0, ALU.mult, ALU.mult, [TMP, TMP2, CST], [Z])
                P.dma("pool", z_s[tsl(9 + h), qg * TG:(qg + 1) * TG], z, Z.sem, reads=[Z])
                state["s"] += 1

            NP = len(pairs)
            load_head(0)
            emit_S(0)
            emit_exp(0)
            emit_exp(1)
            pending = None
            for p in range(NP):
                h, qg, kt = pairs[p]
                if kt == 0 and qg == 0 and h + 1 < 4:
                    load_head(h + 1)
                if p + 1 < NP:
                    emit_S(p + 1)
                    emit_exp(2 * p + 2)
                    emit_exp(2 * p + 3)
                emit_PV(2 * p)
                emit_PV(2 * p + 1)
                if pending is not None and kt == 2:
                    epilogue_b(*pending)
                    pending = None
                if kt == NT - 1:
                    epilogue_a(h, qg)
                    pending = (h, qg)
            epilogue_b(*pending)

        def stage_m5(l):
            A = Arena()
            uT = [A.take(DC * TG).rearrange("p (c t) -> p c t", c=DC) for _ in range(1)]
            UT = [P.buf("uT0")]
            sq = [A.take(TG) for _ in range(2)]
            SQ = [P.buf("sq%d" % i) for i in range(2)]
            rs = A.take(TG)
            RS = P.buf("rs")
            zsl = A.take(13 * TG).rearrange("p (k t) -> p k t", k=13)
            ZS = P.buf("zsl", dma=True)
            mg = A.take(DC * TG).rearrange("p (c t) -> p c t", c=DC)
            MG = [P.buf("mg%d" % i) for i in range(DC)]
            sig = [A.take(TG) for _ in range(2)]
            SIG = [P.buf("sig%d" % i) for i in range(2)]
            tmp = [A.take(TG) for _ in range(2)]
            TMPB = [P.buf("tmp%d" % i) for i in range(2)]
            gcol = cvec[:, l * NCV + 8:l * NCV + 16]
            brs = ((wm5a_d, 8, 0), (wm5b_d, 1, 8), (wm5c_d, 4, 9))
            n = 0
            for tg in range(NTG):
                u, U = uT[0], UT[0]
                norm_group(tg, gcol, u, U, sq, SQ, rs, RS, 6)
                zsrc = z_s[:, gsl(tg)].rearrange("(k p) t -> p k t", p=128)
                for (k0, k1) in ((0, 4), (4, 8), (8, 13)):
                    P.dma("sp", zsl[:, k0:k1, :], zsrc[:, k0:k1, :], ZS.sem, writes=[ZS])
                for dc in range(DC):
                    for br, (wsrc, nk, koff) in enumerate(brs):
                        w, WB = W.next(wsrc[l * 8 + dc], 128, (8 + nk) * 128)
                        wv = w[:, 0:(8 + nk) * 128].rearrange("p (c j) -> p c j", c=8 + nk)
                        pg, py = n % 2, 2 + n % 2
                        for c in range(DC):
                            P.mm(ps[pg][:, :], wv[:, c, :], u[:, c, :], c == 0, c == DC - 1, [WB, U], [PS[pg]])
                        bcol = cvec[:, l * NCV + 24 + br * 8 + dc:l * NCV + 25 + br * 8 + dc]
                        P.act(sig[n % 2], ps[pg][:, :], AF.Sigmoid, [PS[pg], CST], [SIG[n % 2]], bias=bcol)
                        for kk in range(nk):
                            P.mm(ps[py][:, :], wv[:, 8 + kk, :], zsl[:, koff + kk, :], kk == 0, kk == nk - 1, [WB, ZS], [PS[py]])
                        if br == 0:
                            P.tt("dve", mg[:, dc, :], sig[n % 2], ps[py][:, :], ALU.mult, [SIG[n % 2], PS[py]], [MG[dc]])
                        else:
                            P.tt("dve", tmp[n % 2], sig[n % 2], ps[py][:, :], ALU.mult, [SIG[n % 2], PS[py]], [TMPB[n % 2]])
                            P.tt("pool", mg[:, dc, :], mg[:, dc, :], tmp[n % 2], ALU.add, [TMPB[n % 2], MG[dc]], [MG[dc]])
                        n += 1
                for d2 in range(DC):
                    w, WB = W.next(wout_d[l * 8 + d2], 128, 1024)
                    wv = w[:, 0:1024].rearrange("p (c j) -> p c j", c=DC)
                    po = 4 + d2 % 2
                    for c in range(DC):
                        P.mm(ps[po][:, :], wv[:, c, :], mg[:, c, :], c == 0, c == DC - 1, [WB, MG[c]], [PS[po]])
                    P.tt("dve", xT[:, d2, gsl(tg)], ps[po][:, :], xT[:, d2, gsl(tg)], ALU.add, [PS[po], XT[tg]], [XT[tg]])

        def stage_out(raw=False):
            A = Arena()
            xo = [A.take(1024) for _ in range(2)]
            XO = [P.buf("xo%d" % i) for i in range(2)]
            sqo = A.take(1024)
            SQO = P.buf("sqo")
            ss = A.take(8)
            yo = [A.take(1024) for _ in range(2)]
            YO = [P.buf("yo%d" % i, dma=True) for i in range(2)]
            gf = A.take(1024)
            GF = P.buf("gf", dma=True)
            P.dma("sp", gf, bass.AP(tensor=gfin_d.tensor, offset=0, ap=[[0, 128], [1, 1024]]), GF.sem, writes=[GF])
            for tt in range(NT):
                b = tt % 2
                for cq in range(2):
                    pb = (tt * 2 + cq) % 4
                    for k in range(4):
                        c = cq * 4 + k
                        P.tr(ps[pb][:, k * 128:(k + 1) * 128], xT[:, c, tsl(tt)], ident, [XT[tt // 4], CST], [PS[pb]], signal=(k == 3))
                    P.copy("dve" if cq == 0 else "act", xo[b][:, cq * 512:(cq + 1) * 512], ps[pb][:, :], [PS[pb]], [XO[b]])
                if raw:
                    P.copy("dve", yo[b], xo[b], [XO[b]], [YO[b]])
                else:
                    P.tt("pool", sqo, xo[b], xo[b], ALU.mult, [XO[b]], [SQO])
                    P.emit("dve", lambda e: e.reduce_sum(out=ss[:, 0:1], in_=sqo, axis=AX.X), [SQO], [SQO])
                    P.emit("dve", lambda e: e.tensor_scalar(ss[:, 0:1], ss[:, 0:1], 1.0 / D, EPS, op0=ALU.mult, op1=ALU.add), [SQO], [SQO])
                    P.act(ss[:, 0:1], ss[:, 0:1], AF.Sqrt, [SQO], [SQO])
                    P.emit("dve", lambda e: e.reciprocal(ss[:, 0:1], ss[:, 0:1]), [SQO], [SQO])
                    P.stt("dve", yo[b], xo[b], ss[:, 0:1], gf, ALU.mult, ALU.mult, [XO[b], SQO, GF], [YO[b]])
                P.dma("pool", out_d[tsl(tt), :], yo[b], YO[b].sem, reads=[YO[b]])

        def run_all():
            for sname in stages:
                parts = sname.split(":")
                if parts[0] == "setup":
                    stage_setup()
                elif parts[0] == "in":
                    stage_in()
                elif parts[0] == "ffn":
                    stage_ffn(int(parts[1]), int(parts[2]))
                elif parts[0] in ("m1", "m2", "m3", "m4", "m5"):
                    {"m1": stage_m1, "m2": stage_m2, "m3": stage_m3, "m4": stage_m4, "m5": stage_m5}[parts[0]](int(parts[1]))
                elif parts[0] == "out":
                    stage_out(False)
                elif parts[0] == "outraw":
                    stage_out(True)
                else:
                    raise ValueError(sname)
                P.barrier()

        P.dry = True
        run_all()
        P.dry = False
        W.reset()
        run_all()
        P.barrier()
        P.finalize(st)
    return nc, P


def _lhsT_tiles(Wm):
    K, N = Wm.shape
    return Wm.reshape(K // 128, 128, N // 128, 128).transpose(2, 1, 0, 3)


def _rel_bucket(rel):
    rel = np.asarray(rel, dtype=np.int64)
    ret = np.where(rel > 0, 16, 0)
    n = np.abs(rel)
    nf = np.maximum(n, 1).astype(np.float32)
    large = 8 + (np.log(nf / np.float32(8)) / np.float32(math.log(1024 / 8)) * np.float32(8)).astype(np.int32)
    large = np.minimum(large, 15)
    return ret + np.where(n < 8, n, large)


_CONST_CACHE = {}


def _constants():
    if _CONST_CACHE:
        return _CONST_CACHE
    Hh = S // 2
    s = np.arange(Hh, dtype=np.int64)
    ang = 2.0 * np.pi * ((s[:, None] * s[None, :]) % S).astype(np.float64) / S
    norm = 1.0 / math.sqrt(S * 128.0)
    mats = [np.cos(ang) * norm, np.sin(ang) * norm]
    dftf = np.stack([m.reshape(2, 4, 128, 2, 512).transpose(3, 0, 2, 1, 4) for m in mats], axis=1)
    _CONST_CACHE["dftf"] = np.ascontiguousarray(dftf.reshape(8, 128, 2048), dtype=np.float32)
    fconst = np.zeros((128, 520), np.float64)
    fconst[:, 0:512] = (((-1.0) ** np.arange(512)) * norm)[None, :]
    fconst[:, 512:520] = (((-1.0) ** np.arange(128)) * norm)[:, None]
    _CONST_CACHE["fconst"] = np.ascontiguousarray(fconst, dtype=np.float32)
    c = np.arange(128, dtype=np.int64)
    a2 = 2.0 * np.pi * ((c[:, None] * c[None, :]) % 128).astype(np.float64) / 128
    _CONST_CACHE["cs128"] = np.ascontiguousarray(np.concatenate([np.cos(a2), np.sin(a2)], axis=1), dtype=np.float32)
    _CONST_CACHE["ident"] = np.eye(128, dtype=np.float32)
    ohc = np.zeros((33, 1536), np.float32)
    i = np.arange(1535)
    ohc[_rel_bucket(i - 767), i] = 1.0
    _CONST_CACHE["ohc"] = ohc
    ohb = np.zeros((33, 3, 512), np.float32)
    i = np.arange(511)
    for g, d in enumerate(DIL):
        mrel = i - 255
        bk = np.where(np.abs(mrel) <= 64, _rel_bucket(mrel * d), 32)
        ohb[bk, g, i] = 1.0
    _CONST_CACHE["ohb"] = np.ascontiguousarray(ohb.reshape(33, 1536))
    return _CONST_CACHE


def prep_inputs(inp):
    f = lambda a: np.ascontiguousarray(np.asarray(a), dtype=np.float32)
    g = {k: np.asarray(v) for k, v in inp.items()}
    o = dict(_constants())
    wgu, wd, winf, winvb, winvc, wm5a, wm5b, wm5c, wout = [], [], [], [], [], [], [], [], []
    fcols = np.concatenate([np.arange(0, 1792), np.arange(2176, 3200)])
    for l in range(L):
        for (wg_, wu_, wdn_) in ((g["w_ffn1_gate"], g["w_ffn1_up"], g["w_ffn1_down"]),
                                 (g["w_ffn2_gate"], g["w_ffn2_up"], g["w_ffn2_down"])):
            tg_ = _lhsT_tiles(wg_[l])
            tu_ = _lhsT_tiles(wu_[l])
            wgu.append(np.stack([tg_, tu_], axis=2).reshape(NF, 128, 2048))
            wd.append(_lhsT_tiles(wdn_[l]).reshape(8, 128, 2816))
        win = g["w_in"][l]
        winf.append(_lhsT_tiles(win[:, fcols]).reshape(22, 128, 1024))
        winvb.append(win[:, 1792:2176].reshape(8, 128, 384).transpose(1, 0, 2).reshape(1, 128, 3072))
        for hf in range(2):
            winvc.append(win[:, 3200 + hf * 256:3200 + (hf + 1) * 256].reshape(8, 128, 256).transpose(1, 0, 2).reshape(1, 128, 2048))
        for (lst, wb, br) in ((wm5a, g["w_br_a"][l], 0), (wm5b, g["w_br_b"][l], 1), (wm5c, g["w_br_c"][l], 2)):
            gt = _lhsT_tiles(g["w_gate"][l][:, br * 1024:(br + 1) * 1024])
            bt = _lhsT_tiles(wb)
            lst.append(np.concatenate([gt, bt], axis=2).reshape(8, 128, -1))
        wout.append(_lhsT_tiles(g["w_out"][l]).reshape(8, 128, 1024))
    o["wgu"] = f(np.concatenate(wgu, 0))
    o["wd"] = f(np.concatenate(wd, 0))
    o["winf"] = f(np.concatenate(winf, 0))
    o["winvb"] = f(np.concatenate(winvb, 0))
    o["winvc"] = f(np.concatenate(winvc, 0))
    o["wm5a"] = f(np.concatenate(wm5a, 0))
    o["wm5b"] = f(np.concatenate(wm5b, 0))
    o["wm5c"] = f(np.concatenate(wm5c, 0))
    o["wout"] = f(np.concatenate(wout, 0))
    cv = np.zeros((128, L * NCV), np.float32)
    for l in range(L):
        b0 = l * NCV
        cv[:, b0 + 0:b0 + 8] = g["g_ffn1"][l].reshape(8, 128).T
        cv[:, b0 + 8:b0 + 16] = g["g_mix"][l].reshape(8, 128).T
        cv[:, b0 + 16:b0 + 24] = g["g_ffn2"][l].reshape(8, 128).T
        cv[:, b0 + 24:b0 + 48] = g["b_gate"][l].reshape(24, 128).T
        cv[:, b0 + 48] = g["subln_g"][l]
    o["cvec"] = cv
    o["relb"] = f(g["rel_bias"])
    o["lamv"] = f(np.stack([g[k][l] for l in range(L) for k in ("lam_q1", "lam_k1", "lam_q2", "lam_k2")], 0))
    o["gfin"] = f(g["g_final"].reshape(1, 1024))
    return o


_NC_CACHE = {}


def kernel(**inputs):
    shared = prep_inputs(inputs)
    x = np.ascontiguousarray(np.asarray(inputs["x"]), dtype=np.float32)
    if "nc" not in _NC_CACHE:
        _NC_CACHE["nc"] = build()[0]
    nc = _NC_CACHE["nc"]
    in_maps = []
    for b in range(8):
        m = dict(shared)
        m["x"] = x[b]
        in_maps.append(m)
    res = run_bass_kernel_spmd(nc, in_maps, core_ids=list(range(8)))
    return np.stack([np.asarray(r["out"]) for r in res.results], axis=0).astype(np.float32)
```

```python
import math
from contextlib import ExitStack
import numpy as np
import concourse.bass as bass
import concourse.mybir as mybir
from concourse.bass_utils import run_bass_kernel_spmd

F32 = mybir.dt.float32
AF = mybir.ActivationFunctionType
ALU = mybir.AluOpType
AX = mybir.AxisListType

S = 2048
D = 1024
NT = 16
TG = 512
NTG = 4
DC = 8
FF = 2816
NF = 22
L = 2
EPS = 1e-6
SUBLN_EPS = 1e-5
NSLOT = 4
SLOT = 3072
ARENA = 22528
DIL = (1, 4, 16)
NEG = -30000.0
SAME_ENG = True
NCV = 49


class Sem:
    __slots__ = ("name", "h", "count", "barrier")

    def __init__(self, name, barrier=True):
        self.name = name
        self.h = None
        self.count = 0
        self.barrier = barrier


class Buf:
    __slots__ = ("name", "w", "r", "sem")

    def __init__(self, name, sem=None):
        self.name = name
        self.w = None
        self.r = {}
        self.sem = sem


class Q:
    def __init__(self, name, sem):
        self.name = name
        self.ops = []
        self.sem = sem
        self.waited = {}


class Prog:
    ENGS = ("pe", "act", "dve", "pool", "sp")

    def __init__(self, nc):
        self.nc = nc
        self.sems = []
        self.q = {n: Q(n, self.new_sem("e_" + n)) for n in self.ENGS}
        self.ninst = 0
        self.dry = False

    def new_sem(self, name, barrier=True):
        s = Sem("%s_%d" % (name, len(self.sems)), barrier)
        self.sems.append(s)
        return s

    def buf(self, name, dma=False, barrier=True):
        return Buf(name, self.new_sem("d_" + name, barrier) if dma else None)

    def emit(self, eng, fn, reads=(), writes=(), signal=True, dsem=None):
        if self.dry:
            return
        q = self.q[eng]
        deps = {}
        for b in reads:
            if b.w is not None:
                s, v = b.w
                if deps.get(s, 0) < v:
                    deps[s] = v
        for b in writes:
            if b.w is not None:
                s, v = b.w
                if deps.get(s, 0) < v:
                    deps[s] = v
            for s, v in b.r.items():
                if deps.get(s, 0) < v:
                    deps[s] = v
        for s, v in deps.items():
            if s is q.sem and (eng == "pe" or not SAME_ENG):
                continue
            if q.waited.get(s, 0) < v:
                q.ops.append(("w", s, v))
                q.waited[s] = v
        if dsem is not None:
            dsem.count += 16
            tag = (dsem, dsem.count)
            inc = (dsem, 16)
        elif signal:
            q.sem.count += 1
            tag = (q.sem, q.sem.count)
            inc = (q.sem, 1)
        else:
            tag = (q.sem, q.sem.count + 1)
            inc = None
        q.ops.append(("i", fn, inc))
        self.ninst += 1
        for b in reads:
            if b.r.get(tag[0], 0) < tag[1]:
                b.r[tag[0]] = tag[1]
        for b in writes:
            b.w = tag
            b.r = {}

    def barrier(self, engines=None):
        if self.dry:
            return
        for e in engines or self.ENGS:
            q = self.q[e]
            for s in self.sems:
                if s.count > 0 and s.barrier and s is not q.sem and q.waited.get(s, 0) < s.count:
                    q.ops.append(("w", s, s.count))
                    q.waited[s] = s.count

    def mm(self, out, lhsT, rhs, start, stop, reads, writes, signal=None):
        self.emit("pe", lambda e: e.matmul(out, lhsT, rhs, start=start, stop=stop),
                  reads, writes, signal=stop if signal is None else signal)

    def tr(self, out, in_, ident, reads, writes, signal=True):
        self.emit("pe", lambda e: e.transpose(out, in_, ident), reads, writes, signal=signal)

    def act(self, out, in_, func, reads, writes, bias=None, scale=None):
        kw = {}
        if bias is not None:
            kw["bias"] = bias
        if scale is not None:
            kw["scale"] = scale
        self.emit("act", lambda e: e.activation(out=out, in_=in_, func=func, **kw), reads, writes)

    def copy(self, eng, out, in_, reads, writes):
        if eng == "act":
            self.act(out, in_, AF.Copy, reads, writes)
        else:
            self.emit(eng, lambda e: e.tensor_copy(out, in_), reads, writes)

    def tt(self, eng, out, in0, in1, op, reads, writes):
        self.emit(eng, lambda e: e.tensor_tensor(out=out, in0=in0, in1=in1, op=op), reads, writes)

    def stt(self, eng, out, in0, scalar, in1, op0, op1, reads, writes):
        self.emit(eng, lambda e: e.scalar_tensor_tensor(out=out, in0=in0, scalar=scalar, in1=in1, op0=op0, op1=op1),
                  reads, writes)

    def dma(self, eng, out, in_, sem, reads=(), writes=()):
        self.emit(eng, lambda e: e.dma_start(out=out, in_=in_), reads, writes, dsem=sem)

    def finalize(self, stack):
        nc = self.nc
        for s in self.sems:
            if s.count > 0:
                s.h = stack.enter_context(nc.semaphore(s.name))
        block = stack.enter_context(nc.Block())
        decos = {"pe": block.tensor, "act": block.scalar, "dve": block.vector,
                 "pool": block.gpsimd, "sp": block.sync}
        for name in self.ENGS:
            q = self.q[name]

            def body(e, q=q):
                for op in q.ops:
                    if op[0] == "w":
                        e.wait_ge(op[1].h, op[2])
                    else:
                        ins = op[1](e)
                        if op[2] is not None:
                            ins.then_inc(op[2][0].h, op[2][1])

            decos[name](body)


class WStream:
    def __init__(self, P, slots):
        self.P = P
        self.slots = slots
        self.bufs = [P.buf("ws%d" % i, dma=True, barrier=False) for i in range(len(slots))]
        self.plan = []
        self.m = 0
        self.loaded = 0

    def reset(self):
        self.m = 0
        self.loaded = 0

    def next(self, src, rows, n):
        P = self.P
        if P.dry:
            self.plan.append((src, rows, n))
            i = (len(self.plan) - 1) % NSLOT
            return self.slots[i], self.bufs[i]
        m = self.m
        assert self.plan[m][1:] == (rows, n), (m, self.plan[m][1:], rows, n)
        hi = min(m + NSLOT - 1, len(self.plan) - 1)
        while self.loaded <= hi:
            k = self.loaded
            src_k, rows_k, n_k = self.plan[k]
            i = k % NSLOT
            P.dma("sp", self.slots[i][0:rows_k, 0:n_k], src_k, self.bufs[i].sem, writes=[self.bufs[i]])
            self.loaded += 1
        self.m += 1
        i = m % NSLOT
        return self.slots[i], self.bufs[i]


def build(stages=None, dbg=False):
    if stages is None:
        stages = ["setup", "in"]
        for l in range(L):
            stages += ["ffn:%d:0" % l, "m1:%d" % l, "m2:%d" % l, "m3:%d" % l, "m4:%d" % l, "m5:%d" % l, "ffn:%d:1" % l]
        stages += ["out"]
    nc = bass.Bass("TRN2", target_bir_lowering=False)
    skind = "ExternalOutput" if dbg else "Internal"

    def din(name, shape):
        return nc.dram_tensor(name, shape, F32, kind="ExternalInput").ap()

    x_d = din("x", [S, D])
    out_d = nc.dram_tensor("out", [S, D], F32, kind="ExternalOutput").ap()
    wgu_d = din("wgu", [L * 2 * NF, 128, 2048])
    wd_d = din("wd", [L * 2 * 8, 128, 2816])
    winf_d = din("winf", [L * 22, 128, 1024])
    winvb_d = din("winvb", [L, 128, 3072])
    winvc_d = din("winvc", [L * 2, 128, 2048])
    wm5a_d = din("wm5a", [L * 8, 128, 2048])
    wm5b_d = din("wm5b", [L * 8, 128, 1152])
    wm5c_d = din("wm5c", [L * 8, 128, 1536])
    wout_d = din("wout", [L * 8, 128, 1024])
    dftf_d = din("dftf", [8, 128, 2048])
    fconst_d = din("fconst", [128, 520])
    cvec_d = din("cvec", [128, L * NCV])
    cs128_d = din("cs128", [128, 256])
    ident_d = din("ident", [128, 128])
    relb_d = din("relb", [32, 10])
    ohc_d = din("ohc", [33, 1536])
    ohb_d = din("ohb", [33, 3 * 512])
    lamv_d = din("lamv", [L * 4, 64])
    gfin_d = din("gfin", [1, 1024])
    qk_s = nc.dram_tensor("qk_s", [14 * 128, S], F32, kind=skind).ap()
    vb_s = nc.dram_tensor("vb_s", [S, 384], F32, kind=skind).ap()
    vc_s = nc.dram_tensor("vc_s", [S, 512], F32, kind=skind).ap()
    a_s = nc.dram_tensor("a_s", [8 * 128, S], F32, kind=skind).ap()
    z_s = nc.dram_tensor("z_s", [13 * 128, S], F32, kind=skind).ap()
    bvc_s = nc.dram_tensor("bvc_s", [4, 1536], F32, kind=skind).ap()
    bvb_s = nc.dram_tensor("bvb_s", [6, 512], F32, kind=skind).ap()

    with ExitStack() as st:
        P = Prog(nc)
        sb = lambda n, s: st.enter_context(nc.sbuf_tensor(n + "_sb", s, F32))
        xT = sb("xT", [128, DC, S])
        cst = sb("cst", [128, 1024])
        cvec = sb("cvec", [128, L * NCV])
        relbc = sb("relbc", [128, 320])
        wsl = sb("wsl", [128, NSLOT * SLOT])
        arena = sb("arena", [128, ARENA])
        ps = [st.enter_context(nc.psum_tensor("ps%d" % i, [128, 512], F32)) for i in range(8)]
        PS = [P.buf("ps%d" % i) for i in range(8)]
        XT = [P.buf("xT%d" % i) for i in range(NTG)]
        CST = P.buf("cst", dma=True)
        W = WStream(P, [wsl[:, i * SLOT:(i + 1) * SLOT] for i in range(NSLOT)])

        ones_mean = cst[:, 0:128]
        ones_sub = cst[:, 128:256]
        ones1 = cst[:, 256:384]
        ident = cst[:, 384:512]
        cs128 = cst[:, 512:768]
        eps6 = cst[:, 768:769]
        eps5 = cst[:, 769:770]
        lcol = cst[:, 772:780]
        rb33 = cst[0:33, 784:794]

        def tsl(i):
            return slice(i * 128, (i + 1) * 128)

        def gsl(tg):
            return slice(tg * TG, (tg + 1) * TG)

        class Arena:
            def __init__(self):
                self.off = 0

            def take(self, n, rows=128):
                a = arena[0:rows, self.off:self.off + n]
                self.off += n
                assert self.off <= ARENA, self.off
                return a

        def stage_setup():
            A = Arena()
            P.emit("pool", lambda e: e.memset(cst[:, 0:128], 1.0 / D), writes=[CST])
            P.emit("pool", lambda e: e.memset(cst[:, 128:256], 1.0 / 128), writes=[CST])
            P.emit("pool", lambda e: e.memset(cst[:, 256:384], 1.0), writes=[CST])
            P.emit("pool", lambda e: e.memset(cst[:, 768:769], EPS), writes=[CST])
            P.emit("pool", lambda e: e.memset(cst[:, 769:770], SUBLN_EPS), writes=[CST])
            P.emit("pool", lambda e: e.memset(cst[0:33, 784:794], NEG), writes=[CST])
            P.dma("sp", ident, ident_d[:, :], CST.sem, writes=[CST])
            P.dma("sp", cs128, cs128_d[:, :], CST.sem, writes=[CST])
            P.dma("sp", cvec[:], cvec_d[:, :], CST.sem, writes=[CST])
            P.dma("sp", relbc[:], bass.AP(tensor=relb_d.tensor, offset=0, ap=[[0, 128], [1, 320]]), CST.sem, writes=[CST])
            P.dma("sp", cst[0:32, 784:794], relb_d[:, :], CST.sem, writes=[CST])
            lam = A.take(8 * 64).rearrange("p (a b) -> p a b", a=8)
            LB = P.buf("lam", dma=True)
            for i in range(8):
                P.dma("sp", lam[:, i, :], bass.AP(tensor=lamv_d.tensor, offset=i * 64, ap=[[0, 128], [1, 64]]), LB.sem, writes=[LB])
            sc = A.take(8)
            for l in range(L):
                lam_init = 0.8 - 0.6 * math.exp(-0.3 * l)
                for j in range(2):
                    P.tt("dve", lam[:, 4 * l + 2 * j, :], lam[:, 4 * l + 2 * j, :], lam[:, 4 * l + 2 * j + 1, :], ALU.mult, [LB], [LB])
                    P.emit("dve", lambda e, l=l, j=j: e.reduce_sum(out=sc[:, 2 * l + j:2 * l + j + 1], in_=lam[:, 4 * l + 2 * j, :], axis=AX.X), [LB], [LB])
                P.act(sc[:, 2 * l:2 * l + 2], sc[:, 2 * l:2 * l + 2], AF.Exp, [LB], [LB])
                P.tt("dve", sc[:, 4 + l:5 + l], sc[:, 2 * l + 1:2 * l + 2], sc[:, 2 * l:2 * l + 1], ALU.subtract, [LB], [LB])
                P.emit("dve", lambda e, l=l, li=lam_init: e.tensor_scalar(cst[:, 772 + 2 * l:773 + 2 * l], sc[:, 4 + l:5 + l], -li, None, op0=ALU.add), [LB], [CST])
                P.emit("dve", lambda e, l=l, li=lam_init: e.tensor_scalar(cst[:, 773 + 2 * l:774 + 2 * l], cvec[:, l * NCV + 48:l * NCV + 49], 1.0 - li, None, op0=ALU.mult), [CST], [CST])
            oh = A.take(1536, rows=33)
            OH = P.buf("oh", dma=True)
            fv = A.take(1536, rows=4)
            FV = P.buf("fv", dma=True)
            P.dma("sp", oh, ohc_d[:, :], OH.sem, writes=[OH])
            for c3 in range(3):
                P.mm(ps[0][0:4, :], rb33[:, 6:10], oh[:, c3 * 512:(c3 + 1) * 512], True, True, [CST, OH], [PS[0]])
                P.copy("dve", fv[:, c3 * 512:(c3 + 1) * 512], ps[0][0:4, :], [PS[0]], [FV])
            P.dma("pool", bvc_s[:, :], fv, FV.sem, reads=[FV])
            P.dma("sp", oh, ohb_d[:, :], OH.sem, writes=[OH])
            fb = A.take(512, rows=2)
            FB = P.buf("fb", dma=True)
            for g in range(3):
                P.mm(ps[1][0:2, :], rb33[:, 2 * g:2 * g + 2], oh[:, g * 512:(g + 1) * 512], True, True, [CST, OH], [PS[1]])
                P.copy("dve", fb, ps[1][0:2, :], [PS[1]], [FB])
                P.dma("pool", bvb_s[2 * g:2 * g + 2, :], fb, FB.sem, reads=[FB])

        def stage_in():
            A = Arena()
            xin = [A.take(1024) for _ in range(2)]
            XI = [P.buf("xin%d" % i, dma=True) for i in range(2)]
            for tt in range(NT):
                b = tt % 2
                P.dma("sp", xin[b], x_d[tsl(tt), :], XI[b].sem, writes=[XI[b]])
                for cq in range(2):
                    pb = (tt * 2 + cq) % 4
                    for k in range(4):
                        c = cq * 4 + k
                        P.tr(ps[pb][:, k * 128:(k + 1) * 128], xin[b][:, tsl(c)], ident, [XI[b], CST], [PS[pb]], signal=(k == 3))
                    P.copy("dve" if cq == 0 else "act", xT[:, cq * 4:cq * 4 + 4, tsl(tt)],
                           ps[pb][:, :].rearrange("p (a b) -> p a b", a=4), [PS[pb]], [XT[tt // 4]])

        def norm_group(tg, gcol, dst, DST, sq, SQ, rs, RS, psn):
            for c in range(DC):
                b = c % 2
                P.tt("pool", sq[b], xT[:, c, gsl(tg)], xT[:, c, gsl(tg)], ALU.mult, [XT[tg]], [SQ[b]])
                P.mm(ps[psn][:, :], ones_mean, sq[b], c == 0, c == DC - 1, [CST, SQ[b]], [PS[psn]], signal=True)
            P.act(rs, ps[psn][:, :], AF.Sqrt, [PS[psn], CST], [RS], bias=eps6)
            P.emit("dve", lambda e: e.reciprocal(rs, rs), [RS], [RS])
            for c in range(DC):
                P.stt("dve", dst[:, c, :], xT[:, c, gsl(tg)], gcol[:, c:c + 1], rs, ALU.mult, ALU.mult, [XT[tg], RS, CST], [DST])

        def stage_ffn(l, k):
            A = Arena()
            hT = [A.take(DC * TG).rearrange("p (c t) -> p c t", c=DC) for _ in range(2)]
            HT = [P.buf("hT%d" % i) for i in range(2)]
            act = A.take(NF * TG).rearrange("p (f t) -> p f t", f=NF)
            ACTB = [P.buf("act%d" % f) for f in range(NF)]
            sg = [A.take(TG) for _ in range(2)]
            SG = [P.buf("sg%d" % i) for i in range(2)]
            sq = [A.take(TG) for _ in range(2)]
            SQ = [P.buf("sq%d" % i) for i in range(2)]
            rs = A.take(TG)
            RS = P.buf("rs")
            gcol = cvec[:, l * NCV + (0 if k == 0 else 16):l * NCV + (0 if k == 0 else 16) + 8]
            norm_group(0, gcol, hT[0], HT[0], sq, SQ, rs, RS, 6)
            for tg in range(NTG):
                h, H = hT[tg % 2], HT[tg % 2]
                for f in range(NF):
                    w, WB = W.next(wgu_d[(l * 2 + k) * NF + f], 128, 2048)
                    wv = w[:, 0:2048].rearrange("p (a c j) -> p a c j", a=2, c=DC)
                    pg, pu = f % 2, 2 + f % 2
                    for c in range(DC):
                        P.mm(ps[pg][:, :], wv[:, 0, c, :], h[:, c, :], c == 0, c == DC - 1, [WB, H], [PS[pg]])
                    for c in range(DC):
                        P.mm(ps[pu][:, :], wv[:, 1, c, :], h[:, c, :], c == 0, c == DC - 1, [WB, H], [PS[pu]])
                    P.act(sg[f % 2], ps[pg][:, :], AF.Silu, [PS[pg]], [SG[f % 2]])
                    P.tt("dve", act[:, f, :], sg[f % 2], ps[pu][:, :], ALU.mult, [SG[f % 2], PS[pu]], [ACTB[f]])
                if tg + 1 < NTG:
                    norm_group(tg + 1, gcol, hT[(tg + 1) % 2], HT[(tg + 1) % 2], sq, SQ, rs, RS, 6)
                for dc in range(DC):
                    w, WB = W.next(wd_d[(l * 2 + k) * 8 + dc], 128, 2816)
                    wv = w[:, 0:2816].rearrange("p (f j) -> p f j", f=NF)
                    py = 4 + dc % 2
                    for f in range(NF):
                        P.mm(ps[py][:, :], wv[:, f, :], act[:, f, :], f == 0, f == NF - 1, [WB, ACTB[f]], [PS[py]])
                    P.stt("dve", xT[:, dc, gsl(tg)], ps[py][:, :], 0.5, xT[:, dc, gsl(tg)], ALU.mult, ALU.add, [PS[py], XT[tg]], [XT[tg]])

        def stage_m1(l):
            A = Arena()
            uT = [A.take(DC * TG).rearrange("p (c t) -> p c t", c=DC) for _ in range(2)]
            UT = [P.buf("uT%d" % i) for i in range(2)]
            sq = [A.take(TG) for _ in range(2)]
            SQ = [P.buf("sq%d" % i) for i in range(2)]
            rs = A.take(TG)
            RS = P.buf("rs")
            stF = [A.take(TG) for _ in range(3)]
            STF = [P.buf("stF%d" % i, dma=True) for i in range(3)]
            stV = [A.take(TG) for _ in range(2)]
            STV = [P.buf("stV%d" % i, dma=True) for i in range(2)]
            gcol = cvec[:, l * NCV + 8:l * NCV + 16]
            norm_group(0, gcol, uT[0], UT[0], sq, SQ, rs, RS, 6)
            nF = 0
            nV = 0
            for tg in range(NTG):
                u, U = uT[tg % 2], UT[tg % 2]
                for g in range(8):
                    w, WB = W.next(winf_d[l * 22 + g], 128, 1024)
                    wv = w[:, 0:1024].rearrange("p (c j) -> p c j", c=DC)
                    pa = g % 2
                    for c in range(DC):
                        P.mm(ps[pa][:, :], wv[:, c, :], u[:, c, :], c == 0, c == DC - 1, [WB, U], [PS[pa]])
                    P.copy("act" if g % 2 else "dve", stF[nF % 3], ps[pa][:, :], [PS[pa]], [STF[nF % 3]])
                    P.dma("pool", a_s[tsl(g), gsl(tg)], stF[nF % 3], STF[nF % 3].sem, reads=[STF[nF % 3]])
                    nF += 1
                for j in range(14):
                    w, WB = W.next(winf_d[l * 22 + 8 + j], 128, 1024)
                    wv = w[:, 0:1024].rearrange("p (c j) -> p c j", c=DC)
                    pa = j % 2
                    for c in range(DC):
                        P.mm(ps[pa][:, :], wv[:, c, :], u[:, c, :], c == 0, c == DC - 1, [WB, U], [PS[pa]])
                    P.copy("act" if j % 2 else "dve", stF[nF % 3], ps[pa][:, :], [PS[pa]], [STF[nF % 3]])
                    P.dma("pool", qk_s[tsl(j), gsl(tg)], stF[nF % 3], STF[nF % 3].sem, reads=[STF[nF % 3]])
                    nF += 1
                for (src, n, dst, c0) in ((winvb_d[l], 384, vb_s, 0), (winvc_d[l * 2], 256, vc_s, 0), (winvc_d[l * 2 + 1], 256, vc_s, 256)):
                    w, WB = W.next(src, 128, DC * n)
                    wv = w[:, 0:DC * n].rearrange("p (c j) -> p c j", c=DC)
                    for t4 in range(4):
                        pv = 4 + nV % 2
                        for c in range(DC):
                            P.mm(ps[pv][:, 0:n], u[:, c, tsl(t4)], wv[:, c, :], c == 0, c == DC - 1, [WB, U], [PS[pv]])
                        P.copy("act" if nV % 2 else "dve", stV[nV % 2][:, 0:n], ps[pv][:, 0:n], [PS[pv]], [STV[nV % 2]])
                        P.dma("pool", dst[tsl(tg * 4 + t4), c0:c0 + n], stV[nV % 2][:, 0:n], STV[nV % 2].sem, reads=[STV[nV % 2]])
                        nV += 1
                if tg + 1 < NTG:
                    norm_group(tg + 1, gcol, uT[(tg + 1) % 2], UT[(tg + 1) % 2], sq, SQ, rs, RS, 6)

        def stage_m2(l):
            A = Arena()
            H = S // 2
            aT = [A.take(S) for _ in range(2)]
            AT = [P.buf("aTg%d" % i, dma=True) for i in range(2)]
            apf = A.take(H)
            amf = A.take(H)
            FO = P.buf("fold")
            nyb = [A.take(TG) for _ in range(2)]
            NYB = [P.buf("ny%d" % i) for i in range(2)]
            Tf = [A.take(8 * 256).rearrange("p (s j) -> p s j", s=8) for _ in range(2)]
            TF = [P.buf("Tf%d" % i) for i in range(2)]
            osb = [A.take(TG) for _ in range(2)]
            OSB = [P.buf("osb%d" % i) for i in range(2)]
            Y = [A.take(S) for _ in range(2)]
            YB = [P.buf("Y%d" % i, dma=True) for i in range(2)]
            fc = A.take(520)
            FC = P.buf("fc", dma=True)
            sgnn = fc[:, 0:512]
            cnyq = fc[:, 512:520]
            P.dma("sp", fc, fconst_d[:, :], FC.sem, writes=[FC])
            P.emit("pool", lambda e: e.memset(amf[:, 0:1], 0.0), writes=[FO])

            def load(g):
                P.dma("sp", aT[g % 2], a_s[tsl(g), :], AT[g % 2].sem, writes=[AT[g % 2]])

            def fold(g):
                a, AB = aT[g % 2], AT[g % 2]
                P.tt("pool", apf[:, 1:H], a[:, 1:H], a[:, S - 1:H:-1], ALU.add, [AB], [FO])
                P.copy("pool", apf[:, 0:1], a[:, 0:1], [AB], [FO])
                P.tt("dve", amf[:, 1:H], a[:, 1:H], a[:, S - 1:H:-1], ALU.subtract, [AB], [FO])
                ny, NY = nyb[g % 2], NYB[g % 2]
                P.emit("dve", lambda e: e.tensor_scalar(ny, sgnn, a[:, H:H + 1], None, op0=ALU.mult), [AB, FC], [NY])

            def chdft(g):
                t, TB_ = Tf[g % 2], TF[g % 2]
                for st_ in range(8):
                    pc = st_ % 2
                    P.mm(ps[pc][:, 0:128], apf[:, tsl(st_)], cs128[:, 0:128], True, True, [FO, CST], [PS[pc]], signal=False)
                    P.mm(ps[pc][:, 128:256], amf[:, tsl(st_)], cs128[:, 128:256], True, True, [FO, CST], [PS[pc]], signal=True)
                    P.copy("act" if st_ % 2 else "dve", t[:, st_, :], ps[pc][:, 0:256], [PS[pc]], [TB_])

            def seqdft(g):
                t, TB_ = Tf[g % 2], TF[g % 2]
                ny, NY = nyb[g % 2], NYB[g % 2]
                for sg_ in range(2):
                    for mat in range(2):
                        bank = 2 + sg_ if mat == 0 else 4 + sg_
                        for piece in range(2):
                            w, WB = W.next(dftf_d[(sg_ * 2 + mat) * 2 + piece], 128, 2048)
                            wv = w[:, 0:2048].rearrange("p (i j) -> p i j", i=4)
                            for i in range(4):
                                s_t = piece * 4 + i
                                P.mm(ps[bank][:, :], t[:, s_t, mat * 128:(mat + 1) * 128], wv[:, i, :],
                                     piece == 0 and i == 0, mat == 1 and piece == 1 and i == 3,
                                     [WB, TB_], [PS[bank]], signal=(i == 3))
                    P.mm(ps[2 + sg_][:, :], cs128[:, 0:128], ny, False, True, [CST, NY], [PS[2 + sg_]])
                for s_t in range(8):
                    P.mm(ps[6][:, 0:1], t[:, s_t, 0:128], cnyq[:, s_t:s_t + 1], s_t == 0, False, [TB_, FC], [PS[6]], signal=False)
                P.mm(ps[6][:, 0:1], cs128[:, 0:128], ny[:, 0:1], False, True, [CST, NY], [PS[6]])

            def assemble(g):
                y, YB_ = Y[g % 2], YB[g % 2]
                for sg_ in range(2):
                    e_ps, o_ps = ps[2 + sg_], ps[4 + sg_]
                    P.copy("act", osb[sg_], o_ps[:, :], [PS[4 + sg_]], [OSB[sg_]])
                    P.tt("dve", y[:, sg_ * TG:(sg_ + 1) * TG], e_ps[:, :], osb[sg_], ALU.subtract, [PS[2 + sg_], OSB[sg_]], [YB_])
                    if sg_ == 0:
                        P.tt("dve", y[:, S - 1:S - TG:-1], e_ps[:, 1:TG], osb[0][:, 1:TG], ALU.add, [PS[2], OSB[0]], [YB_])
                    else:
                        P.tt("dve", y[:, S - TG:H:-1], e_ps[:, :], osb[1], ALU.add, [PS[3], OSB[1]], [YB_])
                P.copy("act", y[:, H:H + 1], ps[6][:, 0:1], [PS[6]], [YB_])
                P.dma("pool", z_s[tsl(g), :], y, YB_.sem, reads=[YB_])

            load(0)
            load(1)
            fold(0)
            chdft(0)
            for g in range(8):
                if g + 1 < 8:
                    fold(g + 1)
                seqdft(g)
                if g + 1 < 8:
                    chdft(g + 1)
                if g + 2 < 8:
                    load(g + 2)
                assemble(g)

        def stage_m3(l):
            A = Arena()
            qb = A.take(S)
            kb = A.take(S)
            QB = P.buf("qb", dma=True)
            KB = P.buf("kb", dma=True)
            vg = [A.take(NT * 128).rearrange("p (s j) -> p s j", s=NT) for _ in range(2)]
            VG = [P.buf("vg%d" % i, dma=True) for i in range(2)]
            emb = A.take(6 * 384).rearrange("p (a b) -> p a b", a=6)
            EMB = P.buf("emb", dma=True)
            acc = A.take(2 * 2 * S, rows=64).rearrange("p (h k t) -> p h k t", h=2, k=2)
            ACC = [P.buf("acc%d" % i, dma=True) for i in range(2)]
            E = [A.take(384).rearrange("p (a b) -> p a b", a=3) for _ in range(4)]
            EB = [P.buf("E%d" % i) for i in range(4)]
            P.dma("sp", emb, bass.AP(tensor=bvb_s.tensor, offset=0, ap=[[1, 128], [512, 6], [1, 384]]), EMB.sem, writes=[EMB])
            P.act(emb, emb, AF.Exp, [EMB], [EMB])

            def geom(g, i):
                d = DIL[g]
                tpc = NT // d
                tb = i % tpc
                dl = [dd for dd in (-1, 0, 1) if 0 <= tb + dd < tpc]

                def perm(j):
                    r, t_ = j // tpc, j % tpc
                    o = t_ * 128 * d + r
                    return slice(o, o + 127 * d + 1, d)
                return dl, perm

            def load_g(g):
                d = DIL[g]
                tpc = NT // d
                P.dma("sp", qb, qk_s[tsl(g), :], QB.sem, writes=[QB])
                P.dma("sp", kb, qk_s[tsl(3 + g), :], KB.sem, writes=[KB])
                v, VB_ = vg[g % 2], VG[g % 2]
                for r in range(d):
                    for t0 in range(0, tpc, 4):
                        nt_ = min(4, tpc - t0)
                        src = bass.AP(tensor=vb_s.tensor, offset=r * 384 + g * 128 + t0 * 128 * d * 384,
                                      ap=[[d * 384, 128], [128 * d * 384, nt_], [1, 128]])
                        P.dma("sp", v[:, r * tpc + t0:r * tpc + t0 + nt_, :], src, VB_.sem, writes=[VB_])

            upairs = [(g, i) for g in range(3) for i in range(NT)]

            def emit_S(pi):
                g, i = upairs[pi]
                dl, perm = geom(g, i)
                a0, a1 = dl[0] + 1, dl[-1] + 2
                for dd in dl:
                    for hh in range(2):
                        n = 2 * pi + hh
                        hs = slice(hh * 64, (hh + 1) * 64)
                        pss = ps[n % 4][:, 0:384].rearrange("p (a b) -> p a b", a=3)
                        P.mm(pss[:, dd + 1, :], kb[hs, perm(i + dd)], qb[hs, perm(i)], True, True, [KB, QB], [PS[n % 4]], signal=(dd == dl[-1]))
                for hh in range(2):
                    n = 2 * pi + hh
                    pss = ps[n % 4][:, 0:384].rearrange("p (a b) -> p a b", a=3)
                    mview = emb[:, 2 * g + hh, :].rearrange("p (a b) -> p a b", a=3)[:, :, ::-1]
                    e_, EB_ = E[n % 4], EB[n % 4]
                    P.act(e_[:, a0:a1, :], pss[:, a0:a1, :], AF.Exp, [PS[n % 4]], [EB_], scale=0.125)
                    P.tt("dve", e_[:, a0:a1, :], e_[:, a0:a1, :], mview[:, a0:a1, :], ALU.mult, [EB_, EMB], [EB_])

            def emit_PV(pi):
                g, i = upairs[pi]
                dl, perm = geom(g, i)
                v, VB_ = vg[g % 2], VG[g % 2]
                for hh in range(2):
                    n = 2 * pi + hh
                    hs = slice(hh * 64, (hh + 1) * 64)
                    e_, EB_ = E[n % 4], EB[n % 4]
                    pU = 4 + n % 4
                    psu = ps[pU][0:64, 0:256].rearrange("p (a b) -> p a b", a=2)
                    for dd in dl:
                        P.mm(psu[:, 0, :], v[:, i + dd, hs], e_[:, dd + 1, :], dd == dl[0], dd == dl[-1], [VB_, EB_], [PS[pU]], signal=False)
                    for dd in dl:
                        P.mm(psu[:, 1, :], ones1[:, 0:64], e_[:, dd + 1, :], dd == dl[0], dd == dl[-1], [CST, EB_], [PS[pU]])
                    dst = acc[:, hh, :, perm(i)]
                    if g == 0:
                        P.copy("dve", dst, psu, [PS[pU]], [ACC[hh]])
                    else:
                        P.tt("dve", dst, psu, dst, ALU.add, [PS[pU], ACC[hh]], [ACC[hh]])

            NPB = len(upairs)
            load_g(0)
            emit_S(0)
            for pi in range(NPB):
                g, i = upairs[pi]
                if pi + 1 < NPB:
                    if upairs[pi + 1][0] != g:
                        load_g(g + 1)
                    emit_S(pi + 1)
                emit_PV(pi)
            for hh in range(2):
                P.emit("dve", lambda e, hh=hh: e.reciprocal(acc[:, hh, 1, :], acc[:, hh, 1, :]), [ACC[hh]], [ACC[hh]])
                P.tt("dve", acc[:, hh, 0, :], acc[:, hh, 0, :], acc[:, hh, 1, :], ALU.mult, [ACC[hh]], [ACC[hh]])
                P.dma("pool", z_s[8 * 128 + hh * 64:8 * 128 + (hh + 1) * 64, :], acc[:, hh, 0, :], ACC[hh].sem, reads=[ACC[hh]])

        def stage_m4(l):
            A = Arena()
            qh = [A.take(S) for _ in range(2)]
            kh = [A.take(S) for _ in range(2)]
            vh = [A.take(NT * 128).rearrange("p (s j) -> p s j", s=NT) for _ in range(2)]
            strip = [A.take(1408) for _ in range(2)]
            QH = [P.buf("qh%d" % i, dma=True) for i in range(2)]
            KH = [P.buf("kh%d" % i, dma=True) for i in range(2)]
            VH = [P.buf("vh%d" % i, dma=True) for i in range(2)]
            SB_ = [P.buf("strip%d" % i, dma=True) for i in range(2)]
            E = [A.take(TG) for _ in range(4)]
            EB = [P.buf("E%d" % i) for i in range(4)]
            dacc = [[A.take(TG) for _ in range(2)] for _ in range(2)]
            DACC = [[P.buf("dacc%d%d" % (i, j)) for j in range(2)] for i in range(2)]
            r0 = A.take(TG)
            r1 = A.take(TG)
            av = A.take(TG)
            TMP = P.buf("tmp")
            TMP2 = P.buf("tmp2")
            zst = [A.take(TG) for _ in range(2)]
            ZST = [P.buf("zst%d" % i, dma=True) for i in range(2)]
            neglam = lcol[:, 2 * l:2 * l + 1]
            gsub = lcol[:, 2 * l + 1:2 * l + 2]

            def load_head(h):
                b = h % 2
                P.dma("sp", qh[b], qk_s[tsl(6 + h), :], QH[b].sem, writes=[QH[b]])
                P.dma("sp", kh[b], qk_s[tsl(10 + h), :], KH[b].sem, writes=[KH[b]])
                vsrc = vc_s[:, h * 128:(h + 1) * 128].rearrange("(s p) j -> p s j", p=128)
                for q4 in range(4):
                    P.dma("sp", vh[b][:, q4 * 4:(q4 + 1) * 4, :], vsrc[:, q4 * 4:(q4 + 1) * 4, :], VH[b].sem, writes=[VH[b]])
                P.dma("sp", strip[b], bass.AP(tensor=bvc_s.tensor, offset=h * 1536, ap=[[1, 128], [1, 1408]]), SB_[b].sem, writes=[SB_[b]])
                P.act(strip[b], strip[b], AF.Exp, [SB_[b]], [SB_[b]])

            pairs = [(h, qg, kt) for h in range(4) for qg in range(NTG) for kt in range(NT)]
            state = {"s": 0}

            def emit_S(p):
                h, qg, kt = pairs[p]
                b = h % 2
                for m in range(2):
                    n = 2 * p + m
                    ms = slice(m * 64, (m + 1) * 64)
                    P.mm(ps[n % 4][:, :], kh[b][ms, tsl(kt)], qh[b][ms, qg * TG:(qg + 1) * TG], True, True, [KH[b], QH[b]], [PS[n % 4]])

            def emit_exp(n):
                h, qg, kt = pairs[n // 2]
                m = n % 2
                b = h % 2
                Q0 = qg * TG
                pS = n % 4
                e_, EB_ = E[n % 4], EB[n % 4]
                bpos = relbc[:, 31 * 10 + 6 + h:31 * 10 + 7 + h]
                bneg = relbc[:, 15 * 10 + 6 + h:15 * 10 + 7 + h]
                qa = min(max(128 * kt - 640, Q0), Q0 + TG)
                qe = min(max(128 * kt + 768, Q0), Q0 + TG)
                if qa > Q0:
                    P.act(e_[:, 0:qa - Q0], ps[pS][:, 0:qa - Q0], AF.Exp, [PS[pS], CST], [EB_], bias=bpos, scale=0.125)
                if qe > qa:
                    P.act(e_[:, qa - Q0:qe - Q0], ps[pS][:, qa - Q0:qe - Q0], AF.Exp, [PS[pS]], [EB_], scale=0.125)
                    clo = 128 * kt - qe + 768
                    chi = 128 * kt - qa + 767
                    P.tt("dve", e_[:, qa - Q0:qe - Q0], e_[:, qa - Q0:qe - Q0], strip[b][:, clo:chi + 1][:, ::-1], ALU.mult,
                         [EB_, SB_[b]], [EB_])
                if qe < Q0 + TG:
                    P.act(e_[:, qe - Q0:TG], ps[pS][:, qe - Q0:TG], AF.Exp, [PS[pS], CST], [EB_], bias=bneg, scale=0.125)

            def emit_PV(n):
                h, qg, kt = pairs[n // 2]
                m = n % 2
                b = h % 2
                par = (h * NTG + qg) % 2
                e_, EB_ = E[n % 4], EB[n % 4]
                P.mm(ps[4 + m][:, :], vh[b][:, kt, :], e_, kt == 0, kt == NT - 1, [VH[b], EB_], [PS[4 + m]], signal=True)
                eng = "pool" if m == 0 else "dve"
                if kt == 0:
                    P.copy(eng, dacc[par][m], e_, [EB_], [DACC[par][m]])
                else:
                    P.tt(eng, dacc[par][m], dacc[par][m], e_, ALU.add, [EB_, DACC[par][m]], [DACC[par][m]])

            def epilogue_a(h, qg):
                par = (h * NTG + qg) % 2
                P.mm(ps[6][:, :], ones1, dacc[par][0], True, True, [CST, DACC[par][0]], [PS[6]])
                P.mm(ps[7][:, :], ones1, dacc[par][1], True, True, [CST, DACC[par][1]], [PS[7]])
                P.emit("dve", lambda e: e.reciprocal(r0, ps[6][:, :]), [PS[6]], [TMP])
                P.emit("dve", lambda e: e.reciprocal(r1, ps[7][:, :]), [PS[7]], [TMP])
                P.tt("dve", r0, ps[4][:, :], r0, ALU.mult, [PS[4], TMP], [TMP])
                P.tt("dve", r1, ps[5][:, :], r1, ALU.mult, [PS[5], TMP], [TMP])
                P.stt("dve", av, r1, neglam, r0, ALU.mult, ALU.add, [TMP, CST], [TMP2])

            def epilogue_b(h, qg):
                P.tt("pool", r1, av, av, ALU.mult, [TMP2, TMP], [TMP])
                P.mm(ps[6][:, :], ones_sub, r1, True, True, [CST, TMP], [PS[6]])
                P.act(r0, ps[6][:, :], AF.Sqrt, [PS[6], CST, TMP], [TMP], bias=eps5)
                P.emit("dve", lambda e: e.reciprocal(r0, r0), [TMP], [TMP])
                k = state["s"]
                z, Z = zst[k % 2], ZST[k % 2]
                P.stt("dve", z, av, gsub, r0, ALU.mult, ALU.mult, [TMP, TMP2, CST], [Z])
                P.dma("pool", z_s[tsl(9 + h), qg * TG:(qg + 1) * TG], z, Z.sem, reads=[Z])
                state["s"] += 1

            NP = len(pairs)
            load_head(0)
            emit_S(0)
            emit_exp(0)
            emit_exp(1)
            pending = None
            for p in range(NP):
                h, qg, kt = pairs[p]
                if kt == 0 and qg == 0 and h + 1 < 4:
                    load_head(h + 1)
                if p + 1 < NP:
                    emit_S(p + 1)
                    emit_exp(2 * p + 2)
                    emit_exp(2 * p + 3)
                emit_PV(2 * p)
                emit_PV(2 * p + 1)
                if pending is not None and kt == 2:
                    epilogue_b(*pending)
                    pending = None
                if kt == NT - 1:
                    epilogue_a(h, qg)
                    pending = (h, qg)
            epilogue_b(*pending)

        def stage_m5(l):
            A = Arena()
            uT = [A.take(DC * TG).rearrange("p (c t) -> p c t", c=DC) for _ in range(1)]
            UT = [P.buf("uT0")]
            sq = [A.take(TG) for _ in range(2)]
            SQ = [P.buf("sq%d" % i) for i in range(2)]
            rs = A.take(TG)
            RS = P.buf("rs")
            zsl = A.take(13 * TG).rearrange("p (k t) -> p k t", k=13)
            ZS = P.buf("zsl", dma=True)
            mg = A.take(DC * TG).rearrange("p (c t) -> p c t", c=DC)
            MG = [P.buf("mg%d" % i) for i in range(DC)]
            sig = [A.take(TG) for _ in range(2)]
            SIG = [P.buf("sig%d" % i) for i in range(2)]
            tmp = [A.take(TG) for _ in range(2)]
            TMPB = [P.buf("tmp%d" % i) for i in range(2)]
            gcol = cvec[:, l * NCV + 8:l * NCV + 16]
            brs = ((wm5a_d, 8, 0), (wm5b_d, 1, 8), (wm5c_d, 4, 9))
            n = 0
            for tg in range(NTG):
                u, U = uT[0], UT[0]
                norm_group(tg, gcol, u, U, sq, SQ, rs, RS, 6)
                zsrc = z_s[:, gsl(tg)].rearrange("(k p) t -> p k t", p=128)
                for (k0, k1) in ((0, 4), (4, 8), (8, 13)):
                    P.dma("sp", zsl[:, k0:k1, :], zsrc[:, k0:k1, :], ZS.sem, writes=[ZS])
                for dc in range(DC):
                    for br, (wsrc, nk, koff) in enumerate(brs):
                        w, WB = W.next(wsrc[l * 8 + dc], 128, (8 + nk) * 128)
                        wv = w[:, 0:(8 + nk) * 128].rearrange("p (c j) -> p c j", c=8 + nk)
                        pg, py = n % 2, 2 + n % 2
                        for c in range(DC):
                            P.mm(ps[pg][:, :], wv[:, c, :], u[:, c, :], c == 0, c == DC - 1, [WB, U], [PS[pg]])
                        bcol = cvec[:, l * NCV + 24 + br * 8 + dc:l * NCV + 25 + br * 8 + dc]
                        P.act(sig[n % 2], ps[pg][:, :], AF.Sigmoid, [PS[pg], CST], [SIG[n % 2]], bias=bcol)
                        for kk in range(nk):
                            P.mm(ps[py][:, :], wv[:, 8 + kk, :], zsl[:, koff + kk, :], kk == 0, kk == nk - 1, [WB, ZS], [PS[py]])
                        if br == 0:
                            P.tt("dve", mg[:, dc, :], sig[n % 2], ps[py][:, :], ALU.mult, [SIG[n % 2], PS[py]], [MG[dc]])
                        else:
                            P.tt("dve", tmp[n % 2], sig[n % 2], ps[py][:, :], ALU.mult, [SIG[n % 2], PS[py]], [TMPB[n % 2]])
                            P.tt("pool", mg[:, dc, :], mg[:, dc, :], tmp[n % 2], ALU.add, [TMPB[n % 2], MG[dc]], [MG[dc]])
                        n += 1
                for d2 in range(DC):
                    w, WB = W.next(wout_d[l * 8 + d2], 128, 1024)
                    wv = w[:, 0:1024].rearrange("p (c j) -> p c j", c=DC)
                    po = 4 + d2 % 2
                    for c in range(DC):
                        P.mm(ps[po][:, :], wv[:, c, :], mg[:, c, :], c == 0, c == DC - 1, [WB, MG[c]], [PS[po]])
                    P.tt("dve", xT[:, d2, gsl(tg)], ps[po][:, :], xT[:, d2, gsl(tg)], ALU.add, [PS[po], XT[tg]], [XT[tg]])

        def stage_out(raw=False):
            A = Arena()
            xo = [A.take(1024) for _ in range(2)]
            XO = [P.buf("xo%d" % i) for i in range(2)]
            sqo = A.take(1024)
            SQO = P.buf("sqo")
            ss = A.take(8)
            yo = [A.take(1024) for _ in range(2)]
            YO = [P.buf("yo%d" % i, dma=True) for i in range(2)]
            gf = A.take(1024)
            GF = P.buf("gf", dma=True)
            P.dma("sp", gf, bass.AP(tensor=gfin_d.tensor, offset=0, ap=[[0, 128], [1, 1024]]), GF.sem, writes=[GF])
            for tt in range(NT):
                b = tt % 2
                for cq in range(2):
                    pb = (tt * 2 + cq) % 4
                    for k in range(4):
                        c = cq * 4 + k
                        P.tr(ps[pb][:, k * 128:(k + 1) * 128], xT[:, c, tsl(tt)], ident, [XT[tt // 4], CST], [PS[pb]], signal=(k == 3))
                    P.copy("dve" if cq == 0 else "act", xo[b][:, cq * 512:(cq + 1) * 512], ps[pb][:, :], [PS[pb]], [XO[b]])
                if raw:
                    P.copy("dve", yo[b], xo[b], [XO[b]], [YO[b]])
                else:
                    P.tt("pool", sqo, xo[b], xo[b], ALU.mult, [XO[b]], [SQO])
                    P.emit("dve", lambda e: e.reduce_sum(out=ss[:, 0:1], in_=sqo, axis=AX.X), [SQO], [SQO])
                    P.emit("dve", lambda e: e.tensor_scalar(ss[:, 0:1], ss[:, 0:1], 1.0 / D, EPS, op0=ALU.mult, op1=ALU.add), [SQO], [SQO])
                    P.act(ss[:, 0:1], ss[:, 0:1], AF.Sqrt, [SQO], [SQO])
                    P.emit("dve", lambda e: e.reciprocal(ss[:, 0:1], ss[:, 0:1]), [SQO], [SQO])
                    P.stt("dve", yo[b], xo[b], ss[:, 0:1], gf, ALU.mult, ALU.mult, [XO[b], SQO, GF], [YO[b]])
                P.dma("pool", out_d[tsl(tt), :], yo[b], YO[b].sem, reads=[YO[b]])

        def run_all():
            for sname in stages:
                parts = sname.split(":")
                if parts[0] == "setup":
                    stage_setup()
                elif parts[0] == "in":
                    stage_in()
                elif parts[0] == "ffn":
                    stage_ffn(int(parts[1]), int(parts[2]))
                elif parts[0] in ("m1", "m2", "m3", "m4", "m5"):
                    {"m1": stage_m1, "m2": stage_m2, "m3": stage_m3, "m4": stage_m4, "m5": stage_m5}[parts[0]](int(parts[1]))
                elif parts[0] == "out":
                    stage_out(False)
                elif parts[0] == "outraw":
                    stage_out(True)
                else:
                    raise ValueError(sname)
                P.barrier()

        P.dry = True
        run_all()
        P.dry = False
        W.reset()
        run_all()
        P.barrier()
        P.finalize(st)
    return nc, P


def _lhsT_tiles(Wm):
    K, N = Wm.shape
    return Wm.reshape(K // 128, 128, N // 128, 128).transpose(2, 1, 0, 3)


def _rel_bucket(rel):
    rel = np.asarray(rel, dtype=np.int64)
    ret = np.where(rel > 0, 16, 0)
    n = np.abs(rel)
    nf = np.maximum(n, 1).astype(np.float32)
    large = 8 + (np.log(nf / np.float32(8)) / np.float32(math.log(1024 / 8)) * np.float32(8)).astype(np.int32)
    large = np.minimum(large, 15)
    return ret + np.where(n < 8, n, large)


_CONST_CACHE = {}


def _constants():
    if _CONST_CACHE:
        return _CONST_CACHE
    Hh = S // 2
    s = np.arange(Hh, dtype=np.int64)
    ang = 2.0 * np.pi * ((s[:, None] * s[None, :]) % S).astype(np.float64) / S
    norm = 1.0 / math.sqrt(S * 128.0)
    mats = [np.cos(ang) * norm, np.sin(ang) * norm]
    dftf = np.stack([m.reshape(2, 4, 128, 2, 512).transpose(3, 0, 2, 1, 4) for m in mats], axis=1)
    _CONST_CACHE["dftf"] = np.ascontiguousarray(dftf.reshape(8, 128, 2048), dtype=np.float32)
    fconst = np.zeros((128, 520), np.float64)
    fconst[:, 0:512] = (((-1.0) ** np.arange(512)) * norm)[None, :]
    fconst[:, 512:520] = (((-1.0) ** np.arange(128)) * norm)[:, None]
    _CONST_CACHE["fconst"] = np.ascontiguousarray(fconst, dtype=np.float32)
    c = np.arange(128, dtype=np.int64)
    a2 = 2.0 * np.pi * ((c[:, None] * c[None, :]) % 128).astype(np.float64) / 128
    _CONST_CACHE["cs128"] = np.ascontiguousarray(np.concatenate([np.cos(a2), np.sin(a2)], axis=1), dtype=np.float32)
    _CONST_CACHE["ident"] = np.eye(128, dtype=np.float32)
    ohc = np.zeros((33, 1536), np.float32)
    i = np.arange(1535)
    ohc[_rel_bucket(i - 767), i] = 1.0
    _CONST_CACHE["ohc"] = ohc
    ohb = np.zeros((33, 3, 512), np.float32)
    i = np.arange(511)
    for g, d in enumerate(DIL):
        mrel = i - 255
        bk = np.where(np.abs(mrel) <= 64, _rel_bucket(mrel * d), 32)
        ohb[bk, g, i] = 1.0
    _CONST_CACHE["ohb"] = np.ascontiguousarray(ohb.reshape(33, 1536))
    return _CONST_CACHE


def prep_inputs(inp):
    f = lambda a: np.ascontiguousarray(np.asarray(a), dtype=np.float32)
    g = {k: np.asarray(v) for k, v in inp.items()}
    o = dict(_constants())
    wgu, wd, winf, winvb, winvc, wm5a, wm5b, wm5c, wout = [], [], [], [], [], [], [], [], []
    fcols = np.concatenate([np.arange(0, 1792), np.arange(2176, 3200)])
    for l in range(L):
        for (wg_, wu_, wdn_) in ((g["w_ffn1_gate"], g["w_ffn1_up"], g["w_ffn1_down"]),
                                 (g["w_ffn2_gate"], g["w_ffn2_up"], g["w_ffn2_down"])):
            tg_ = _lhsT_tiles(wg_[l])
            tu_ = _lhsT_tiles(wu_[l])
            wgu.append(np.stack([tg_, tu_], axis=2).reshape(NF, 128, 2048))
            wd.append(_lhsT_tiles(wdn_[l]).reshape(8, 128, 2816))
        win = g["w_in"][l]
        winf.append(_lhsT_tiles(win[:, fcols]).reshape(22, 128, 1024))
        winvb.append(win[:, 1792:2176].reshape(8, 128, 384).transpose(1, 0, 2).reshape(1, 128, 3072))
        for hf in range(2):
            winvc.append(win[:, 3200 + hf * 256:3200 + (hf + 1) * 256].reshape(8, 128, 256).transpose(1, 0, 2).reshape(1, 128, 2048))
        for (lst, wb, br) in ((wm5a, g["w_br_a"][l], 0), (wm5b, g["w_br_b"][l], 1), (wm5c, g["w_br_c"][l], 2)):
            gt = _lhsT_tiles(g["w_gate"][l][:, br * 1024:(br + 1) * 1024])
            bt = _lhsT_tiles(wb)
            lst.append(np.concatenate([gt, bt], axis=2).reshape(8, 128, -1))
        wout.append(_lhsT_tiles(g["w_out"][l]).reshape(8, 128, 1024))
    o["wgu"] = f(np.concatenate(wgu, 0))
    o["wd"] = f(np.concatenate(wd, 0))
    o["winf"] = f(np.concatenate(winf, 0))
    o["winvb"] = f(np.concatenate(winvb, 0))
    o["winvc"] = f(np.concatenate(winvc, 0))
    o["wm5a"] = f(np.concatenate(wm5a, 0))
    o["wm5b"] = f(np.concatenate(wm5b, 0))
    o["wm5c"] = f(np.concatenate(wm5c, 0))
    o["wout"] = f(np.concatenate(wout, 0))
    cv = np.zeros((128, L * NCV), np.float32)
    for l in range(L):
        b0 = l * NCV
        cv[:, b0 + 0:b0 + 8] = g["g_ffn1"][l].reshape(8, 128).T
        cv[:, b0 + 8:b0 + 16] = g["g_mix"][l].reshape(8, 128).T
        cv[:, b0 + 16:b0 + 24] = g["g_ffn2"][l].reshape(8, 128).T
        cv[:, b0 + 24:b0 + 48] = g["b_gate"][l].reshape(24, 128).T
        cv[:, b0 + 48] = g["subln_g"][l]
    o["cvec"] = cv
    o["relb"] = f(g["rel_bias"])
    o["lamv"] = f(np.stack([g[k][l] for l in range(L) for k in ("lam_q1", "lam_k1", "lam_q2", "lam_k2")], 0))
    o["gfin"] = f(g["g_final"].reshape(1, 1024))
    return o


_NC_CACHE = {}


def kernel(**inputs):
    shared = prep_inputs(inputs)
    x = np.ascontiguousarray(np.asarray(inputs["x"]), dtype=np.float32)
    if "nc" not in _NC_CACHE:
        _NC_CACHE["nc"] = build()[0]
    nc = _NC_CACHE["nc"]
    in_maps = []
    for b in range(8):
        m = dict(shared)
        m["x"] = x[b]
        in_maps.append(m)
    res = run_bass_kernel_spmd(nc, in_maps, core_ids=list(range(8)))
    return np.stack([np.asarray(r["out"]) for r in res.results], axis=0).astype(np.float32)
```

```python
import math
from contextlib import ExitStack
import numpy as np
import concourse.bass as bass
import concourse.mybir as mybir
from concourse.bass_utils import run_bass_kernel_spmd

F32 = mybir.dt.float32
BF16 = mybir.dt.bfloat16
AF = mybir.ActivationFunctionType
ALU = mybir.AluOpType
AX = mybir.AxisListType

S = 2048
D = 1024
NT = 16
TG = 512
NTG = 4
DC = 8
FF = 2816
NF = 22
L = 2
EPS = 1e-6
SUBLN_EPS = 1e-5
NSLOT = 3
SLOT = 3072
ARENA = 25600
DIL = (1, 4, 16)
NEG = -30000.0
SAME_ENG = True
NCV = 49


class Sem:
    __slots__ = ("name", "h", "count", "barrier")

    def __init__(self, name, barrier=True):
        self.name = name
        self.h = None
        self.count = 0
        self.barrier = barrier


class Buf:
    __slots__ = ("name", "w", "r", "sem")

    def __init__(self, name, sem=None):
        self.name = name
        self.w = None
        self.r = {}
        self.sem = sem


class Q:
    def __init__(self, name, sem):
        self.name = name
        self.ops = []
        self.sem = sem
        self.waited = {}


class Prog:
    ENGS = ("pe", "act", "dve", "pool", "sp")

    def __init__(self, nc):
        self.nc = nc
        self.sems = []
        self.q = {n: Q(n, self.new_sem("e_" + n)) for n in self.ENGS}
        self.ninst = 0
        self.dry = False

    def new_sem(self, name, barrier=True):
        s = Sem("%s_%d" % (name, len(self.sems)), barrier)
        self.sems.append(s)
        return s

    def buf(self, name, dma=False, barrier=True):
        return Buf(name, self.new_sem("d_" + name, barrier) if dma else None)

    def emit(self, eng, fn, reads=(), writes=(), signal=True, dsem=None):
        if self.dry:
            return
        q = self.q[eng]
        deps = {}
        for b in reads:
            if b.w is not None:
                s, v = b.w
                if deps.get(s, 0) < v:
                    deps[s] = v
        for b in writes:
            if b.w is not None:
                s, v = b.w
                if deps.get(s, 0) < v:
                    deps[s] = v
            for s, v in b.r.items():
                if deps.get(s, 0) < v:
                    deps[s] = v
        for s, v in deps.items():
            if s is q.sem and (eng == "pe" or not SAME_ENG):
                continue
            if q.waited.get(s, 0) < v:
                q.ops.append(("w", s, v))
                q.waited[s] = v
        if dsem is not None:
            dsem.count += 16
            tag = (dsem, dsem.count)
            inc = (dsem, 16)
        elif signal:
            q.sem.count += 1
            tag = (q.sem, q.sem.count)
            inc = (q.sem, 1)
        else:
            tag = (q.sem, q.sem.count + 1)
            inc = None
        q.ops.append(("i", fn, inc))
        self.ninst += 1
        for b in reads:
            if b.r.get(tag[0], 0) < tag[1]:
                b.r[tag[0]] = tag[1]
        for b in writes:
            b.w = tag
            b.r = {}

    def barrier(self, engines=None):
        if self.dry:
            return
        for e in engines or self.ENGS:
            q = self.q[e]
            for s in self.sems:
                if s.count > 0 and s.barrier and s is not q.sem and q.waited.get(s, 0) < s.count:
                    q.ops.append(("w", s, s.count))
                    q.waited[s] = s.count

    def mm(self, out, lhsT, rhs, start, stop, reads, writes, signal=None):
        self.emit("pe", lambda e: e.matmul(out, lhsT, rhs, start=start, stop=stop),
                  reads, writes, signal=stop if signal is None else signal)

    def tr(self, out, in_, ident, reads, writes, signal=True):
        self.emit("pe", lambda e: e.transpose(out, in_, ident), reads, writes, signal=signal)

    def act(self, out, in_, func, reads, writes, bias=None, scale=None):
        kw = {}
        if bias is not None:
            kw["bias"] = bias
        if scale is not None:
            kw["scale"] = scale
        self.emit("act", lambda e: e.activation(out=out, in_=in_, func=func, **kw), reads, writes)

    def copy(self, eng, out, in_, reads, writes):
        if eng == "act":
            self.act(out, in_, AF.Copy, reads, writes)
        else:
            self.emit(eng, lambda e: e.tensor_copy(out, in_), reads, writes)

    def tt(self, eng, out, in0, in1, op, reads, writes):
        self.emit(eng, lambda e: e.tensor_tensor(out=out, in0=in0, in1=in1, op=op), reads, writes)

    def stt(self, eng, out, in0, scalar, in1, op0, op1, reads, writes):
        self.emit(eng, lambda e: e.scalar_tensor_tensor(out=out, in0=in0, scalar=scalar, in1=in1, op0=op0, op1=op1),
                  reads, writes)

    def dma(self, eng, out, in_, sem, reads=(), writes=()):
        self.emit(eng, lambda e: e.dma_start(out=out, in_=in_), reads, writes, dsem=sem)

    def finalize(self, stack):
        nc = self.nc
        for s in self.sems:
            if s.count > 0:
                s.h = stack.enter_context(nc.semaphore(s.name))
        block = stack.enter_context(nc.Block())
        decos = {"pe": block.tensor, "act": block.scalar, "dve": block.vector,
                 "pool": block.gpsimd, "sp": block.sync}
        for name in self.ENGS:
            q = self.q[name]

            def body(e, q=q):
                for op in q.ops:
                    if op[0] == "w":
                        e.wait_ge(op[1].h, op[2])
                    else:
                        ins = op[1](e)
                        if op[2] is not None:
                            ins.then_inc(op[2][0].h, op[2][1])

            decos[name](body)


class WStream:
    def __init__(self, P, slots):
        self.P = P
        self.slots = slots
        self.bufs = [P.buf("ws%d" % i, dma=True, barrier=False) for i in range(len(slots))]
        self.plan = []
        self.m = 0
        self.loaded = 0

    def reset(self):
        self.m = 0
        self.loaded = 0

    def next(self, src, rows, n):
        P = self.P
        if P.dry:
            self.plan.append((src, rows, n))
            i = (len(self.plan) - 1) % NSLOT
            return self.slots[i], self.bufs[i]
        m = self.m
        assert self.plan[m][1:] == (rows, n), (m, self.plan[m][1:], rows, n)
        hi = min(m + NSLOT - 1, len(self.plan) - 1)
        while self.loaded <= hi:
            k = self.loaded
            src_k, rows_k, n_k = self.plan[k]
            i = k % NSLOT
            P.dma("sp", self.slots[i][0:rows_k, 0:n_k], src_k, self.bufs[i].sem, writes=[self.bufs[i]])
            self.loaded += 1
        self.m += 1
        i = m % NSLOT
        return self.slots[i], self.bufs[i]


def build(stages=None, dbg=False):
    if stages is None:
        stages = ["setup", "in"]
        for l in range(L):
            stages += ["ffn:%d:0" % l, "m1:%d" % l, "m2:%d" % l, "m3:%d" % l, "m4:%d" % l, "m5:%d" % l, "ffn:%d:1" % l]
        stages += ["out"]
    nc = bass.Bass("TRN2", target_bir_lowering=False)
    skind = "ExternalOutput" if dbg else "Internal"

    def din(name, shape):
        return nc.dram_tensor(name, shape, F32, kind="ExternalInput").ap()

    x_d = din("x", [S, D])
    out_d = nc.dram_tensor("out", [S, D], F32, kind="ExternalOutput").ap()
    wgu_d = din("wgu", [L * 2 * NF, 128, 2048])
    wd_d = din("wd", [L * 2 * 8, 128, 2816])
    winf_d = din("winf", [L * 22, 128, 1024])
    winvb_d = din("winvb", [L, 128, 3072])
    winvc_d = din("winvc", [L * 2, 128, 2048])
    wm5a_d = din("wm5a", [L * 8, 128, 2048])
    wm5b_d = din("wm5b", [L * 8, 128, 1152])
    wm5c_d = din("wm5c", [L * 8, 128, 1536])
    wout_d = din("wout", [L * 8, 128, 1024])
    dftf_d = din("dftf", [8, 128, 2048])
    fconst_d = din("fconst", [128, 520])
    cvec_d = din("cvec", [128, L * NCV])
    cs128_d = din("cs128", [128, 256])
    ident_d = din("ident", [128, 128])
    relb_d = din("relb", [32, 10])
    ohc_d = din("ohc", [33, 1536])
    ohb_d = din("ohb", [33, 3 * 512])
    lamv_d = din("lamv", [L * 4, 64])
    gfin_d = din("gfin", [1, 1024])
    qk_s = nc.dram_tensor("qk_s", [14 * 128, S], F32, kind=skind).ap()
    vb_s = nc.dram_tensor("vb_s", [S, 384], F32, kind=skind).ap()
    vc_s = nc.dram_tensor("vc_s", [S, 512], F32, kind=skind).ap()
    a_s = nc.dram_tensor("a_s", [8 * 128, S], F32, kind=skind).ap()
    z_s = nc.dram_tensor("z_s", [13 * 128, S], F32, kind=skind).ap()
    bvc_s = nc.dram_tensor("bvc_s", [4, 1536], F32, kind=skind).ap()
    bvb_s = nc.dram_tensor("bvb_s", [6, 512], F32, kind=skind).ap()

    with ExitStack() as st:
        P = Prog(nc)
        sb = lambda n, s: st.enter_context(nc.sbuf_tensor(n + "_sb", s, F32))
        xT = sb("xT", [128, DC, S])
        cst = sb("cst", [128, 1024])
        cvec = sb("cvec", [128, L * NCV])
        relbc = sb("relbc", [128, 320])
        wsl = sb("wsl", [128, NSLOT * SLOT])
        arena = sb("arena", [128, ARENA])
        ps = [st.enter_context(nc.psum_tensor("ps%d" % i, [128, 512], F32)) for i in range(8)]
        PS = [P.buf("ps%d" % i) for i in range(8)]
        XT = [P.buf("xT%d" % i) for i in range(NTG)]
        CST = P.buf("cst", dma=True)
        W = WStream(P, [wsl[:, i * SLOT:(i + 1) * SLOT] for i in range(NSLOT)])

        ones_mean = cst[:, 0:128]
        ones_sub = cst[:, 128:256]
        ones1 = cst[:, 256:384]
        ident = cst[:, 384:512]
        cs128 = cst[:, 512:768]
        eps6 = cst[:, 768:769]
        eps5 = cst[:, 769:770]
        lcol = cst[:, 772:780]
        rb33 = cst[0:33, 784:794]

        def tsl(i):
            return slice(i * 128, (i + 1) * 128)

        def gsl(tg):
            return slice(tg * TG, (tg + 1) * TG)

        class Arena:
            def __init__(self):
                self.off = 0

            def take(self, n, rows=128):
                a = arena[0:rows, self.off:self.off + n]
                self.off += n
                assert self.off <= ARENA, self.off
                return a

        def stage_setup():
            A = Arena()
            P.emit("pool", lambda e: e.memset(cst[:, 0:128], 1.0 / D), writes=[CST])
            P.emit("pool", lambda e: e.memset(cst[:, 128:256], 1.0 / 128), writes=[CST])
            P.emit("pool", lambda e: e.memset(cst[:, 256:384], 1.0), writes=[CST])
            P.emit("pool", lambda e: e.memset(cst[:, 768:769], EPS), writes=[CST])
            P.emit("pool", lambda e: e.memset(cst[:, 769:770], SUBLN_EPS), writes=[CST])
            P.emit("pool", lambda e: e.memset(cst[0:33, 784:794], NEG), writes=[CST])
            P.dma("sp", ident, ident_d[:, :], CST.sem, writes=[CST])
            P.dma("sp", cs128, cs128_d[:, :], CST.sem, writes=[CST])
            P.dma("sp", cvec[:], cvec_d[:, :], CST.sem, writes=[CST])
            P.dma("sp", relbc[:], bass.AP(tensor=relb_d.tensor, offset=0, ap=[[0, 128], [1, 320]]), CST.sem, writes=[CST])
            P.dma("sp", cst[0:32, 784:794], relb_d[:, :], CST.sem, writes=[CST])
            lam = A.take(8 * 64).rearrange("p (a b) -> p a b", a=8)
            LB = P.buf("lam", dma=True)
            for i in range(8):
                P.dma("sp", lam[:, i, :], bass.AP(tensor=lamv_d.tensor, offset=i * 64, ap=[[0, 128], [1, 64]]), LB.sem, writes=[LB])
            sc = A.take(8)
            for l in range(L):
                lam_init = 0.8 - 0.6 * math.exp(-0.3 * l)
                for j in range(2):
                    P.tt("dve", lam[:, 4 * l + 2 * j, :], lam[:, 4 * l + 2 * j, :], lam[:, 4 * l + 2 * j + 1, :], ALU.mult, [LB], [LB])
                    P.emit("dve", lambda e, l=l, j=j: e.reduce_sum(out=sc[:, 2 * l + j:2 * l + j + 1], in_=lam[:, 4 * l + 2 * j, :], axis=AX.X), [LB], [LB])
                P.act(sc[:, 2 * l:2 * l + 2], sc[:, 2 * l:2 * l + 2], AF.Exp, [LB], [LB])
                P.tt("dve", sc[:, 4 + l:5 + l], sc[:, 2 * l + 1:2 * l + 2], sc[:, 2 * l:2 * l + 1], ALU.subtract, [LB], [LB])
                P.emit("dve", lambda e, l=l, li=lam_init: e.tensor_scalar(cst[:, 772 + 2 * l:773 + 2 * l], sc[:, 4 + l:5 + l], -li, None, op0=ALU.add), [LB], [CST])
                P.emit("dve", lambda e, l=l, li=lam_init: e.tensor_scalar(cst[:, 773 + 2 * l:774 + 2 * l], cvec[:, l * NCV + 48:l * NCV + 49], 1.0 - li, None, op0=ALU.mult), [CST], [CST])
            oh = A.take(1536, rows=33)
            OH = P.buf("oh", dma=True)
            fv = A.take(1536, rows=4)
            FV = P.buf("fv", dma=True)
            P.dma("sp", oh, ohc_d[:, :], OH.sem, writes=[OH])
            for c3 in range(3):
                P.mm(ps[0][0:4, :], rb33[:, 6:10], oh[:, c3 * 512:(c3 + 1) * 512], True, True, [CST, OH], [PS[0]])
                P.copy("dve", fv[:, c3 * 512:(c3 + 1) * 512], ps[0][0:4, :], [PS[0]], [FV])
            P.dma("pool", bvc_s[:, :], fv, FV.sem, reads=[FV])
            P.dma("sp", oh, ohb_d[:, :], OH.sem, writes=[OH])
            fb = A.take(512, rows=2)
            FB = P.buf("fb", dma=True)
            for g in range(3):
                P.mm(ps[1][0:2, :], rb33[:, 2 * g:2 * g + 2], oh[:, g * 512:(g + 1) * 512], True, True, [CST, OH], [PS[1]])
                P.copy("dve", fb, ps[1][0:2, :], [PS[1]], [FB])
                P.dma("pool", bvb_s[2 * g:2 * g + 2, :], fb, FB.sem, reads=[FB])

        def stage_in():
            A = Arena()
            xin = [A.take(1024) for _ in range(2)]
            XI = [P.buf("xin%d" % i, dma=True) for i in range(2)]
            for tt in range(NT):
                b = tt % 2
                P.dma("sp", xin[b], x_d[tsl(tt), :], XI[b].sem, writes=[XI[b]])
                for cq in range(2):
                    pb = (tt * 2 + cq) % 4
                    for k in range(4):
                        c = cq * 4 + k
                        P.tr(ps[pb][:, k * 128:(k + 1) * 128], xin[b][:, tsl(c)], ident, [XI[b], CST], [PS[pb]], signal=(k == 3))
                    P.copy("dve" if cq == 0 else "act", xT[:, cq * 4:cq * 4 + 4, tsl(tt)],
                           ps[pb][:, :].rearrange("p (a b) -> p a b", a=4), [PS[pb]], [XT[tt // 4]])

        def norm_group(tg, gcol, dst, DST, sq, SQ, rs, RS, psn):
            for c in range(DC):
                b = c % 2
                P.tt("pool", sq[b], xT[:, c, gsl(tg)], xT[:, c, gsl(tg)], ALU.mult, [XT[tg]], [SQ[b]])
                P.mm(ps[psn][:, :], ones_mean, sq[b], c == 0, c == DC - 1, [CST, SQ[b]], [PS[psn]], signal=True)
            P.act(rs, ps[psn][:, :], AF.Sqrt, [PS[psn], CST], [RS], bias=eps6)
            P.emit("dve", lambda e: e.reciprocal(rs, rs), [RS], [RS])
            for c in range(DC):
                P.stt("dve", dst[:, c, :], xT[:, c, gsl(tg)], gcol[:, c:c + 1], rs, ALU.mult, ALU.mult, [XT[tg], RS, CST], [DST])

        def stage_ffn(l, k):
            A = Arena()

            def bfv(n):
                return A.take(n // 2).bitcast(BF16)

            hh = bfv(DC * TG).rearrange("p (c t) -> p c t", c=DC)
            hl = bfv(DC * TG).rearrange("p (c t) -> p c t", c=DC)
            HH = [P.buf("hh%d" % c) for c in range(DC)]
            HL = [P.buf("hl%d" % c) for c in range(DC)]
            ah = bfv(NF * TG).rearrange("p (f t) -> p f t", f=NF)
            al = bfv(NF * TG).rearrange("p (f t) -> p f t", f=NF)
            AH = [P.buf("ah%d" % f) for f in range(NF)]
            AL = [P.buf("al%d" % f) for f in range(NF)]
            wh = [bfv(2816) for _ in range(2)]
            wl = [bfv(2816) for _ in range(2)]
            WH = [P.buf("wh%d" % i) for i in range(2)]
            WL = [P.buf("wl%d" % i) for i in range(2)]
            sg = [A.take(TG) for _ in range(2)]
            SG = [P.buf("sg%d" % i) for i in range(2)]
            sq = [A.take(TG) for _ in range(2)]
            SQ = [P.buf("sq%d" % i) for i in range(2)]
            rs = A.take(TG)
            RS = P.buf("rs")
            gcol = cvec[:, l * NCV + (0 if k == 0 else 16):l * NCV + (0 if k == 0 else 16) + 8]

            def norm_split(tg):
                for c in range(DC):
                    b = c % 2
                    P.tt("pool", sq[b], xT[:, c, gsl(tg)], xT[:, c, gsl(tg)], ALU.mult, [XT[tg]], [SQ[b]])
                    P.mm(ps[6][:, :], ones_mean, sq[b], c == 0, c == DC - 1, [CST, SQ[b]], [PS[6]], signal=True)
                P.act(rs, ps[6][:, :], AF.Sqrt, [PS[6], CST], [RS], bias=eps6)
                P.emit("dve", lambda e: e.reciprocal(rs, rs), [RS], [RS])
                for c in range(DC):
                    b = c % 2
                    P.stt("dve", sq[b], xT[:, c, gsl(tg)], gcol[:, c:c + 1], rs, ALU.mult, ALU.mult, [XT[tg], RS, CST], [SQ[b]])
                    P.act(hh[:, c, :], sq[b], AF.Copy, [SQ[b]], [HH[c]])
                    P.tt("pool", hl[:, c, :], sq[b], hh[:, c, :], ALU.subtract, [SQ[b], HH[c]], [HL[c]])

            tiles = []
            for tg in range(NTG):
                tiles += [("gu", tg, f) for f in range(NF)] + [("dn", tg, dc) for dc in range(DC)]

            def prepare(i):
                kind, tg, j = tiles[i]
                n = 2048 if kind == "gu" else 2816
                src = wgu_d[(l * 2 + k) * NF + j] if kind == "gu" else wd_d[(l * 2 + k) * 8 + j]
                w, WB = W.next(src, 128, n)
                b = i % 2
                P.act(wh[b][:, 0:n], w[:, 0:n], AF.Copy, [WB], [WH[b]])
                P.tt("dve", wl[b][:, 0:n], w[:, 0:n], wh[b][:, 0:n], ALU.subtract, [WB, WH[b]], [WL[b]])

            norm_split(0)
            prepare(0)
            for i, (kind, tg, j) in enumerate(tiles):
                if i + 1 < len(tiles):
                    prepare(i + 1)
                b = i % 2
                if kind == "gu":
                    f = j
                    whv = wh[b][:, 0:2048].rearrange("p (a c j) -> p a c j", a=2, c=DC)
                    wlv = wl[b][:, 0:2048].rearrange("p (a c j) -> p a c j", a=2, c=DC)
                    pg, pu = f % 2, 2 + f % 2
                    for a_, pb in ((0, pg), (1, pu)):
                        for c in range(DC):
                            P.mm(ps[pb][:, :], whv[:, a_, c, :], hh[:, c, :], c == 0, False, [WH[b], HH[c]], [PS[pb]], signal=False)
                            P.mm(ps[pb][:, :], whv[:, a_, c, :], hl[:, c, :], False, False, [WH[b], HL[c]], [PS[pb]], signal=False)
                            P.mm(ps[pb][:, :], wlv[:, a_, c, :], hh[:, c, :], False, c == DC - 1, [WL[b], HH[c]], [PS[pb]])
                    P.act(sg[f % 2], ps[pg][:, :], AF.Silu, [PS[pg]], [SG[f % 2]])
                    P.tt("dve", sg[f % 2], sg[f % 2], ps[pu][:, :], ALU.mult, [SG[f % 2], PS[pu]], [SG[f % 2]])
                    P.act(ah[:, f, :], sg[f % 2], AF.Copy, [SG[f % 2]], [AH[f]])
                    P.tt("pool", al[:, f, :], sg[f % 2], ah[:, f, :], ALU.subtract, [SG[f % 2], AH[f]], [AL[f]])
                else:
                    dc = j
                    if dc == 0 and tg + 1 < NTG:
                        norm_split(tg + 1)
                    whv = wh[b][:, 0:2816].rearrange("p (f j) -> p f j", f=NF)
                    wlv = wl[b][:, 0:2816].rearrange("p (f j) -> p f j", f=NF)
                    py = 4 + dc % 2
                    for f in range(NF):
                        P.mm(ps[py][:, :], whv[:, f, :], ah[:, f, :], f == 0, False, [WH[b], AH[f]], [PS[py]], signal=False)
                        P.mm(ps[py][:, :], whv[:, f, :], al[:, f, :], False, False, [WH[b], AL[f]], [PS[py]], signal=False)
                        P.mm(ps[py][:, :], wlv[:, f, :], ah[:, f, :], False, f == NF - 1, [WL[b], AH[f]], [PS[py]])
                    P.stt("dve", xT[:, dc, gsl(tg)], ps[py][:, :], 0.5, xT[:, dc, gsl(tg)], ALU.mult, ALU.add, [PS[py], XT[tg]], [XT[tg]])

        def stage_m1(l):
            A = Arena()
            uT = [A.take(DC * TG).rearrange("p (c t) -> p c t", c=DC) for _ in range(2)]
            UT = [P.buf("uT%d" % i) for i in range(2)]
            sq = [A.take(TG) for _ in range(2)]
            SQ = [P.buf("sq%d" % i) for i in range(2)]
            rs = A.take(TG)
            RS = P.buf("rs")
            stF = [A.take(TG) for _ in range(3)]
            STF = [P.buf("stF%d" % i, dma=True) for i in range(3)]
            stV = [A.take(TG) for _ in range(2)]
            STV = [P.buf("stV%d" % i, dma=True) for i in range(2)]
            gcol = cvec[:, l * NCV + 8:l * NCV + 16]
            norm_group(0, gcol, uT[0], UT[0], sq, SQ, rs, RS, 6)
            nF = 0
            nV = 0
            for tg in range(NTG):
                u, U = uT[tg % 2], UT[tg % 2]
                for g in range(8):
                    w, WB = W.next(winf_d[l * 22 + g], 128, 1024)
                    wv = w[:, 0:1024].rearrange("p (c j) -> p c j", c=DC)
                    pa = g % 2
                    for c in range(DC):
                        P.mm(ps[pa][:, :], wv[:, c, :], u[:, c, :], c == 0, c == DC - 1, [WB, U], [PS[pa]])
                    P.copy("act" if g % 2 else "dve", stF[nF % 3], ps[pa][:, :], [PS[pa]], [STF[nF % 3]])
                    P.dma("pool", a_s[tsl(g), gsl(tg)], stF[nF % 3], STF[nF % 3].sem, reads=[STF[nF % 3]])
                    nF += 1
                for j in range(14):
                    w, WB = W.next(winf_d[l * 22 + 8 + j], 128, 1024)
                    wv = w[:, 0:1024].rearrange("p (c j) -> p c j", c=DC)
                    pa = j % 2
                    for c in range(DC):
                        P.mm(ps[pa][:, :], wv[:, c, :], u[:, c, :], c == 0, c == DC - 1, [WB, U], [PS[pa]])
                    P.copy("act" if j % 2 else "dve", stF[nF % 3], ps[pa][:, :], [PS[pa]], [STF[nF % 3]])
                    P.dma("pool", qk_s[tsl(j), gsl(tg)], stF[nF % 3], STF[nF % 3].sem, reads=[STF[nF % 3]])
                    nF += 1
                for (src, n, dst, c0) in ((winvb_d[l], 384, vb_s, 0), (winvc_d[l * 2], 256, vc_s, 0), (winvc_d[l * 2 + 1], 256, vc_s, 256)):
                    w, WB = W.next(src, 128, DC * n)
                    wv = w[:, 0:DC * n].rearrange("p (c j) -> p c j", c=DC)
                    for t4 in range(4):
                        pv = 4 + nV % 2
                        for c in range(DC):
                            P.mm(ps[pv][:, 0:n], u[:, c, tsl(t4)], wv[:, c, :], c == 0, c == DC - 1, [WB, U], [PS[pv]])
                        P.copy("act" if nV % 2 else "dve", stV[nV % 2][:, 0:n], ps[pv][:, 0:n], [PS[pv]], [STV[nV % 2]])
                        P.dma("pool", dst[tsl(tg * 4 + t4), c0:c0 + n], stV[nV % 2][:, 0:n], STV[nV % 2].sem, reads=[STV[nV % 2]])
                        nV += 1
                if tg + 1 < NTG:
                    norm_group(tg + 1, gcol, uT[(tg + 1) % 2], UT[(tg + 1) % 2], sq, SQ, rs, RS, 6)

        def stage_m2(l):
            A = Arena()
            H = S // 2
            aT = [A.take(S) for _ in range(2)]
            AT = [P.buf("aTg%d" % i, dma=True) for i in range(2)]
            apf = A.take(H)
            amf = A.take(H)
            FO = P.buf("fold")
            nyb = [A.take(TG) for _ in range(2)]
            NYB = [P.buf("ny%d" % i) for i in range(2)]
            Tf = [A.take(8 * 256).rearrange("p (s j) -> p s j", s=8) for _ in range(2)]
            TF = [P.buf("Tf%d" % i) for i in range(2)]
            osb = [A.take(TG) for _ in range(2)]
            OSB = [P.buf("osb%d" % i) for i in range(2)]
            Y = [A.take(S) for _ in range(2)]
            YB = [P.buf("Y%d" % i, dma=True) for i in range(2)]
            fc = A.take(520)
            FC = P.buf("fc", dma=True)
            sgnn = fc[:, 0:512]
            cnyq = fc[:, 512:520]
            P.dma("sp", fc, fconst_d[:, :], FC.sem, writes=[FC])
            P.emit("pool", lambda e: e.memset(amf[:, 0:1], 0.0), writes=[FO])

            def load(g):
                P.dma("sp", aT[g % 2], a_s[tsl(g), :], AT[g % 2].sem, writes=[AT[g % 2]])

            def fold(g):
                a, AB = aT[g % 2], AT[g % 2]
                P.tt("pool", apf[:, 1:H], a[:, 1:H], a[:, S - 1:H:-1], ALU.add, [AB], [FO])
                P.copy("pool", apf[:, 0:1], a[:, 0:1], [AB], [FO])
                P.tt("dve", amf[:, 1:H], a[:, 1:H], a[:, S - 1:H:-1], ALU.subtract, [AB], [FO])
                ny, NY = nyb[g % 2], NYB[g % 2]
                P.emit("dve", lambda e: e.tensor_scalar(ny, sgnn, a[:, H:H + 1], None, op0=ALU.mult), [AB, FC], [NY])

            def chdft(g):
                t, TB_ = Tf[g % 2], TF[g % 2]
                for st_ in range(8):
                    pc = st_ % 2
                    P.mm(ps[pc][:, 0:128], apf[:, tsl(st_)], cs128[:, 0:128], True, True, [FO, CST], [PS[pc]], signal=False)
                    P.mm(ps[pc][:, 128:256], amf[:, tsl(st_)], cs128[:, 128:256], True, True, [FO, CST], [PS[pc]], signal=True)
                    P.copy("act" if st_ % 2 else "dve", t[:, st_, :], ps[pc][:, 0:256], [PS[pc]], [TB_])

            def seqdft(g):
                t, TB_ = Tf[g % 2], TF[g % 2]
                ny, NY = nyb[g % 2], NYB[g % 2]
                for sg_ in range(2):
                    for mat in range(2):
                        bank = 2 + sg_ if mat == 0 else 4 + sg_
                        for piece in range(2):
                            w, WB = W.next(dftf_d[(sg_ * 2 + mat) * 2 + piece], 128, 2048)
                            wv = w[:, 0:2048].rearrange("p (i j) -> p i j", i=4)
                            for i in range(4):
                                s_t = piece * 4 + i
                                P.mm(ps[bank][:, :], t[:, s_t, mat * 128:(mat + 1) * 128], wv[:, i, :],
                                     piece == 0 and i == 0, mat == 1 and piece == 1 and i == 3,
                                     [WB, TB_], [PS[bank]], signal=(i == 3))
                    P.mm(ps[2 + sg_][:, :], cs128[:, 0:128], ny, False, True, [CST, NY], [PS[2 + sg_]])
                for s_t in range(8):
                    P.mm(ps[6][:, 0:1], t[:, s_t, 0:128], cnyq[:, s_t:s_t + 1], s_t == 0, False, [TB_, FC], [PS[6]], signal=False)
                P.mm(ps[6][:, 0:1], cs128[:, 0:128], ny[:, 0:1], False, True, [CST, NY], [PS[6]])

            def assemble(g):
                y, YB_ = Y[g % 2], YB[g % 2]
                for sg_ in range(2):
                    e_ps, o_ps = ps[2 + sg_], ps[4 + sg_]
                    P.copy("act", osb[sg_], o_ps[:, :], [PS[4 + sg_]], [OSB[sg_]])
                    P.tt("dve", y[:, sg_ * TG:(sg_ + 1) * TG], e_ps[:, :], osb[sg_], ALU.subtract, [PS[2 + sg_], OSB[sg_]], [YB_])
                    if sg_ == 0:
                        P.tt("dve", y[:, S - 1:S - TG:-1], e_ps[:, 1:TG], osb[0][:, 1:TG], ALU.add, [PS[2], OSB[0]], [YB_])
                    else:
                        P.tt("dve", y[:, S - TG:H:-1], e_ps[:, :], osb[1], ALU.add, [PS[3], OSB[1]], [YB_])
                P.copy("act", y[:, H:H + 1], ps[6][:, 0:1], [PS[6]], [YB_])
                P.dma("pool", z_s[tsl(g), :], y, YB_.sem, reads=[YB_])

            load(0)
            load(1)
            fold(0)
            chdft(0)
            for g in range(8):
                if g + 1 < 8:
                    fold(g + 1)
                seqdft(g)
                if g + 1 < 8:
                    chdft(g + 1)
                if g + 2 < 8:
                    load(g + 2)
                assemble(g)

        def stage_m3(l):
            A = Arena()
            qb = A.take(S)
            kb = A.take(S)
            QB = P.buf("qb", dma=True)
            KB = P.buf("kb", dma=True)
            vg = [A.take(NT * 128).rearrange("p (s j) -> p s j", s=NT) for _ in range(2)]
            VG = [P.buf("vg%d" % i, dma=True) for i in range(2)]
            emb = A.take(6 * 384).rearrange("p (a b) -> p a b", a=6)
            EMB = P.buf("emb", dma=True)
            acc = A.take(2 * 2 * S, rows=64).rearrange("p (h k t) -> p h k t", h=2, k=2)
            ACC = [P.buf("acc%d" % i, dma=True) for i in range(2)]
            E = [A.take(384).rearrange("p (a b) -> p a b", a=3) for _ in range(4)]
            EB = [P.buf("E%d" % i) for i in range(4)]
            P.dma("sp", emb, bass.AP(tensor=bvb_s.tensor, offset=0, ap=[[1, 128], [512, 6], [1, 384]]), EMB.sem, writes=[EMB])
            P.act(emb, emb, AF.Exp, [EMB], [EMB])

            def geom(g, i):
                d = DIL[g]
                tpc = NT // d
                tb = i % tpc
                dl = [dd for dd in (-1, 0, 1) if 0 <= tb + dd < tpc]

                def perm(j):
                    r, t_ = j // tpc, j % tpc
                    o = t_ * 128 * d + r
                    return slice(o, o + 127 * d + 1, d)
                return dl, perm

            def load_g(g):
                d = DIL[g]
                tpc = NT // d
                P.dma("sp", qb, qk_s[tsl(g), :], QB.sem, writes=[QB])
                P.dma("sp", kb, qk_s[tsl(3 + g), :], KB.sem, writes=[KB])
                v, VB_ = vg[g % 2], VG[g % 2]
                for r in range(d):
                    for t0 in range(0, tpc, 4):
                        nt_ = min(4, tpc - t0)
                        src = bass.AP(tensor=vb_s.tensor, offset=r * 384 + g * 128 + t0 * 128 * d * 384,
                                      ap=[[d * 384, 128], [128 * d * 384, nt_], [1, 128]])
                        P.dma("sp", v[:, r * tpc + t0:r * tpc + t0 + nt_, :], src, VB_.sem, writes=[VB_])

            upairs = [(g, i) for g in range(3) for i in range(NT)]

            def emit_S(pi):
                g, i = upairs[pi]
                dl, perm = geom(g, i)
                a0, a1 = dl[0] + 1, dl[-1] + 2
                for dd in dl:
                    for hh in range(2):
                        n = 2 * pi + hh
                        hs = slice(hh * 64, (hh + 1) * 64)
                        pss = ps[n % 4][:, 0:384].rearrange("p (a b) -> p a b", a=3)
                        P.mm(pss[:, dd + 1, :], kb[hs, perm(i + dd)], qb[hs, perm(i)], True, True, [KB, QB], [PS[n % 4]], signal=(dd == dl[-1]))
                for hh in range(2):
                    n = 2 * pi + hh
                    pss = ps[n % 4][:, 0:384].rearrange("p (a b) -> p a b", a=3)
                    mview = emb[:, 2 * g + hh, :].rearrange("p (a b) -> p a b", a=3)[:, :, ::-1]
                    e_, EB_ = E[n % 4], EB[n % 4]
                    P.act(e_[:, a0:a1, :], pss[:, a0:a1, :], AF.Exp, [PS[n % 4]], [EB_], scale=0.125)
                    P.tt("dve", e_[:, a0:a1, :], e_[:, a0:a1, :], mview[:, a0:a1, :], ALU.mult, [EB_, EMB], [EB_])

            def emit_PV(pi):
                g, i = upairs[pi]
                dl, perm = geom(g, i)
                v, VB_ = vg[g % 2], VG[g % 2]
                for hh in range(2):
                    n = 2 * pi + hh
                    hs = slice(hh * 64, (hh + 1) * 64)
                    e_, EB_ = E[n % 4], EB[n % 4]
                    pU = 4 + n % 4
                    psu = ps[pU][0:64, 0:256].rearrange("p (a b) -> p a b", a=2)
                    for dd in dl:
                        P.mm(psu[:, 0, :], v[:, i + dd, hs], e_[:, dd + 1, :], dd == dl[0], dd == dl[-1], [VB_, EB_], [PS[pU]], signal=False)
                    for dd in dl:
                        P.mm(psu[:, 1, :], ones1[:, 0:64], e_[:, dd + 1, :], dd == dl[0], dd == dl[-1], [CST, EB_], [PS[pU]])
                    dst = acc[:, hh, :, perm(i)]
                    if g == 0:
                        P.copy("dve", dst, psu, [PS[pU]], [ACC[hh]])
                    else:
                        P.tt("dve", dst, psu, dst, ALU.add, [PS[pU], ACC[hh]], [ACC[hh]])

            NPB = len(upairs)
            load_g(0)
            emit_S(0)
            for pi in range(NPB):
                g, i = upairs[pi]
                if pi + 1 < NPB:
                    if upairs[pi + 1][0] != g:
                        load_g(g + 1)
                    emit_S(pi + 1)
                emit_PV(pi)
            for hh in range(2):
                P.emit("dve", lambda e, hh=hh: e.reciprocal(acc[:, hh, 1, :], acc[:, hh, 1, :]), [ACC[hh]], [ACC[hh]])
                P.tt("dve", acc[:, hh, 0, :], acc[:, hh, 0, :], acc[:, hh, 1, :], ALU.mult, [ACC[hh]], [ACC[hh]])
                P.dma("pool", z_s[8 * 128 + hh * 64:8 * 128 + (hh + 1) * 64, :], acc[:, hh, 0, :], ACC[hh].sem, reads=[ACC[hh]])

        def stage_m4(l):
            A = Arena()
            qh = [A.take(S) for _ in range(2)]
            kh = [A.take(S) for _ in range(2)]
            vh = [A.take(NT * 128).rearrange("p (s j) -> p s j", s=NT) for _ in range(2)]
            strip = [A.take(1408) for _ in range(2)]
            QH = [P.buf("qh%d" % i, dma=True) for i in range(2)]
            KH = [P.buf("kh%d" % i, dma=True) for i in range(2)]
            VH = [P.buf("vh%d" % i, dma=True) for i in range(2)]
            SB_ = [P.buf("strip%d" % i, dma=True) for i in range(2)]
            E = [A.take(TG) for _ in range(4)]
            EB = [P.buf("E%d" % i) for i in range(4)]
            dacc = [[A.take(TG) for _ in range(2)] for _ in range(2)]
            DACC = [[P.buf("dacc%d%d" % (i, j)) for j in range(2)] for i in range(2)]
            r0 = A.take(TG)
            r1 = A.take(TG)
            av = A.take(TG)
            TMP = P.buf("tmp")
            TMP2 = P.buf("tmp2")
            zst = [A.take(TG) for _ in range(2)]
            ZST = [P.buf("zst%d" % i, dma=True) for i in range(2)]
            neglam = lcol[:, 2 * l:2 * l + 1]
            gsub = lcol[:, 2 * l + 1:2 * l + 2]

            def load_head(h):
                b = h % 2
                P.dma("sp", qh[b], qk_s[tsl(6 + h), :], QH[b].sem, writes=[QH[b]])
                P.dma("sp", kh[b], qk_s[tsl(10 + h), :], KH[b].sem, writes=[KH[b]])
                vsrc = vc_s[:, h * 128:(h + 1) * 128].rearrange("(s p) j -> p s j", p=128)
                for q4 in range(4):
                    P.dma("sp", vh[b][:, q4 * 4:(q4 + 1) * 4, :], vsrc[:, q4 * 4:(q4 + 1) * 4, :], VH[b].sem, writes=[VH[b]])
                P.dma("sp", strip[b], bass.AP(tensor=bvc_s.tensor, offset=h * 1536, ap=[[1, 128], [1, 1408]]), SB_[b].sem, writes=[SB_[b]])
                P.act(strip[b], strip[b], AF.Exp, [SB_[b]], [SB_[b]])

            pairs = [(h, qg, kt) for h in range(4) for qg in range(NTG) for kt in range(NT)]
            state = {"s": 0}

            def emit_S(p):
                h, qg, kt = pairs[p]
                b = h % 2
                for m in range(2):
                    n = 2 * p + m
                    ms = slice(m * 64, (m + 1) * 64)
                    P.mm(ps[n % 4][:, :], kh[b][ms, tsl(kt)], qh[b][ms, qg * TG:(qg + 1) * TG], True, True, [KH[b], QH[b]], [PS[n % 4]])

            def emit_exp(n):
                h, qg, kt = pairs[n // 2]
                m = n % 2
                b = h % 2
                Q0 = qg * TG
                pS = n % 4
                e_, EB_ = E[n % 4], EB[n % 4]
                bpos = relbc[:, 31 * 10 + 6 + h:31 * 10 + 7 + h]
                bneg = relbc[:, 15 * 10 + 6 + h:15 * 10 + 7 + h]
                qa = min(max(128 * kt - 640, Q0), Q0 + TG)
                qe = min(max(128 * kt + 768, Q0), Q0 + TG)
                if qa > Q0:
                    P.act(e_[:, 0:qa - Q0], ps[pS][:, 0:qa - Q0], AF.Exp, [PS[pS], CST], [EB_], bias=bpos, scale=0.125)
                if qe > qa:
                    P.act(e_[:, qa - Q0:qe - Q0], ps[pS][:, qa - Q0:qe - Q0], AF.Exp, [PS[pS]], [EB_], scale=0.125)
                    clo = 128 * kt - qe + 768
                    chi = 128 * kt - qa + 767
                    P.tt("dve", e_[:, qa - Q0:qe - Q0], e_[:, qa - Q0:qe - Q0], strip[b][:, clo:chi + 1][:, ::-1], ALU.mult,
                         [EB_, SB_[b]], [EB_])
                if qe < Q0 + TG:
                    P.act(e_[:, qe - Q0:TG], ps[pS][:, qe - Q0:TG], AF.Exp, [PS[pS], CST], [EB_], bias=bneg, scale=0.125)

            def emit_PV(n):
                h, qg, kt = pairs[n // 2]
                m = n % 2
                b = h % 2
                par = (h * NTG + qg) % 2
                e_, EB_ = E[n % 4], EB[n % 4]
                P.mm(ps[4 + m][:, :], vh[b][:, kt, :], e_, kt == 0, kt == NT - 1, [VH[b], EB_], [PS[4 + m]], signal=True)
                eng = "pool" if m == 0 else "dve"
                if kt == 0:
                    P.copy(eng, dacc[par][m], e_, [EB_], [DACC[par][m]])
                else:
                    P.tt(eng, dacc[par][m], dacc[par][m], e_, ALU.add, [EB_, DACC[par][m]], [DACC[par][m]])

            def epilogue_a(h, qg):
                par = (h * NTG + qg) % 2
                P.mm(ps[6][:, :], ones1, dacc[par][0], True, True, [CST, DACC[par][0]], [PS[6]])
                P.mm(ps[7][:, :], ones1, dacc[par][1], True, True, [CST, DACC[par][1]], [PS[7]])
                P.emit("dve", lambda e: e.reciprocal(r0, ps[6][:, :]), [PS[6]], [TMP])
                P.emit("dve", lambda e: e.reciprocal(r1, ps[7][:, :]), [PS[7]], [TMP])
                P.tt("dve", r0, ps[4][:, :], r0, ALU.mult, [PS[4], TMP], [TMP])
                P.tt("dve", r1, ps[5][:, :], r1, ALU.mult, [PS[5], TMP], [TMP])
                P.stt("dve", av, r1, neglam, r0, ALU.mult, ALU.add, [TMP, CST], [TMP2])

            def epilogue_b(h, qg):
                P.tt("pool", r1, av, av, ALU.mult, [TMP2, TMP], [TMP])
                P.mm(ps[6][:, :], ones_sub, r1, True, True, [CST, TMP], [PS[6]])
                P.act(r0, ps[6][:, :], AF.Sqrt, [PS[6], CST, TMP], [TMP], bias=eps5)
                P.emit("dve", lambda e: e.reciprocal(r0, r0), [TMP], [TMP])
                k = state["s"]
                z, Z = zst[k % 2], ZST[k % 2]
                P.stt("dve", z, av, gsub, r0, ALU.mult, ALU.mult, [TMP, TMP2, CST], [Z])
                P.dma("pool", z_s[tsl(9 + h), qg * TG:(qg + 1) * TG], z, Z.sem, reads=[Z])
                state["s"] += 1

            NP = len(pairs)
            load_head(0)
            emit_S(0)
            emit_exp(0)
            emit_exp(1)
            pending = None
            for p in range(NP):
                h, qg, kt = pairs[p]
                if kt == 0 and qg == 0 and h + 1 < 4:
                    load_head(h + 1)
                if p + 1 < NP:
                    emit_S(p + 1)
                    emit_exp(2 * p + 2)
                    emit_exp(2 * p + 3)
                emit_PV(2 * p)
                emit_PV(2 * p + 1)
                if pending is not None and kt == 2:
                    epilogue_b(*pending)
                    pending = None
                if kt == NT - 1:
                    epilogue_a(h, qg)
                    pending = (h, qg)
            epilogue_b(*pending)

        def stage_m5(l):
            A = Arena()
            uT = [A.take(DC * TG).rearrange("p (c t) -> p c t", c=DC) for _ in range(1)]
            UT = [P.buf("uT0")]
            sq = [A.take(TG) for _ in range(2)]
            SQ = [P.buf("sq%d" % i) for i in range(2)]
            rs = A.take(TG)
            RS = P.buf("rs")
            zsl = A.take(13 * TG).rearrange("p (k t) -> p k t", k=13)
            ZS = P.buf("zsl", dma=True)
            mg = A.take(DC * TG).rearrange("p (c t) -> p c t", c=DC)
            MG = [P.buf("mg%d" % i) for i in range(DC)]
            sig = [A.take(TG) for _ in range(2)]
            SIG = [P.buf("sig%d" % i) for i in range(2)]
            tmp = [A.take(TG) for _ in range(2)]
            TMPB = [P.buf("tmp%d" % i) for i in range(2)]
            gcol = cvec[:, l * NCV + 8:l * NCV + 16]
            brs = ((wm5a_d, 8, 0), (wm5b_d, 1, 8), (wm5c_d, 4, 9))
            n = 0
            for tg in range(NTG):
                u, U = uT[0], UT[0]
                norm_group(tg, gcol, u, U, sq, SQ, rs, RS, 6)
                zsrc = z_s[:, gsl(tg)].rearrange("(k p) t -> p k t", p=128)
                for (k0, k1) in ((0, 4), (4, 8), (8, 13)):
                    P.dma("sp", zsl[:, k0:k1, :], zsrc[:, k0:k1, :], ZS.sem, writes=[ZS])
                for dc in range(DC):
                    for br, (wsrc, nk, koff) in enumerate(brs):
                        w, WB = W.next(wsrc[l * 8 + dc], 128, (8 + nk) * 128)
                        wv = w[:, 0:(8 + nk) * 128].rearrange("p (c j) -> p c j", c=8 + nk)
                        pg, py = n % 2, 2 + n % 2
                        for c in range(DC):
                            P.mm(ps[pg][:, :], wv[:, c, :], u[:, c, :], c == 0, c == DC - 1, [WB, U], [PS[pg]])
                        bcol = cvec[:, l * NCV + 24 + br * 8 + dc:l * NCV + 25 + br * 8 + dc]
                        P.act(sig[n % 2], ps[pg][:, :], AF.Sigmoid, [PS[pg], CST], [SIG[n % 2]], bias=bcol)
                        for kk in range(nk):
                            P.mm(ps[py][:, :], wv[:, 8 + kk, :], zsl[:, koff + kk, :], kk == 0, kk == nk - 1, [WB, ZS], [PS[py]])
                        if br == 0:
                            P.tt("dve", mg[:, dc, :], sig[n % 2], ps[py][:, :], ALU.mult, [SIG[n % 2], PS[py]], [MG[dc]])
                        else:
                            P.tt("dve", tmp[n % 2], sig[n % 2], ps[py][:, :], ALU.mult, [SIG[n % 2], PS[py]], [TMPB[n % 2]])
                            P.tt("pool", mg[:, dc, :], mg[:, dc, :], tmp[n % 2], ALU.add, [TMPB[n % 2], MG[dc]], [MG[dc]])
                        n += 1
                for d2 in range(DC):
                    w, WB = W.next(wout_d[l * 8 + d2], 128, 1024)
                    wv = w[:, 0:1024].rearrange("p (c j) -> p c j", c=DC)
                    po = 4 + d2 % 2
                    for c in range(DC):
                        P.mm(ps[po][:, :], wv[:, c, :], mg[:, c, :], c == 0, c == DC - 1, [WB, MG[c]], [PS[po]])
                    P.tt("dve", xT[:, d2, gsl(tg)], ps[po][:, :], xT[:, d2, gsl(tg)], ALU.add, [PS[po], XT[tg]], [XT[tg]])

        def stage_out(raw=False):
            A = Arena()
            xo = [A.take(1024) for _ in range(2)]
            XO = [P.buf("xo%d" % i) for i in range(2)]
            sqo = A.take(1024)
            SQO = P.buf("sqo")
            ss = A.take(8)
            yo = [A.take(1024) for _ in range(2)]
            YO = [P.buf("yo%d" % i, dma=True) for i in range(2)]
            gf = A.take(1024)
            GF = P.buf("gf", dma=True)
            P.dma("sp", gf, bass.AP(tensor=gfin_d.tensor, offset=0, ap=[[0, 128], [1, 1024]]), GF.sem, writes=[GF])
            for tt in range(NT):
                b = tt % 2
                for cq in range(2):
                    pb = (tt * 2 + cq) % 4
                    for k in range(4):
                        c = cq * 4 + k
                        P.tr(ps[pb][:, k * 128:(k + 1) * 128], xT[:, c, tsl(tt)], ident, [XT[tt // 4], CST], [PS[pb]], signal=(k == 3))
                    P.copy("dve" if cq == 0 else "act", xo[b][:, cq * 512:(cq + 1) * 512], ps[pb][:, :], [PS[pb]], [XO[b]])
                if raw:
                    P.copy("dve", yo[b], xo[b], [XO[b]], [YO[b]])
                else:
                    P.tt("pool", sqo, xo[b], xo[b], ALU.mult, [XO[b]], [SQO])
                    P.emit("dve", lambda e: e.reduce_sum(out=ss[:, 0:1], in_=sqo, axis=AX.X), [SQO], [SQO])
                    P.emit("dve", lambda e: e.tensor_scalar(ss[:, 0:1], ss[:, 0:1], 1.0 / D, EPS, op0=ALU.mult, op1=ALU.add), [SQO], [SQO])
                    P.act(ss[:, 0:1], ss[:, 0:1], AF.Sqrt, [SQO], [SQO])
                    P.emit("dve", lambda e: e.reciprocal(ss[:, 0:1], ss[:, 0:1]), [SQO], [SQO])
                    P.stt("dve", yo[b], xo[b], ss[:, 0:1], gf, ALU.mult, ALU.mult, [XO[b], SQO, GF], [YO[b]])
                P.dma("pool", out_d[tsl(tt), :], yo[b], YO[b].sem, reads=[YO[b]])

        def run_all():
            for sname in stages:
                parts = sname.split(":")
                if parts[0] == "setup":
                    stage_setup()
                elif parts[0] == "in":
                    stage_in()
                elif parts[0] == "ffn":
                    stage_ffn(int(parts[1]), int(parts[2]))
                elif parts[0] in ("m1", "m2", "m3", "m4", "m5"):
                    {"m1": stage_m1, "m2": stage_m2, "m3": stage_m3, "m4": stage_m4, "m5": stage_m5}[parts[0]](int(parts[1]))
                elif parts[0] == "out":
                    stage_out(False)
                elif parts[0] == "outraw":
                    stage_out(True)
                else:
                    raise ValueError(sname)
                P.barrier()

        P.dry = True
        run_all()
        P.dry = False
        W.reset()
        run_all()
        P.barrier()
        P.finalize(st)
    return nc, P


def _lhsT_tiles(Wm):
    K, N = Wm.shape
    return Wm.reshape(K // 128, 128, N // 128, 128).transpose(2, 1, 0, 3)


def _rel_bucket(rel):
    rel = np.asarray(rel, dtype=np.int64)
    ret = np.where(rel > 0, 16, 0)
    n = np.abs(rel)
    nf = np.maximum(n, 1).astype(np.float32)
    large = 8 + (np.log(nf / np.float32(8)) / np.float32(math.log(1024 / 8)) * np.float32(8)).astype(np.int32)
    large = np.minimum(large, 15)
    return ret + np.where(n < 8, n, large)


_CONST_CACHE = {}


def _constants():
    if _CONST_CACHE:
        return _CONST_CACHE
    Hh = S // 2
    s = np.arange(Hh, dtype=np.int64)
    ang = 2.0 * np.pi * ((s[:, None] * s[None, :]) % S).astype(np.float64) / S
    norm = 1.0 / math.sqrt(S * 128.0)
    mats = [np.cos(ang) * norm, np.sin(ang) * norm]
    dftf = np.stack([m.reshape(2, 4, 128, 2, 512).transpose(3, 0, 2, 1, 4) for m in mats], axis=1)
    _CONST_CACHE["dftf"] = np.ascontiguousarray(dftf.reshape(8, 128, 2048), dtype=np.float32)
    fconst = np.zeros((128, 520), np.float64)
    fconst[:, 0:512] = (((-1.0) ** np.arange(512)) * norm)[None, :]
    fconst[:, 512:520] = (((-1.0) ** np.arange(128)) * norm)[:, None]
    _CONST_CACHE["fconst"] = np.ascontiguousarray(fconst, dtype=np.float32)
    c = np.arange(128, dtype=np.int64)
    a2 = 2.0 * np.pi * ((c[:, None] * c[None, :]) % 128).astype(np.float64) / 128
    _CONST_CACHE["cs128"] = np.ascontiguousarray(np.concatenate([np.cos(a2), np.sin(a2)], axis=1), dtype=np.float32)
    _CONST_CACHE["ident"] = np.eye(128, dtype=np.float32)
    ohc = np.zeros((33, 1536), np.float32)
    i = np.arange(1535)
    ohc[_rel_bucket(i - 767), i] = 1.0
    _CONST_CACHE["ohc"] = ohc
    ohb = np.zeros((33, 3, 512), np.float32)
    i = np.arange(511)
    for g, d in enumerate(DIL):
        mrel = i - 255
        bk = np.where(np.abs(mrel) <= 64, _rel_bucket(mrel * d), 32)
        ohb[bk, g, i] = 1.0
    _CONST_CACHE["ohb"] = np.ascontiguousarray(ohb.reshape(33, 1536))
    return _CONST_CACHE


def prep_inputs(inp):
    f = lambda a: np.ascontiguousarray(np.asarray(a), dtype=np.float32)
    g = {k: np.asarray(v) for k, v in inp.items()}
    o = dict(_constants())
    wgu, wd, winf, winvb, winvc, wm5a, wm5b, wm5c, wout = [], [], [], [], [], [], [], [], []
    fcols = np.concatenate([np.arange(0, 1792), np.arange(2176, 3200)])
    for l in range(L):
        for (wg_, wu_, wdn_) in ((g["w_ffn1_gate"], g["w_ffn1_up"], g["w_ffn1_down"]),
                                 (g["w_ffn2_gate"], g["w_ffn2_up"], g["w_ffn2_down"])):
            tg_ = _lhsT_tiles(wg_[l])
            tu_ = _lhsT_tiles(wu_[l])
            wgu.append(np.stack([tg_, tu_], axis=2).reshape(NF, 128, 2048))
            wd.append(_lhsT_tiles(wdn_[l]).reshape(8, 128, 2816))
        win = g["w_in"][l]
        winf.append(_lhsT_tiles(win[:, fcols]).reshape(22, 128, 1024))
        winvb.append(win[:, 1792:2176].reshape(8, 128, 384).transpose(1, 0, 2).reshape(1, 128, 3072))
        for hf in range(2):
            winvc.append(win[:, 3200 + hf * 256:3200 + (hf + 1) * 256].reshape(8, 128, 256).transpose(1, 0, 2).reshape(1, 128, 2048))
        for (lst, wb, br) in ((wm5a, g["w_br_a"][l], 0), (wm5b, g["w_br_b"][l], 1), (wm5c, g["w_br_c"][l], 2)):
            gt = _lhsT_tiles(g["w_gate"][l][:, br * 1024:(br + 1) * 1024])
            bt = _lhsT_tiles(wb)
            lst.append(np.concatenate([gt, bt], axis=2).reshape(8, 128, -1))
        wout.append(_lhsT_tiles(g["w_out"][l]).reshape(8, 128, 1024))
    o["wgu"] = f(np.concatenate(wgu, 0))
    o["wd"] = f(np.concatenate(wd, 0))
    o["winf"] = f(np.concatenate(winf, 0))
    o["winvb"] = f(np.concatenate(winvb, 0))
    o["winvc"] = f(np.concatenate(winvc, 0))
    o["wm5a"] = f(np.concatenate(wm5a, 0))
    o["wm5b"] = f(np.concatenate(wm5b, 0))
    o["wm5c"] = f(np.concatenate(wm5c, 0))
    o["wout"] = f(np.concatenate(wout, 0))
    cv = np.zeros((128, L * NCV), np.float32)
    for l in range(L):
        b0 = l * NCV
        cv[:, b0 + 0:b0 + 8] = g["g_ffn1"][l].reshape(8, 128).T
        cv[:, b0 + 8:b0 + 16] = g["g_mix"][l].reshape(8, 128).T
        cv[:, b0 + 16:b0 + 24] = g["g_ffn2"][l].reshape(8, 128).T
        cv[:, b0 + 24:b0 + 48] = g["b_gate"][l].reshape(24, 128).T
        cv[:, b0 + 48] = g["subln_g"][l]
    o["cvec"] = cv
    o["relb"] = f(g["rel_bias"])
    o["lamv"] = f(np.stack([g[k][l] for l in range(L) for k in ("lam_q1", "lam_k1", "lam_q2", "lam_k2")], 0))
    o["gfin"] = f(g["g_final"].reshape(1, 1024))
    return o


_NC_CACHE = {}


def kernel(**inputs):
    shared = prep_inputs(inputs)
    x = np.ascontiguousarray(np.asarray(inputs["x"]), dtype=np.float32)
    if "nc" not in _NC_CACHE:
        _NC_CACHE["nc"] = build()[0]
    nc = _NC_CACHE["nc"]
    in_maps = []
    for b in range(8):
        m = dict(shared)
        m["x"] = x[b]
        in_maps.append(m)
    res = run_bass_kernel_spmd(nc, in_maps, core_ids=list(range(8)))
    return np.stack([np.asarray(r["out"]) for r in res.results], axis=0).astype(np.float32)
```

```python
import math
from contextlib import ExitStack
import numpy as np
import concourse.bass as bass
import concourse.mybir as mybir
from concourse.bass_utils import run_bass_kernel_spmd

F32 = mybir.dt.float32
BF16 = mybir.dt.bfloat16
AF = mybir.ActivationFunctionType
ALU = mybir.AluOpType
AX = mybir.AxisListType

S = 2048
D = 1024
NT = 16
TG = 512
NTG = 4
DC = 8
FF = 2816
NF = 22
L = 2
EPS = 1e-6
SUBLN_EPS = 1e-5
NSLOT = 3
SLOT = 3072
ARENA = 25600
DIL = (1, 4, 16)
NEG = -30000.0
SAME_ENG = True
NCV = 49


class Sem:
    __slots__ = ("name", "h", "count", "barrier")

    def __init__(self, name, barrier=True):
        self.name = name
        self.h = None
        self.count = 0
        self.barrier = barrier


class Buf:
    __slots__ = ("name", "w", "r", "sem")

    def __init__(self, name, sem=None):
        self.name = name
        self.w = None
        self.r = {}
        self.sem = sem


class Q:
    def __init__(self, name, sem):
        self.name = name
        self.ops = []
        self.sem = sem
        self.waited = {}


class Prog:
    ENGS = ("pe", "act", "dve", "pool", "sp")

    def __init__(self, nc):
        self.nc = nc
        self.sems = []
        self.q = {n: Q(n, self.new_sem("e_" + n)) for n in self.ENGS}
        self.ninst = 0
        self.dry = False

    def new_sem(self, name, barrier=True):
        s = Sem("%s_%d" % (name, len(self.sems)), barrier)
        self.sems.append(s)
        return s

    def buf(self, name, dma=False, barrier=True):
        return Buf(name, self.new_sem("d_" + name, barrier) if dma else None)

    def emit(self, eng, fn, reads=(), writes=(), signal=True, dsem=None):
        if self.dry:
            return
        q = self.q[eng]
        deps = {}
        for b in reads:
            if b.w is not None:
                s, v = b.w
                if deps.get(s, 0) < v:
                    deps[s] = v
        for b in writes:
            if b.w is not None:
                s, v = b.w
                if deps.get(s, 0) < v:
                    deps[s] = v
            for s, v in b.r.items():
                if deps.get(s, 0) < v:
                    deps[s] = v
        for s, v in deps.items():
            if s is q.sem and (eng == "pe" or not SAME_ENG):
                continue
            if q.waited.get(s, 0) < v:
                q.ops.append(("w", s, v))
                q.waited[s] = v
        if dsem is not None:
            dsem.count += 16
            tag = (dsem, dsem.count)
            inc = (dsem, 16)
        elif signal:
            q.sem.count += 1
            tag = (q.sem, q.sem.count)
            inc = (q.sem, 1)
        else:
            tag = (q.sem, q.sem.count + 1)
            inc = None
        q.ops.append(("i", fn, inc))
        self.ninst += 1
        for b in reads:
            if b.r.get(tag[0], 0) < tag[1]:
                b.r[tag[0]] = tag[1]
        for b in writes:
            b.w = tag
            b.r = {}

    def barrier(self, engines=None):
        if self.dry:
            return
        for e in engines or self.ENGS:
            q = self.q[e]
            for s in self.sems:
                if s.count > 0 and s.barrier and s is not q.sem and q.waited.get(s, 0) < s.count:
                    q.ops.append(("w", s, s.count))
                    q.waited[s] = s.count

    def mm(self, out, lhsT, rhs, start, stop, reads, writes, signal=None):
        self.emit("pe", lambda e: e.matmul(out, lhsT, rhs, start=start, stop=stop),
                  reads, writes, signal=stop if signal is None else signal)

    def tr(self, out, in_, ident, reads, writes, signal=True):
        self.emit("pe", lambda e: e.transpose(out, in_, ident), reads, writes, signal=signal)

    def act(self, out, in_, func, reads, writes, bias=None, scale=None):
        kw = {}
        if bias is not None:
            kw["bias"] = bias
        if scale is not None:
            kw["scale"] = scale
        self.emit("act", lambda e: e.activation(out=out, in_=in_, func=func, **kw), reads, writes)

    def copy(self, eng, out, in_, reads, writes):
        if eng == "act":
            self.act(out, in_, AF.Copy, reads, writes)
        else:
            self.emit(eng, lambda e: e.tensor_copy(out, in_), reads, writes)

    def tt(self, eng, out, in0, in1, op, reads, writes):
        self.emit(eng, lambda e: e.tensor_tensor(out=out, in0=in0, in1=in1, op=op), reads, writes)

    def stt(self, eng, out, in0, scalar, in1, op0, op1, reads, writes):
        self.emit(eng, lambda e: e.scalar_tensor_tensor(out=out, in0=in0, scalar=scalar, in1=in1, op0=op0, op1=op1),
                  reads, writes)

    def dma(self, eng, out, in_, sem, reads=(), writes=()):
        self.emit(eng, lambda e: e.dma_start(out=out, in_=in_), reads, writes, dsem=sem)

    def finalize(self, stack):
        nc = self.nc
        for s in self.sems:
            if s.count > 0:
                s.h = stack.enter_context(nc.semaphore(s.name))
        block = stack.enter_context(nc.Block())
        decos = {"pe": block.tensor, "act": block.scalar, "dve": block.vector,
                 "pool": block.gpsimd, "sp": block.sync}
        for name in self.ENGS:
            q = self.q[name]

            def body(e, q=q):
                for op in q.ops:
                    if op[0] == "w":
                        e.wait_ge(op[1].h, op[2])
                    else:
                        ins = op[1](e)
                        if op[2] is not None:
                            ins.then_inc(op[2][0].h, op[2][1])

            decos[name](body)


class WStream:
    def __init__(self, P, slots):
        self.P = P
        self.slots = slots
        self.bufs = [P.buf("ws%d" % i, dma=True, barrier=False) for i in range(len(slots))]
        self.plan = []
        self.m = 0
        self.loaded = 0

    def reset(self):
        self.m = 0
        self.loaded = 0

    def next(self, src, rows, n):
        P = self.P
        if P.dry:
            self.plan.append((src, rows, n))
            i = (len(self.plan) - 1) % NSLOT
            return self.slots[i], self.bufs[i]
        m = self.m
        assert self.plan[m][1:] == (rows, n), (m, self.plan[m][1:], rows, n)
        hi = min(m + NSLOT - 1, len(self.plan) - 1)
        while self.loaded <= hi:
            k = self.loaded
            src_k, rows_k, n_k = self.plan[k]
            i = k % NSLOT
            P.dma("sp", self.slots[i][0:rows_k, 0:n_k], src_k, self.bufs[i].sem, writes=[self.bufs[i]])
            self.loaded += 1
        self.m += 1
        i = m % NSLOT
        return self.slots[i], self.bufs[i]


def build(stages=None, dbg=False):
    if stages is None:
        stages = ["setup", "in"]
        for l in range(L):
            stages += ["ffn:%d:0" % l, "m1:%d" % l, "m2:%d" % l, "m3:%d" % l, "m4:%d" % l, "m5:%d" % l, "ffn:%d:1" % l]
        stages += ["out"]
    nc = bass.Bass("TRN2", target_bir_lowering=False)
    skind = "ExternalOutput" if dbg else "Internal"

    def din(name, shape):
        return nc.dram_tensor(name, shape, F32, kind="ExternalInput").ap()

    x_d = din("x", [S, D])
    out_d = nc.dram_tensor("out", [S, D], F32, kind="ExternalOutput").ap()
    wgu_d = din("wgu", [L * 2 * NF, 128, 2048])
    wd_d = din("wd", [L * 2 * 8, 128, 2816])
    winf_d = din("winf", [L * 22, 128, 1024])
    winvb_d = din("winvb", [L, 128, 3072])
    winvc_d = din("winvc", [L * 2, 128, 2048])
    wm5a_d = din("wm5a", [L * 8, 128, 2048])
    wm5b_d = din("wm5b", [L * 8, 128, 1152])
    wm5c_d = din("wm5c", [L * 8, 128, 1536])
    wout_d = din("wout", [L * 8, 128, 1024])
    dftf_d = din("dftf", [8, 128, 2048])
    fconst_d = din("fconst", [128, 520])
    cvec_d = din("cvec", [128, L * NCV])
    cs128_d = din("cs128", [128, 256])
    ident_d = din("ident", [128, 128])
    relb_d = din("relb", [32, 10])
    ohc_d = din("ohc", [33, 1536])
    ohb_d = din("ohb", [33, 3 * 512])
    lamv_d = din("lamv", [L * 4, 64])
    gfin_d = din("gfin", [1, 1024])
    qk_s = nc.dram_tensor("qk_s", [14 * 128, S], F32, kind=skind).ap()
    vb_s = nc.dram_tensor("vb_s", [S, 384], F32, kind=skind).ap()
    vc_s = nc.dram_tensor("vc_s", [S, 512], F32, kind=skind).ap()
    a_s = nc.dram_tensor("a_s", [8 * 128, S], F32, kind=skind).ap()
    z_s = nc.dram_tensor("z_s", [13 * 128, S], F32, kind=skind).ap()
    bvc_s = nc.dram_tensor("bvc_s", [4, 1536], F32, kind=skind).ap()
    bvb_s = nc.dram_tensor("bvb_s", [6, 512], F32, kind=skind).ap()

    with ExitStack() as st:
        P = Prog(nc)
        sb = lambda n, s: st.enter_context(nc.sbuf_tensor(n + "_sb", s, F32))
        xT = sb("xT", [128, DC, S])
        cst = sb("cst", [128, 1024])
        cvec = sb("cvec", [128, L * NCV])
        relbc = sb("relbc", [128, 320])
        wsl = sb("wsl", [128, NSLOT * SLOT])
        arena = sb("arena", [128, ARENA])
        ps = [st.enter_context(nc.psum_tensor("ps%d" % i, [128, 512], F32)) for i in range(8)]
        PS = [P.buf("ps%d" % i) for i in range(8)]
        XT = [P.buf("xT%d" % i) for i in range(NTG)]
        CST = P.buf("cst", dma=True)
        W = WStream(P, [wsl[:, i * SLOT:(i + 1) * SLOT] for i in range(NSLOT)])

        ones_mean = cst[:, 0:128]
        ones_sub = cst[:, 128:256]
        ones1 = cst[:, 256:384]
        ident = cst[:, 384:512]
        cs128 = cst[:, 512:768]
        eps6 = cst[:, 768:769]
        eps5 = cst[:, 769:770]
        lcol = cst[:, 772:780]
        rb33 = cst[0:33, 784:794]

        def tsl(i):
            return slice(i * 128, (i + 1) * 128)

        def gsl(tg):
            return slice(tg * TG, (tg + 1) * TG)

        class Arena:
            def __init__(self):
                self.off = 0

            def take(self, n, rows=128):
                a = arena[0:rows, self.off:self.off + n]
                self.off += n
                assert self.off <= ARENA, self.off
                return a

        def stage_setup():
            A = Arena()
            P.emit("pool", lambda e: e.memset(cst[:, 0:128], 1.0 / D), writes=[CST])
            P.emit("pool", lambda e: e.memset(cst[:, 128:256], 1.0 / 128), writes=[CST])
            P.emit("pool", lambda e: e.memset(cst[:, 256:384], 1.0), writes=[CST])
            P.emit("pool", lambda e: e.memset(cst[:, 768:769], EPS), writes=[CST])
            P.emit("pool", lambda e: e.memset(cst[:, 769:770], SUBLN_EPS), writes=[CST])
            P.emit("pool", lambda e: e.memset(cst[0:33, 784:794], NEG), writes=[CST])
            P.dma("sp", ident, ident_d[:, :], CST.sem, writes=[CST])
            P.dma("sp", cs128, cs128_d[:, :], CST.sem, writes=[CST])
            P.dma("sp", cvec[:], cvec_d[:, :], CST.sem, writes=[CST])
            P.dma("sp", relbc[:], bass.AP(tensor=relb_d.tensor, offset=0, ap=[[0, 128], [1, 320]]), CST.sem, writes=[CST])
            P.dma("sp", cst[0:32, 784:794], relb_d[:, :], CST.sem, writes=[CST])
            lam = A.take(8 * 64).rearrange("p (a b) -> p a b", a=8)
            LB = P.buf("lam", dma=True)
            for i in range(8):
                P.dma("sp", lam[:, i, :], bass.AP(tensor=lamv_d.tensor, offset=i * 64, ap=[[0, 128], [1, 64]]), LB.sem, writes=[LB])
            sc = A.take(8)
            for l in range(L):
                lam_init = 0.8 - 0.6 * math.exp(-0.3 * l)
                for j in range(2):
                    P.tt("dve", lam[:, 4 * l + 2 * j, :], lam[:, 4 * l + 2 * j, :], lam[:, 4 * l + 2 * j + 1, :], ALU.mult, [LB], [LB])
                    P.emit("dve", lambda e, l=l, j=j: e.reduce_sum(out=sc[:, 2 * l + j:2 * l + j + 1], in_=lam[:, 4 * l + 2 * j, :], axis=AX.X), [LB], [LB])
                P.act(sc[:, 2 * l:2 * l + 2], sc[:, 2 * l:2 * l + 2], AF.Exp, [LB], [LB])
                P.tt("dve", sc[:, 4 + l:5 + l], sc[:, 2 * l + 1:2 * l + 2], sc[:, 2 * l:2 * l + 1], ALU.subtract, [LB], [LB])
                P.emit("dve", lambda e, l=l, li=lam_init: e.tensor_scalar(cst[:, 772 + 2 * l:773 + 2 * l], sc[:, 4 + l:5 + l], -li, None, op0=ALU.add), [LB], [CST])
                P.emit("dve", lambda e, l=l, li=lam_init: e.tensor_scalar(cst[:, 773 + 2 * l:774 + 2 * l], cvec[:, l * NCV + 48:l * NCV + 49], 1.0 - li, None, op0=ALU.mult), [CST], [CST])
            oh = A.take(1536, rows=33)
            OH = P.buf("oh", dma=True)
            fv = A.take(1536, rows=4)
            FV = P.buf("fv", dma=True)
            P.dma("sp", oh, ohc_d[:, :], OH.sem, writes=[OH])
            for c3 in range(3):
                P.mm(ps[0][0:4, :], rb33[:, 6:10], oh[:, c3 * 512:(c3 + 1) * 512], True, True, [CST, OH], [PS[0]])
                P.copy("dve", fv[:, c3 * 512:(c3 + 1) * 512], ps[0][0:4, :], [PS[0]], [FV])
            P.dma("pool", bvc_s[:, :], fv, FV.sem, reads=[FV])
            P.dma("sp", oh, ohb_d[:, :], OH.sem, writes=[OH])
            fb = A.take(512, rows=2)
            FB = P.buf("fb", dma=True)
            for g in range(3):
                P.mm(ps[1][0:2, :], rb33[:, 2 * g:2 * g + 2], oh[:, g * 512:(g + 1) * 512], True, True, [CST, OH], [PS[1]])
                P.copy("dve", fb, ps[1][0:2, :], [PS[1]], [FB])
                P.dma("pool", bvb_s[2 * g:2 * g + 2, :], fb, FB.sem, reads=[FB])

        def stage_in():
            A = Arena()
            xin = [A.take(1024) for _ in range(2)]
            XI = [P.buf("xin%d" % i, dma=True) for i in range(2)]
            for tt in range(NT):
                b = tt % 2
                P.dma("sp", xin[b], x_d[tsl(tt), :], XI[b].sem, writes=[XI[b]])
                for cq in range(2):
                    pb = (tt * 2 + cq) % 4
                    for k in range(4):
                        c = cq * 4 + k
                        P.tr(ps[pb][:, k * 128:(k + 1) * 128], xin[b][:, tsl(c)], ident, [XI[b], CST], [PS[pb]], signal=(k == 3))
                    P.copy("dve" if cq == 0 else "act", xT[:, cq * 4:cq * 4 + 4, tsl(tt)],
                           ps[pb][:, :].rearrange("p (a b) -> p a b", a=4), [PS[pb]], [XT[tt // 4]])

        def norm_group(tg, gcol, dst, DST, sq, SQ, rs, RS, psn):
            for c in range(DC):
                b = c % 2
                P.tt("pool", sq[b], xT[:, c, gsl(tg)], xT[:, c, gsl(tg)], ALU.mult, [XT[tg]], [SQ[b]])
                P.mm(ps[psn][:, :], ones_mean, sq[b], c == 0, c == DC - 1, [CST, SQ[b]], [PS[psn]], signal=True)
            P.act(rs, ps[psn][:, :], AF.Sqrt, [PS[psn], CST], [RS], bias=eps6)
            P.emit("dve", lambda e: e.reciprocal(rs, rs), [RS], [RS])
            for c in range(DC):
                P.stt("dve", dst[:, c, :], xT[:, c, gsl(tg)], gcol[:, c:c + 1], rs, ALU.mult, ALU.mult, [XT[tg], RS, CST], [DST])

        def stage_ffn(l, k):
            A = Arena()

            def bfv(n):
                return A.take(n // 2).bitcast(BF16)

            hh = bfv(DC * TG).rearrange("p (c t) -> p c t", c=DC)
            hl = bfv(DC * TG).rearrange("p (c t) -> p c t", c=DC)
            HH = [P.buf("hh%d" % c) for c in range(DC)]
            HL = [P.buf("hl%d" % c) for c in range(DC)]
            ah = bfv(NF * TG).rearrange("p (f t) -> p f t", f=NF)
            al = bfv(NF * TG).rearrange("p (f t) -> p f t", f=NF)
            AH = [P.buf("ah%d" % f) for f in range(NF)]
            AL = [P.buf("al%d" % f) for f in range(NF)]
            wh = [bfv(2816) for _ in range(2)]
            wl = [bfv(2816) for _ in range(2)]
            WH = [P.buf("wh%d" % i) for i in range(2)]
            WL = [P.buf("wl%d" % i) for i in range(2)]
            sg = [A.take(TG) for _ in range(2)]
            SG = [P.buf("sg%d" % i) for i in range(2)]
            sq = [A.take(TG) for _ in range(2)]
            SQ = [P.buf("sq%d" % i) for i in range(2)]
            rs = A.take(TG)
            RS = P.buf("rs")
            gcol = cvec[:, l * NCV + (0 if k == 0 else 16):l * NCV + (0 if k == 0 else 16) + 8]

            def norm_split(tg):
                for c in range(DC):
                    b = c % 2
                    P.tt("pool", sq[b], xT[:, c, gsl(tg)], xT[:, c, gsl(tg)], ALU.mult, [XT[tg]], [SQ[b]])
                    P.mm(ps[6][:, :], ones_mean, sq[b], c == 0, c == DC - 1, [CST, SQ[b]], [PS[6]], signal=True)
                P.act(rs, ps[6][:, :], AF.Sqrt, [PS[6], CST], [RS], bias=eps6)
                P.emit("dve", lambda e: e.reciprocal(rs, rs), [RS], [RS])
                for c in range(DC):
                    b = c % 2
                    P.stt("dve", sq[b], xT[:, c, gsl(tg)], gcol[:, c:c + 1], rs, ALU.mult, ALU.mult, [XT[tg], RS, CST], [SQ[b]])
                    P.act(hh[:, c, :], sq[b], AF.Copy, [SQ[b]], [HH[c]])
                    P.tt("pool", hl[:, c, :], sq[b], hh[:, c, :], ALU.subtract, [SQ[b], HH[c]], [HL[c]])

            tiles = []
            for tg in range(NTG):
                tiles += [("gu", tg, f) for f in range(NF)] + [("dn", tg, dc) for dc in range(DC)]

            def prepare(i):
                kind, tg, j = tiles[i]
                n = 2048 if kind == "gu" else 2816
                src = wgu_d[(l * 2 + k) * NF + j] if kind == "gu" else wd_d[(l * 2 + k) * 8 + j]
                w, WB = W.next(src, 128, n)
                b = i % 2
                P.act(wh[b][:, 0:n], w[:, 0:n], AF.Copy, [WB], [WH[b]])
                P.tt("dve", wl[b][:, 0:n], w[:, 0:n], wh[b][:, 0:n], ALU.subtract, [WB, WH[b]], [WL[b]])

            norm_split(0)
            prepare(0)
            for i, (kind, tg, j) in enumerate(tiles):
                if i + 1 < len(tiles):
                    prepare(i + 1)
                b = i % 2
                if kind == "gu":
                    f = j
                    whv = wh[b][:, 0:2048].rearrange("p (a c j) -> p a c j", a=2, c=DC)
                    wlv = wl[b][:, 0:2048].rearrange("p (a c j) -> p a c j", a=2, c=DC)
                    pg, pu = f % 2, 2 + f % 2
                    for a_, pb in ((0, pg), (1, pu)):
                        for c in range(DC):
                            P.mm(ps[pb][:, :], whv[:, a_, c, :], hh[:, c, :], c == 0, False, [WH[b], HH[c]], [PS[pb]], signal=False)
                            P.mm(ps[pb][:, :], whv[:, a_, c, :], hl[:, c, :], False, False, [WH[b], HL[c]], [PS[pb]], signal=False)
                            P.mm(ps[pb][:, :], wlv[:, a_, c, :], hh[:, c, :], False, c == DC - 1, [WL[b], HH[c]], [PS[pb]])
                    P.act(sg[f % 2], ps[pg][:, :], AF.Silu, [PS[pg]], [SG[f % 2]])
                    P.tt("dve", sg[f % 2], sg[f % 2], ps[pu][:, :], ALU.mult, [SG[f % 2], PS[pu]], [SG[f % 2]])
                    P.act(ah[:, f, :], sg[f % 2], AF.Copy, [SG[f % 2]], [AH[f]])
                    P.tt("pool", al[:, f, :], sg[f % 2], ah[:, f, :], ALU.subtract, [SG[f % 2], AH[f]], [AL[f]])
                else:
                    dc = j
                    if dc == 0 and tg + 1 < NTG:
                        norm_split(tg + 1)
                    whv = wh[b][:, 0:2816].rearrange("p (f j) -> p f j", f=NF)
                    wlv = wl[b][:, 0:2816].rearrange("p (f j) -> p f j", f=NF)
                    py = 4 + dc % 2
                    for f in range(NF):
                        P.mm(ps[py][:, :], whv[:, f, :], ah[:, f, :], f == 0, False, [WH[b], AH[f]], [PS[py]], signal=False)
                        P.mm(ps[py][:, :], whv[:, f, :], al[:, f, :], False, False, [WH[b], AL[f]], [PS[py]], signal=False)
                        P.mm(ps[py][:, :], wlv[:, f, :], ah[:, f, :], False, f == NF - 1, [WL[b], AH[f]], [PS[py]])
                    P.stt("dve", xT[:, dc, gsl(tg)], ps[py][:, :], 0.5, xT[:, dc, gsl(tg)], ALU.mult, ALU.add, [PS[py], XT[tg]], [XT[tg]])

        def stage_m1(l):
            A = Arena()

            def bfv(n):
                return A.take(n // 2).bitcast(BF16)

            uh = [bfv(DC * TG).rearrange("p (c t) -> p c t", c=DC) for _ in range(2)]
            ul = [bfv(DC * TG).rearrange("p (c t) -> p c t", c=DC) for _ in range(2)]
            UH = [[P.buf("uh%d_%d" % (i, c)) for c in range(DC)] for i in range(2)]
            UL = [[P.buf("ul%d_%d" % (i, c)) for c in range(DC)] for i in range(2)]
            sq = [A.take(TG) for _ in range(2)]
            SQ = [P.buf("sq%d" % i) for i in range(2)]
            rs = A.take(TG)
            RS = P.buf("rs")
            stF = [A.take(TG) for _ in range(3)]
            STF = [P.buf("stF%d" % i, dma=True) for i in range(3)]
            stV = [A.take(TG) for _ in range(2)]
            STV = [P.buf("stV%d" % i, dma=True) for i in range(2)]
            wh = [bfv(3072) for _ in range(2)]
            wl = [bfv(3072) for _ in range(2)]
            WH = [P.buf("wh%d" % i) for i in range(2)]
            WL = [P.buf("wl%d" % i) for i in range(2)]
            gcol = cvec[:, l * NCV + 8:l * NCV + 16]

            def norm_split(tg):
                hh, hl, HH, HL = uh[tg % 2], ul[tg % 2], UH[tg % 2], UL[tg % 2]
                for c in range(DC):
                    b = c % 2
                    P.tt("pool", sq[b], xT[:, c, gsl(tg)], xT[:, c, gsl(tg)], ALU.mult, [XT[tg]], [SQ[b]])
                    P.mm(ps[6][:, :], ones_mean, sq[b], c == 0, c == DC - 1, [CST, SQ[b]], [PS[6]], signal=True)
                P.act(rs, ps[6][:, :], AF.Sqrt, [PS[6], CST], [RS], bias=eps6)
                P.emit("dve", lambda e: e.reciprocal(rs, rs), [RS], [RS])
                for c in range(DC):
                    b = c % 2
                    P.stt("dve", sq[b], xT[:, c, gsl(tg)], gcol[:, c:c + 1], rs, ALU.mult, ALU.mult, [XT[tg], RS, CST], [SQ[b]])
                    P.act(hh[:, c, :], sq[b], AF.Copy, [SQ[b]], [HH[c]])
                    P.tt("pool", hl[:, c, :], sq[b], hh[:, c, :], ALU.subtract, [SQ[b], HH[c]], [HL[c]])

            tiles = []
            for tg in range(NTG):
                tiles += [("f", tg, j, winf_d[l * 22 + j], 1024) for j in range(22)]
                tiles += [("v", tg, (vb_s, 0, 384), winvb_d[l], 3072),
                          ("v", tg, (vc_s, 0, 256), winvc_d[l * 2], 2048),
                          ("v", tg, (vc_s, 256, 256), winvc_d[l * 2 + 1], 2048)]

            def prepare(i):
                kind, tg, j, src, n = tiles[i]
                w, WB = W.next(src, 128, n)
                b = i % 2
                P.act(wh[b][:, 0:n], w[:, 0:n], AF.Copy, [WB], [WH[b]])
                P.tt("dve", wl[b][:, 0:n], w[:, 0:n], wh[b][:, 0:n], ALU.subtract, [WB, WH[b]], [WL[b]])

            norm_split(0)
            prepare(0)
            nF = 0
            nV = 0
            for i, (kind, tg, j, src, n) in enumerate(tiles):
                if i + 1 < len(tiles):
                    prepare(i + 1)
                b = i % 2
                hh, hl, HH, HL = uh[tg % 2], ul[tg % 2], UH[tg % 2], UL[tg % 2]
                if kind == "f":
                    whv = wh[b][:, 0:1024].rearrange("p (c j) -> p c j", c=DC)
                    wlv = wl[b][:, 0:1024].rearrange("p (c j) -> p c j", c=DC)
                    pa = j % 2
                    for c in range(DC):
                        P.mm(ps[pa][:, :], whv[:, c, :], hh[:, c, :], c == 0, False, [WH[b], HH[c]], [PS[pa]], signal=False)
                        P.mm(ps[pa][:, :], whv[:, c, :], hl[:, c, :], False, False, [WH[b], HL[c]], [PS[pa]], signal=False)
                        P.mm(ps[pa][:, :], wlv[:, c, :], hh[:, c, :], False, c == DC - 1, [WL[b], HH[c]], [PS[pa]])
                    P.copy("act" if j % 2 else "dve", stF[nF % 3], ps[pa][:, :], [PS[pa]], [STF[nF % 3]])
                    dst = a_s[tsl(j), gsl(tg)] if j < 8 else qk_s[tsl(j - 8), gsl(tg)]
                    P.dma("pool", dst, stF[nF % 3], STF[nF % 3].sem, reads=[STF[nF % 3]])
                    nF += 1
                else:
                    dstT, c0, nc_ = j
                    if c0 == 0 and dstT is vb_s and tg + 1 < NTG:
                        norm_split(tg + 1)
                    whv = wh[b][:, 0:DC * nc_].rearrange("p (c j) -> p c j", c=DC)
                    wlv = wl[b][:, 0:DC * nc_].rearrange("p (c j) -> p c j", c=DC)
                    for t4 in range(4):
                        pv = 4 + nV % 2
                        for c in range(DC):
                            P.mm(ps[pv][:, 0:nc_], hh[:, c, tsl(t4)], whv[:, c, :], c == 0, False, [WH[b], HH[c]], [PS[pv]], signal=False)
                            P.mm(ps[pv][:, 0:nc_], hl[:, c, tsl(t4)], whv[:, c, :], False, False, [WH[b], HL[c]], [PS[pv]], signal=False)
                            P.mm(ps[pv][:, 0:nc_], hh[:, c, tsl(t4)], wlv[:, c, :], False, c == DC - 1, [WL[b], HH[c]], [PS[pv]])
                        P.copy("act" if nV % 2 else "dve", stV[nV % 2][:, 0:nc_], ps[pv][:, 0:nc_], [PS[pv]], [STV[nV % 2]])
                        P.dma("pool", dstT[tsl(tg * 4 + t4), c0:c0 + nc_], stV[nV % 2][:, 0:nc_], STV[nV % 2].sem, reads=[STV[nV % 2]])
                        nV += 1

        def stage_m2(l):
            A = Arena()
            H = S // 2
            aT = [A.take(S) for _ in range(2)]
            AT = [P.buf("aTg%d" % i, dma=True) for i in range(2)]
            apf = A.take(H)
            amf = A.take(H)
            FO = P.buf("fold")
            nyb = [A.take(TG) for _ in range(2)]
            NYB = [P.buf("ny%d" % i) for i in range(2)]
            Tf = [A.take(8 * 256).rearrange("p (s j) -> p s j", s=8) for _ in range(2)]
            TF = [P.buf("Tf%d" % i) for i in range(2)]
            osb = [A.take(TG) for _ in range(2)]
            OSB = [P.buf("osb%d" % i) for i in range(2)]
            Y = [A.take(S) for _ in range(2)]
            YB = [P.buf("Y%d" % i, dma=True) for i in range(2)]
            fc = A.take(520)
            FC = P.buf("fc", dma=True)
            sgnn = fc[:, 0:512]
            cnyq = fc[:, 512:520]
            P.dma("sp", fc, fconst_d[:, :], FC.sem, writes=[FC])
            P.emit("pool", lambda e: e.memset(amf[:, 0:1], 0.0), writes=[FO])

            def load(g):
                P.dma("sp", aT[g % 2], a_s[tsl(g), :], AT[g % 2].sem, writes=[AT[g % 2]])

            def fold(g):
                a, AB = aT[g % 2], AT[g % 2]
                P.tt("pool", apf[:, 1:H], a[:, 1:H], a[:, S - 1:H:-1], ALU.add, [AB], [FO])
                P.copy("pool", apf[:, 0:1], a[:, 0:1], [AB], [FO])
                P.tt("dve", amf[:, 1:H], a[:, 1:H], a[:, S - 1:H:-1], ALU.subtract, [AB], [FO])
                ny, NY = nyb[g % 2], NYB[g % 2]
                P.emit("dve", lambda e: e.tensor_scalar(ny, sgnn, a[:, H:H + 1], None, op0=ALU.mult), [AB, FC], [NY])

            def chdft(g):
                t, TB_ = Tf[g % 2], TF[g % 2]
                for st_ in range(8):
                    pc = st_ % 2
                    P.mm(ps[pc][:, 0:128], apf[:, tsl(st_)], cs128[:, 0:128], True, True, [FO, CST], [PS[pc]], signal=False)
                    P.mm(ps[pc][:, 128:256], amf[:, tsl(st_)], cs128[:, 128:256], True, True, [FO, CST], [PS[pc]], signal=True)
                    P.copy("act" if st_ % 2 else "dve", t[:, st_, :], ps[pc][:, 0:256], [PS[pc]], [TB_])

            def seqdft(g):
                t, TB_ = Tf[g % 2], TF[g % 2]
                ny, NY = nyb[g % 2], NYB[g % 2]
                for sg_ in range(2):
                    for mat in range(2):
                        bank = 2 + sg_ if mat == 0 else 4 + sg_
                        for piece in range(2):
                            w, WB = W.next(dftf_d[(sg_ * 2 + mat) * 2 + piece], 128, 2048)
                            wv = w[:, 0:2048].rearrange("p (i j) -> p i j", i=4)
                            for i in range(4):
                                s_t = piece * 4 + i
                                P.mm(ps[bank][:, :], t[:, s_t, mat * 128:(mat + 1) * 128], wv[:, i, :],
                                     piece == 0 and i == 0, mat == 1 and piece == 1 and i == 3,
                                     [WB, TB_], [PS[bank]], signal=(i == 3))
                    P.mm(ps[2 + sg_][:, :], cs128[:, 0:128], ny, False, True, [CST, NY], [PS[2 + sg_]])
                for s_t in range(8):
                    P.mm(ps[6][:, 0:1], t[:, s_t, 0:128], cnyq[:, s_t:s_t + 1], s_t == 0, False, [TB_, FC], [PS[6]], signal=False)
                P.mm(ps[6][:, 0:1], cs128[:, 0:128], ny[:, 0:1], False, True, [CST, NY], [PS[6]])

            def assemble(g):
                y, YB_ = Y[g % 2], YB[g % 2]
                for sg_ in range(2):
                    e_ps, o_ps = ps[2 + sg_], ps[4 + sg_]
                    P.copy("act", osb[sg_], o_ps[:, :], [PS[4 + sg_]], [OSB[sg_]])
                    P.tt("dve", y[:, sg_ * TG:(sg_ + 1) * TG], e_ps[:, :], osb[sg_], ALU.subtract, [PS[2 + sg_], OSB[sg_]], [YB_])
                    if sg_ == 0:
                        P.tt("dve", y[:, S - 1:S - TG:-1], e_ps[:, 1:TG], osb[0][:, 1:TG], ALU.add, [PS[2], OSB[0]], [YB_])
                    else:
                        P.tt("dve", y[:, S - TG:H:-1], e_ps[:, :], osb[1], ALU.add, [PS[3], OSB[1]], [YB_])
                P.copy("act", y[:, H:H + 1], ps[6][:, 0:1], [PS[6]], [YB_])
                P.dma("pool", z_s[tsl(g), :], y, YB_.sem, reads=[YB_])

            load(0)
            load(1)
            fold(0)
            chdft(0)
            for g in range(8):
                if g + 1 < 8:
                    fold(g + 1)
                seqdft(g)
                if g + 1 < 8:
                    chdft(g + 1)
                if g + 2 < 8:
                    load(g + 2)
                assemble(g)

        def stage_m3(l):
            A = Arena()
            qb = A.take(S)
            kb = A.take(S)
            QB = P.buf("qb", dma=True)
            KB = P.buf("kb", dma=True)
            vg = [A.take(NT * 128).rearrange("p (s j) -> p s j", s=NT) for _ in range(2)]
            VG = [P.buf("vg%d" % i, dma=True) for i in range(2)]
            emb = A.take(6 * 384).rearrange("p (a b) -> p a b", a=6)
            EMB = P.buf("emb", dma=True)
            acc = A.take(2 * 2 * S, rows=64).rearrange("p (h k t) -> p h k t", h=2, k=2)
            ACC = [P.buf("acc%d" % i, dma=True) for i in range(2)]
            E = [A.take(384).rearrange("p (a b) -> p a b", a=3) for _ in range(4)]
            EB = [P.buf("E%d" % i) for i in range(4)]
            P.dma("sp", emb, bass.AP(tensor=bvb_s.tensor, offset=0, ap=[[1, 128], [512, 6], [1, 384]]), EMB.sem, writes=[EMB])
            P.act(emb, emb, AF.Exp, [EMB], [EMB])

            def geom(g, i):
                d = DIL[g]
                tpc = NT // d
                tb = i % tpc
                dl = [dd for dd in (-1, 0, 1) if 0 <= tb + dd < tpc]

                def perm(j):
                    r, t_ = j // tpc, j % tpc
                    o = t_ * 128 * d + r
                    return slice(o, o + 127 * d + 1, d)
                return dl, perm

            def load_g(g):
                d = DIL[g]
                tpc = NT // d
                P.dma("sp", qb, qk_s[tsl(g), :], QB.sem, writes=[QB])
                P.dma("sp", kb, qk_s[tsl(3 + g), :], KB.sem, writes=[KB])
                v, VB_ = vg[g % 2], VG[g % 2]
                for r in range(d):
                    for t0 in range(0, tpc, 4):
                        nt_ = min(4, tpc - t0)
                        src = bass.AP(tensor=vb_s.tensor, offset=r * 384 + g * 128 + t0 * 128 * d * 384,
                                      ap=[[d * 384, 128], [128 * d * 384, nt_], [1, 128]])
                        P.dma("sp", v[:, r * tpc + t0:r * tpc + t0 + nt_, :], src, VB_.sem, writes=[VB_])

            upairs = [(g, i) for g in range(3) for i in range(NT)]

            def emit_S(pi):
                g, i = upairs[pi]
                dl, perm = geom(g, i)
                a0, a1 = dl[0] + 1, dl[-1] + 2
                for dd in dl:
                    for hh in range(2):
                        n = 2 * pi + hh
                        hs = slice(hh * 64, (hh + 1) * 64)
                        pss = ps[n % 4][:, 0:384].rearrange("p (a b) -> p a b", a=3)
                        P.mm(pss[:, dd + 1, :], kb[hs, perm(i + dd)], qb[hs, perm(i)], True, True, [KB, QB], [PS[n % 4]], signal=(dd == dl[-1]))
                for hh in range(2):
                    n = 2 * pi + hh
                    pss = ps[n % 4][:, 0:384].rearrange("p (a b) -> p a b", a=3)
                    mview = emb[:, 2 * g + hh, :].rearrange("p (a b) -> p a b", a=3)[:, :, ::-1]
                    e_, EB_ = E[n % 4], EB[n % 4]
                    P.act(e_[:, a0:a1, :], pss[:, a0:a1, :], AF.Exp, [PS[n % 4]], [EB_], scale=0.125)
                    P.tt("dve", e_[:, a0:a1, :], e_[:, a0:a1, :], mview[:, a0:a1, :], ALU.mult, [EB_, EMB], [EB_])

            def emit_PV(pi):
                g, i = upairs[pi]
                dl, perm = geom(g, i)
                v, VB_ = vg[g % 2], VG[g % 2]
                for hh in range(2):
                    n = 2 * pi + hh
                    hs = slice(hh * 64, (hh + 1) * 64)
                    e_, EB_ = E[n % 4], EB[n % 4]
                    pU = 4 + n % 4
                    psu = ps[pU][0:64, 0:256].rearrange("p (a b) -> p a b", a=2)
                    for dd in dl:
                        P.mm(psu[:, 0, :], v[:, i + dd, hs], e_[:, dd + 1, :], dd == dl[0], dd == dl[-1], [VB_, EB_], [PS[pU]], signal=False)
                    for dd in dl:
                        P.mm(psu[:, 1, :], ones1[:, 0:64], e_[:, dd + 1, :], dd == dl[0], dd == dl[-1], [CST, EB_], [PS[pU]])
                    dst = acc[:, hh, :, perm(i)]
                    if g == 0:
                        P.copy("dve", dst, psu, [PS[pU]], [ACC[hh]])
                    else:
                        P.tt("dve", dst, psu, dst, ALU.add, [PS[pU], ACC[hh]], [ACC[hh]])

            NPB = len(upairs)
            load_g(0)
            emit_S(0)
            for pi in range(NPB):
                g, i = upairs[pi]
                if pi + 1 < NPB:
                    if upairs[pi + 1][0] != g:
                        load_g(g + 1)
                    emit_S(pi + 1)
                emit_PV(pi)
            for hh in range(2):
                P.emit("dve", lambda e, hh=hh: e.reciprocal(acc[:, hh, 1, :], acc[:, hh, 1, :]), [ACC[hh]], [ACC[hh]])
                P.tt("dve", acc[:, hh, 0, :], acc[:, hh, 0, :], acc[:, hh, 1, :], ALU.mult, [ACC[hh]], [ACC[hh]])
                P.dma("pool", z_s[8 * 128 + hh * 64:8 * 128 + (hh + 1) * 64, :], acc[:, hh, 0, :], ACC[hh].sem, reads=[ACC[hh]])

        def stage_m4(l):
            A = Arena()
            qh = [A.take(S) for _ in range(2)]
            kh = [A.take(S) for _ in range(2)]
            vh = [A.take(NT * 128).rearrange("p (s j) -> p s j", s=NT) for _ in range(2)]
            strip = [A.take(1408) for _ in range(2)]
            QH = [P.buf("qh%d" % i, dma=True) for i in range(2)]
            KH = [P.buf("kh%d" % i, dma=True) for i in range(2)]
            VH = [P.buf("vh%d" % i, dma=True) for i in range(2)]
            SB_ = [P.buf("strip%d" % i, dma=True) for i in range(2)]
            E = [A.take(TG) for _ in range(4)]
            EB = [P.buf("E%d" % i) for i in range(4)]
            dacc = [[A.take(TG) for _ in range(2)] for _ in range(2)]
            DACC = [[P.buf("dacc%d%d" % (i, j)) for j in range(2)] for i in range(2)]
            r0 = A.take(TG)
            r1 = A.take(TG)
            av = A.take(TG)
            TMP = P.buf("tmp")
            TMP2 = P.buf("tmp2")
            zst = [A.take(TG) for _ in range(2)]
            ZST = [P.buf("zst%d" % i, dma=True) for i in range(2)]
            neglam = lcol[:, 2 * l:2 * l + 1]
            gsub = lcol[:, 2 * l + 1:2 * l + 2]

            def load_head(h):
                b = h % 2
                P.dma("sp", qh[b], qk_s[tsl(6 + h), :], QH[b].sem, writes=[QH[b]])
                P.dma("sp", kh[b], qk_s[tsl(10 + h), :], KH[b].sem, writes=[KH[b]])
                vsrc = vc_s[:, h * 128:(h + 1) * 128].rearrange("(s p) j -> p s j", p=128)
                for q4 in range(4):
                    P.dma("sp", vh[b][:, q4 * 4:(q4 + 1) * 4, :], vsrc[:, q4 * 4:(q4 + 1) * 4, :], VH[b].sem, writes=[VH[b]])
                P.dma("sp", strip[b], bass.AP(tensor=bvc_s.tensor, offset=h * 1536, ap=[[1, 128], [1, 1408]]), SB_[b].sem, writes=[SB_[b]])
                P.act(strip[b], strip[b], AF.Exp, [SB_[b]], [SB_[b]])

            pairs = [(h, qg, kt) for h in range(4) for qg in range(NTG) for kt in range(NT)]
            state = {"s": 0}

            def emit_S(p):
                h, qg, kt = pairs[p]
                b = h % 2
                for m in range(2):
                    n = 2 * p + m
                    ms = slice(m * 64, (m + 1) * 64)
                    P.mm(ps[n % 4][:, :], kh[b][ms, tsl(kt)], qh[b][ms, qg * TG:(qg + 1) * TG], True, True, [KH[b], QH[b]], [PS[n % 4]])

            def emit_exp(n):
                h, qg, kt = pairs[n // 2]
                m = n % 2
                b = h % 2
                Q0 = qg * TG
                pS = n % 4
                e_, EB_ = E[n % 4], EB[n % 4]
                bpos = relbc[:, 31 * 10 + 6 + h:31 * 10 + 7 + h]
                bneg = relbc[:, 15 * 10 + 6 + h:15 * 10 + 7 + h]
                qa = min(max(128 * kt - 640, Q0), Q0 + TG)
                qe = min(max(128 * kt + 768, Q0), Q0 + TG)
                if qa > Q0:
                    P.act(e_[:, 0:qa - Q0], ps[pS][:, 0:qa - Q0], AF.Exp, [PS[pS], CST], [EB_], bias=bpos, scale=0.125)
                if qe > qa:
                    P.act(e_[:, qa - Q0:qe - Q0], ps[pS][:, qa - Q0:qe - Q0], AF.Exp, [PS[pS]], [EB_], scale=0.125)
                    clo = 128 * kt - qe + 768
                    chi = 128 * kt - qa + 767
                    P.tt("dve", e_[:, qa - Q0:qe - Q0], e_[:, qa - Q0:qe - Q0], strip[b][:, clo:chi + 1][:, ::-1], ALU.mult,
                         [EB_, SB_[b]], [EB_])
                if qe < Q0 + TG:
                    P.act(e_[:, qe - Q0:TG], ps[pS][:, qe - Q0:TG], AF.Exp, [PS[pS], CST], [EB_], bias=bneg, scale=0.125)

            def emit_PV(n):
                h, qg, kt = pairs[n // 2]
                m = n % 2
                b = h % 2
                par = (h * NTG + qg) % 2
                e_, EB_ = E[n % 4], EB[n % 4]
                P.mm(ps[4 + m][:, :], vh[b][:, kt, :], e_, kt == 0, kt == NT - 1, [VH[b], EB_], [PS[4 + m]], signal=True)
                eng = "pool" if m == 0 else "dve"
                if kt == 0:
                    P.copy(eng, dacc[par][m], e_, [EB_], [DACC[par][m]])
                else:
                    P.tt(eng, dacc[par][m], dacc[par][m], e_, ALU.add, [EB_, DACC[par][m]], [DACC[par][m]])

            def epilogue_a(h, qg):
                par = (h * NTG + qg) % 2
                P.mm(ps[6][:, :], ones1, dacc[par][0], True, True, [CST, DACC[par][0]], [PS[6]])
                P.mm(ps[7][:, :], ones1, dacc[par][1], True, True, [CST, DACC[par][1]], [PS[7]])
                P.emit("dve", lambda e: e.reciprocal(r0, ps[6][:, :]), [PS[6]], [TMP])
                P.emit("dve", lambda e: e.reciprocal(r1, ps[7][:, :]), [PS[7]], [TMP])
                P.tt("dve", r0, ps[4][:, :], r0, ALU.mult, [PS[4], TMP], [TMP])
                P.tt("dve", r1, ps[5][:, :], r1, ALU.mult, [PS[5], TMP], [TMP])
                P.stt("dve", av, r1, neglam, r0, ALU.mult, ALU.add, [TMP, CST], [TMP2])

            def epilogue_b(h, qg):
                P.tt("pool", r1, av, av, ALU.mult, [TMP2, TMP], [TMP])
                P.mm(ps[6][:, :], ones_sub, r1, True, True, [CST, TMP], [PS[6]])
                P.act(r0, ps[6][:, :], AF.Sqrt, [PS[6], CST, TMP], [TMP], bias=eps5)
                P.emit("dve", lambda e: e.reciprocal(r0, r0), [TMP], [TMP])
                k = state["s"]
                z, Z = zst[k % 2], ZST[k % 2]
                P.stt("dve", z, av, gsub, r0, ALU.mult, ALU.mult, [TMP, TMP2, CST], [Z])
                P.dma("pool", z_s[tsl(9 + h), qg * TG:(qg + 1) * TG], z, Z.sem, reads=[Z])
                state["s"] += 1

            NP = len(pairs)
            load_head(0)
            emit_S(0)
            emit_exp(0)
            emit_exp(1)
            pending = None
            for p in range(NP):
                h, qg, kt = pairs[p]
                if kt == 0 and qg == 0 and h + 1 < 4:
                    load_head(h + 1)
                if p + 1 < NP:
                    emit_S(p + 1)
                    emit_exp(2 * p + 2)
                    emit_exp(2 * p + 3)
                emit_PV(2 * p)
                emit_PV(2 * p + 1)
                if pending is not None and kt == 2:
                    epilogue_b(*pending)
                    pending = None
                if kt == NT - 1:
                    epilogue_a(h, qg)
                    pending = (h, qg)
            epilogue_b(*pending)

        def stage_m5(l):
            A = Arena()
            uT = [A.take(DC * TG).rearrange("p (c t) -> p c t", c=DC) for _ in range(1)]
            UT = [P.buf("uT0")]
            sq = [A.take(TG) for _ in range(2)]
            SQ = [P.buf("sq%d" % i) for i in range(2)]
            rs = A.take(TG)
            RS = P.buf("rs")
            zsl = A.take(13 * TG).rearrange("p (k t) -> p k t", k=13)
            ZS = P.buf("zsl", dma=True)
            mg = A.take(DC * TG).rearrange("p (c t) -> p c t", c=DC)
            MG = [P.buf("mg%d" % i) for i in range(DC)]
            sig = [A.take(TG) for _ in range(2)]
            SIG = [P.buf("sig%d" % i) for i in range(2)]
            tmp = [A.take(TG) for _ in range(2)]
            TMPB = [P.buf("tmp%d" % i) for i in range(2)]
            gcol = cvec[:, l * NCV + 8:l * NCV + 16]
            brs = ((wm5a_d, 8, 0), (wm5b_d, 1, 8), (wm5c_d, 4, 9))
            n = 0
            for tg in range(NTG):
                u, U = uT[0], UT[0]
                norm_group(tg, gcol, u, U, sq, SQ, rs, RS, 6)
                zsrc = z_s[:, gsl(tg)].rearrange("(k p) t -> p k t", p=128)
                for (k0, k1) in ((0, 4), (4, 8), (8, 13)):
                    P.dma("sp", zsl[:, k0:k1, :], zsrc[:, k0:k1, :], ZS.sem, writes=[ZS])
                for dc in range(DC):
                    for br, (wsrc, nk, koff) in enumerate(brs):
                        w, WB = W.next(wsrc[l * 8 + dc], 128, (8 + nk) * 128)
                        wv = w[:, 0:(8 + nk) * 128].rearrange("p (c j) -> p c j", c=8 + nk)
                        pg, py = n % 2, 2 + n % 2
                        for c in range(DC):
                            P.mm(ps[pg][:, :], wv[:, c, :], u[:, c, :], c == 0, c == DC - 1, [WB, U], [PS[pg]])
                        bcol = cvec[:, l * NCV + 24 + br * 8 + dc:l * NCV + 25 + br * 8 + dc]
                        P.act(sig[n % 2], ps[pg][:, :], AF.Sigmoid, [PS[pg], CST], [SIG[n % 2]], bias=bcol)
                        for kk in range(nk):
                            P.mm(ps[py][:, :], wv[:, 8 + kk, :], zsl[:, koff + kk, :], kk == 0, kk == nk - 1, [WB, ZS], [PS[py]])
                        if br == 0:
                            P.tt("dve", mg[:, dc, :], sig[n % 2], ps[py][:, :], ALU.mult, [SIG[n % 2], PS[py]], [MG[dc]])
                        else:
                            P.tt("dve", tmp[n % 2], sig[n % 2], ps[py][:, :], ALU.mult, [SIG[n % 2], PS[py]], [TMPB[n % 2]])
                            P.tt("pool", mg[:, dc, :], mg[:, dc, :], tmp[n % 2], ALU.add, [TMPB[n % 2], MG[dc]], [MG[dc]])
                        n += 1
                for d2 in range(DC):
                    w, WB = W.next(wout_d[l * 8 + d2], 128, 1024)
                    wv = w[:, 0:1024].rearrange("p (c j) -> p c j", c=DC)
                    po = 4 + d2 % 2
                    for c in range(DC):
                        P.mm(ps[po][:, :], wv[:, c, :], mg[:, c, :], c == 0, c == DC - 1, [WB, MG[c]], [PS[po]])
                    P.tt("dve", xT[:, d2, gsl(tg)], ps[po][:, :], xT[:, d2, gsl(tg)], ALU.add, [PS[po], XT[tg]], [XT[tg]])

        def stage_out(raw=False):
            A = Arena()
            xo = [A.take(1024) for _ in range(2)]
            XO = [P.buf("xo%d" % i) for i in range(2)]
            sqo = A.take(1024)
            SQO = P.buf("sqo")
            ss = A.take(8)
            yo = [A.take(1024) for _ in range(2)]
            YO = [P.buf("yo%d" % i, dma=True) for i in range(2)]
            gf = A.take(1024)
            GF = P.buf("gf", dma=True)
            P.dma("sp", gf, bass.AP(tensor=gfin_d.tensor, offset=0, ap=[[0, 128], [1, 1024]]), GF.sem, writes=[GF])
            for tt in range(NT):
                b = tt % 2
                for cq in range(2):
                    pb = (tt * 2 + cq) % 4
                    for k in range(4):
                        c = cq * 4 + k
                        P.tr(ps[pb][:, k * 128:(k + 1) * 128], xT[:, c, tsl(tt)], ident, [XT[tt // 4], CST], [PS[pb]], signal=(k == 3))
                    P.copy("dve" if cq == 0 else "act", xo[b][:, cq * 512:(cq + 1) * 512], ps[pb][:, :], [PS[pb]], [XO[b]])
                if raw:
                    P.copy("dve", yo[b], xo[b], [XO[b]], [YO[b]])
                else:
                    P.tt("pool", sqo, xo[b], xo[b], ALU.mult, [XO[b]], [SQO])
                    P.emit("dve", lambda e: e.reduce_sum(out=ss[:, 0:1], in_=sqo, axis=AX.X), [SQO], [SQO])
                    P.emit("dve", lambda e: e.tensor_scalar(ss[:, 0:1], ss[:, 0:1], 1.0 / D, EPS, op0=ALU.mult, op1=ALU.add), [SQO], [SQO])
                    P.act(ss[:, 0:1], ss[:, 0:1], AF.Sqrt, [SQO], [SQO])
                    P.emit("dve", lambda e: e.reciprocal(ss[:, 0:1], ss[:, 0:1]), [SQO], [SQO])
                    P.stt("dve", yo[b], xo[b], ss[:, 0:1], gf, ALU.mult, ALU.mult, [XO[b], SQO, GF], [YO[b]])
                P.dma("pool", out_d[tsl(tt), :], yo[b], YO[b].sem, reads=[YO[b]])

        def run_all():
            for sname in stages:
                parts = sname.split(":")
                if parts[0] == "setup":
                    stage_setup()
                elif parts[0] == "in":
                    stage_in()
                elif parts[0] == "ffn":
                    stage_ffn(int(parts[1]), int(parts[2]))
                elif parts[0] in ("m1", "m2", "m3", "m4", "m5"):
                    {"m1": stage_m1, "m2": stage_m2, "m3": stage_m3, "m4": stage_m4, "m5": stage_m5}[parts[0]](int(parts[1]))
                elif parts[0] == "out":
                    stage_out(False)
                elif parts[0] == "outraw":
                    stage_out(True)
                else:
                    raise ValueError(sname)
                P.barrier()

        P.dry = True
        run_all()
        P.dry = False
        W.reset()
        run_all()
        P.barrier()
        P.finalize(st)
    return nc, P


def _lhsT_tiles(Wm):
    K, N = Wm.shape
    return Wm.reshape(K // 128, 128, N // 128, 128).transpose(2, 1, 0, 3)


def _rel_bucket(rel):
    rel = np.asarray(rel, dtype=np.int64)
    ret = np.where(rel > 0, 16, 0)
    n = np.abs(rel)
    nf = np.maximum(n, 1).astype(np.float32)
    large = 8 + (np.log(nf / np.float32(8)) / np.float32(math.log(1024 / 8)) * np.float32(8)).astype(np.int32)
    large = np.minimum(large, 15)
    return ret + np.where(n < 8, n, large)


_CONST_CACHE = {}


def _constants():
    if _CONST_CACHE:
        return _CONST_CACHE
    Hh = S // 2
    s = np.arange(Hh, dtype=np.int64)
    ang = 2.0 * np.pi * ((s[:, None] * s[None, :]) % S).astype(np.float64) / S
    norm = 1.0 / math.sqrt(S * 128.0)
    mats = [np.cos(ang) * norm, np.sin(ang) * norm]
    dftf = np.stack([m.reshape(2, 4, 128, 2, 512).transpose(3, 0, 2, 1, 4) for m in mats], axis=1)
    _CONST_CACHE["dftf"] = np.ascontiguousarray(dftf.reshape(8, 128, 2048), dtype=np.float32)
    fconst = np.zeros((128, 520), np.float64)
    fconst[:, 0:512] = (((-1.0) ** np.arange(512)) * norm)[None, :]
    fconst[:, 512:520] = (((-1.0) ** np.arange(128)) * norm)[:, None]
    _CONST_CACHE["fconst"] = np.ascontiguousarray(fconst, dtype=np.float32)
    c = np.arange(128, dtype=np.int64)
    a2 = 2.0 * np.pi * ((c[:, None] * c[None, :]) % 128).astype(np.float64) / 128
    _CONST_CACHE["cs128"] = np.ascontiguousarray(np.concatenate([np.cos(a2), np.sin(a2)], axis=1), dtype=np.float32)
    _CONST_CACHE["ident"] = np.eye(128, dtype=np.float32)
    ohc = np.zeros((33, 1536), np.float32)
    i = np.arange(1535)
    ohc[_rel_bucket(i - 767), i] = 1.0
    _CONST_CACHE["ohc"] = ohc
    ohb = np.zeros((33, 3, 512), np.float32)
    i = np.arange(511)
    for g, d in enumerate(DIL):
        mrel = i - 255
        bk = np.where(np.abs(mrel) <= 64, _rel_bucket(mrel * d), 32)
        ohb[bk, g, i] = 1.0
    _CONST_CACHE["ohb"] = np.ascontiguousarray(ohb.reshape(33, 1536))
    return _CONST_CACHE


def prep_inputs(inp):
    f = lambda a: np.ascontiguousarray(np.asarray(a), dtype=np.float32)
    g = {k: np.asarray(v) for k, v in inp.items()}
    o = dict(_constants())
    wgu, wd, winf, winvb, winvc, wm5a, wm5b, wm5c, wout = [], [], [], [], [], [], [], [], []
    fcols = np.concatenate([np.arange(0, 1792), np.arange(2176, 3200)])
    for l in range(L):
        for (wg_, wu_, wdn_) in ((g["w_ffn1_gate"], g["w_ffn1_up"], g["w_ffn1_down"]),
                                 (g["w_ffn2_gate"], g["w_ffn2_up"], g["w_ffn2_down"])):
            tg_ = _lhsT_tiles(wg_[l])
            tu_ = _lhsT_tiles(wu_[l])
            wgu.append(np.stack([tg_, tu_], axis=2).reshape(NF, 128, 2048))
            wd.append(_lhsT_tiles(wdn_[l]).reshape(8, 128, 2816))
        win = g["w_in"][l]
        winf.append(_lhsT_tiles(win[:, fcols]).reshape(22, 128, 1024))
        winvb.append(win[:, 1792:2176].reshape(8, 128, 384).transpose(1, 0, 2).reshape(1, 128, 3072))
        for hf in range(2):
            winvc.append(win[:, 3200 + hf * 256:3200 + (hf + 1) * 256].reshape(8, 128, 256).transpose(1, 0, 2).reshape(1, 128, 2048))
        for (lst, wb, br) in ((wm5a, g["w_br_a"][l], 0), (wm5b, g["w_br_b"][l], 1), (wm5c, g["w_br_c"][l], 2)):
            gt = _lhsT_tiles(g["w_gate"][l][:, br * 1024:(br + 1) * 1024])
            bt = _lhsT_tiles(wb)
            lst.append(np.concatenate([gt, bt], axis=2).reshape(8, 128, -1))
        wout.append(_lhsT_tiles(g["w_out"][l]).reshape(8, 128, 1024))
    o["wgu"] = f(np.concatenate(wgu, 0))
    o["wd"] = f(np.concatenate(wd, 0))
    o["winf"] = f(np.concatenate(winf, 0))
    o["winvb"] = f(np.concatenate(winvb, 0))
    o["winvc"] = f(np.concatenate(winvc, 0))
    o["wm5a"] = f(np.concatenate(wm5a, 0))
    o["wm5b"] = f(np.concatenate(wm5b, 0))
    o["wm5c"] = f(np.concatenate(wm5c, 0))
    o["wout"] = f(np.concatenate(wout, 0))
    cv = np.zeros((128, L * NCV), np.float32)
    for l in range(L):
        b0 = l * NCV
        cv[:, b0 + 0:b0 + 8] = g["g_ffn1"][l].reshape(8, 128).T
        cv[:, b0 + 8:b0 + 16] = g["g_mix"][l].reshape(8, 128).T
        cv[:, b0 + 16:b0 + 24] = g["g_ffn2"][l].reshape(8, 128).T
        cv[:, b0 + 24:b0 + 48] = g["b_gate"][l].reshape(24, 128).T
        cv[:, b0 + 48] = g["subln_g"][l]
    o["cvec"] = cv
    o["relb"] = f(g["rel_bias"])
    o["lamv"] = f(np.stack([g[k][l] for l in range(L) for k in ("lam_q1", "lam_k1", "lam_q2", "lam_k2")], 0))
    o["gfin"] = f(g["g_final"].reshape(1, 1024))
    return o


_NC_CACHE = {}


def kernel(**inputs):
    shared = prep_inputs(inputs)
    x = np.ascontiguousarray(np.asarray(inputs["x"]), dtype=np.float32)
    if "nc" not in _NC_CACHE:
        _NC_CACHE["nc"] = build()[0]
    nc = _NC_CACHE["nc"]
    in_maps = []
    for b in range(8):
        m = dict(shared)
        m["x"] = x[b]
        in_maps.append(m)
    res = run_bass_kernel_spmd(nc, in_maps, core_ids=list(range(8)))
    return np.stack([np.asarray(r["out"]) for r in res.results], axis=0).astype(np.float32)
```
